# Optimizing a Trainium2 kernel written in Bass

```python
import jax, jax.numpy as jnp
from jax import lax
import numpy as np

D_MODEL = 1024
BATCH = 8
SEQ = 2048
DEPTH = 2
DEC_BATCH = 128
DEC_SEQ = 4
PAST_LEN = 16384
PAGE_SIZE = 128

N_BRANCH = 4
MIX_W = D_MODEL // N_BRANCH
POOL_WINDOWS = (2, 4, 8, 16)
POOL_GROUPS = len(POOL_WINDOWS)
POOL_GW = MIX_W // POOL_GROUPS
POOL_PAST = max(POOL_WINDOWS) - 1
RET_HEADS = 4
RET_DK = MIX_W // RET_HEADS
RET_DV = MIX_W // RET_HEADS
RET_CHUNK = 128
ROPE_BASE = 10000.0
SC_WIDTH = 3
RWKV_HEADS = 4
RWKV_HD = MIX_W // RWKV_HEADS
LORA_W = 64
LORA_A = 64
LORA_G = 128
RWKV_COLS = 3 * MIX_W + LORA_W + LORA_A + LORA_G
D_FF = 2816
FFN_WIDTH = 3
NORM_EPS = 1e-6
GN_EPS = 1e-6
RWKV_LN_EPS = 64e-5
L2_EPS = 1e-12

COL_POOL = 0
COL_RET = COL_POOL + MIX_W
COL_SC = COL_RET + 4 * MIX_W
COL_RWKV = COL_SC + 3 * MIX_W
COL_GATE = COL_RWKV + RWKV_COLS
IN_COLS = COL_GATE + N_BRANCH * D_MODEL

kernel_name = 'hybrid_pool_ret_sconv_rwkv7_step'


def rmsnorm(x, g):
    xf = x.astype(jnp.float32)
    y = xf * lax.rsqrt(jnp.mean(xf * xf, axis=-1, keepdims=True) + NORM_EPS)
    return (y * g.astype(jnp.float32)).astype(x.dtype)


def causal_dwconv(z, past, w):
    full = jnp.concatenate([past.astype(z.dtype), z], axis=1)
    T = z.shape[1]
    K = w.shape[0]
    y = sum(full[:, k:k + T] * w[k] for k in range(K))
    return y, full[:, -(K - 1):]


def pool_mixer(u, past, pos, pool_w, pool_scale):
    B, T, _ = u.shape
    full = jnp.concatenate([past.astype(u.dtype), u], axis=1)
    cs = jnp.cumsum(full.astype(jnp.float32), axis=1)
    cs = jnp.concatenate([jnp.zeros_like(cs[:, :1]), cs], axis=1)
    hi = cs[:, POOL_PAST + 1:POOL_PAST + 1 + T]
    means = []
    for g, w in enumerate(POOL_WINDOWS):
        sl = slice(g * POOL_GW, (g + 1) * POOL_GW)
        lo = cs[:, POOL_PAST + 1 - w:POOL_PAST + 1 - w + T, sl]
        cnt = jnp.minimum(w, pos + 1).astype(jnp.float32)[None, :, None]
        means.append((hi[..., sl] - lo) / cnt)
    m = jnp.concatenate(means, axis=-1) - u.astype(jnp.float32)
    m = m.astype(u.dtype).reshape(B, T, POOL_GROUPS, POOL_GW)
    y = jnp.einsum('btgc,gcd->btgd', m, pool_w).reshape(B, T, MIX_W) * pool_scale
    return y, full[:, -POOL_PAST:]


def rotary(x, pos):
    half = x.shape[-1] // 2
    inv = ROPE_BASE ** (-jnp.arange(half, dtype=jnp.float32) / half)
    ang = pos.astype(jnp.float32)[:, None] * inv[None, :]
    cos = jnp.cos(ang)[None, :, None, :]
    sin = jnp.sin(ang)[None, :, None, :]
    xf = x.astype(jnp.float32)
    x1, x2 = xf[..., :half], xf[..., half:]
    return jnp.concatenate([x1 * cos - x2 * sin, x1 * sin + x2 * cos], axis=-1).astype(x.dtype)


def retention(q, k, v, S0):
    f32 = jnp.float32
    B, T, H, DK = q.shape
    C = RET_CHUNK if T % RET_CHUNK == 0 else T
    N = T // C
    lg = jnp.log1p(-(2.0 ** (-5.0 - jnp.arange(H, dtype=f32))))
    idx = jnp.arange(C, dtype=f32)
    diff = idx[:, None] - idx[None, :]
    dmask = jnp.where(diff >= 0, jnp.exp(lg[:, None, None] * jnp.maximum(diff, 0.0)), 0.0)
    q_dec = jnp.exp(lg[:, None] * (idx[None, :] + 1.0))
    k_dec = jnp.exp(lg[:, None] * (C - 1.0 - idx[None, :]))
    c_dec = jnp.exp(lg * C)

    def to_chunks(t):
        return t.astype(f32).reshape(B, N, C, H, t.shape[-1]).transpose(1, 0, 2, 3, 4)

    def step(S, inp):
        qb, kb, vb = inp
        s = jnp.einsum('bihd,bjhd->bhij', qb, kb) * dmask
        o = jnp.einsum('bhij,bjhe->bihe', s, vb)
        o = o + jnp.einsum('bihd,hi,bhde->bihe', qb, q_dec, S)
        S = S * c_dec[None, :, None, None] + jnp.einsum('bjhd,hj,bjhe->bhde', kb, k_dec, vb)
        return S, o

    S, o = lax.scan(step, S0.astype(f32), (to_chunks(q), to_chunks(k), to_chunks(v)))
    o = o.transpose(1, 0, 2, 3, 4).reshape(B, T, H, v.shape[-1])
    return o, S


def rwkv7_scan(r, w, k, v, a_vec, b_vec, S0):
    def step(S, inp):
        rt, wt, kt, vt, at, bt = inp
        S = (S * wt[:, :, None, :]
             + jnp.einsum('bhij,bhj->bhi', S, at)[..., None] * bt[:, :, None, :]
             + vt[..., :, None] * kt[..., None, :])
        return S, jnp.einsum('bhij,bhj->bhi', S, rt)

    xs = tuple(jnp.moveaxis(t, 1, 0) for t in (r, w, k, v, a_vec, b_vec))
    S, y = lax.scan(step, S0, xs)
    return jnp.moveaxis(y, 0, 1), S


def rwkv7_mixer(z, shift_prev, S0, p):
    f32 = jnp.float32
    B, T, _ = z.shape
    prev = jnp.concatenate([shift_prev[:, None].astype(z.dtype), z[:, :-1]], axis=1)
    zs = z + (prev - z) * p['rw_mu']
    o = 3 * MIX_W
    r = zs[..., :MIX_W].astype(f32)
    k = zs[..., MIX_W:2 * MIX_W].astype(f32)
    v = zs[..., 2 * MIX_W:o].astype(f32)
    wl = zs[..., o:o + LORA_W]
    al = zs[..., o + LORA_W:o + LORA_W + LORA_A]
    gl = zs[..., o + LORA_W + LORA_A:]
    w = -jax.nn.softplus(-(p['rw_w0'] + jnp.tanh(wl) @ p['rw_w_lora']).astype(f32)) - 0.5
    decay = jnp.exp(-jnp.exp(w))
    a = jax.nn.sigmoid((p['rw_a0'] + al @ p['rw_a_lora']).astype(f32))
    g = (jax.nn.sigmoid(gl) @ p['rw_g_lora']).astype(f32)

    def hs(t):
        return t.reshape(B, T, RWKV_HEADS, RWKV_HD)

    kk = hs(k * p['rw_k_k'].astype(f32))
    kk = kk / jnp.maximum(jnp.sqrt(jnp.sum(kk * kk, axis=-1, keepdims=True)), L2_EPS)
    k = k * (1.0 + (a - 1.0) * p['rw_k_a'].astype(f32))
    rh, kh, vh = hs(r), hs(k), hs(v)
    y, S = rwkv7_scan(rh, hs(decay), kh, vh, -kk, kk * hs(a), S0.astype(f32))
    mean = jnp.mean(y, axis=-1, keepdims=True)
    var = jnp.mean(jnp.square(y - mean), axis=-1, keepdims=True)
    yn = ((y - mean) * lax.rsqrt(var + RWKV_LN_EPS)).reshape(B, T, MIX_W)
    yn = yn * p['rw_ln_g'].astype(f32) + p['rw_ln_b'].astype(f32)
    bonus = jnp.sum(rh * kh * p['rw_r_k'].astype(f32), axis=-1, keepdims=True) * vh
    out = (yn + bonus.reshape(B, T, MIX_W)) * g
    return out.astype(z.dtype), z[:, -1], S


def block(x, c, pos0, states, p):
    st_pool, st_ret, st_sc, st_shift, st_wkv, st_ffn = states
    f32 = jnp.float32
    B, T, _ = x.shape
    pos = pos0 + jnp.arange(T, dtype=jnp.int32)
    mod = (jax.nn.silu(c) @ p['w_ada'] + p['b_ada']).reshape(B, 6, 1, D_MODEL)
    h = rmsnorm(x, p['norm1_g']) * (1 + mod[:, 1]) + mod[:, 0]
    z = h @ p['w_in']

    yA, n_pool = pool_mixer(z[..., COL_POOL:COL_RET], st_pool, pos, p['pool_w'], p['pool_scale'])

    zr = z[..., COL_RET:COL_SC].reshape(B, T, 4, RET_HEADS, RET_DK)
    q = rotary(zr[:, :, 0], pos)
    kr = rotary(zr[:, :, 1], pos) * (RET_DK ** -0.5)
    o, n_ret = retention(q, kr, zr[:, :, 2], st_ret)
    o = o * lax.rsqrt(jnp.mean(o * o, axis=-1, keepdims=True) + GN_EPS)
    yB = (o.reshape(B, T, MIX_W) * jax.nn.silu(zr[:, :, 3].reshape(B, T, MIX_W).astype(f32))).astype(x.dtype)

    zc = z[..., COL_SC:COL_RWKV]
    conv_y, n_sc = causal_dwconv(zc[..., 2 * MIX_W:] * zc[..., :MIX_W], st_sc, p['sc_w'])
    yC = zc[..., MIX_W:2 * MIX_W] * conv_y

    yD, n_shift, n_wkv = rwkv7_mixer(z[..., COL_RWKV:COL_GATE], st_shift, st_wkv, p)

    branches = jnp.stack([yA, yB, yC, yD], axis=2)
    proj = jnp.einsum('btnc,ncd->btnd', branches, p['w_br'])
    gates = jax.nn.sigmoid(z[..., COL_GATE:].reshape(B, T, N_BRANCH, D_MODEL))
    mixed = jnp.sum(gates * proj, axis=2) @ p['w_out']
    x = x + mod[:, 2] * mixed

    h2 = rmsnorm(x, p['norm2_g']) * (1 + mod[:, 4]) + mod[:, 3]
    up, n_ffn = causal_dwconv(h2 @ p['w_up'], st_ffn, p['ffn_w'])
    act = jax.nn.silu(up[..., :D_FF]) * up[..., D_FF:]
    x = x + mod[:, 5] * (act @ p['w_down'])
    new = tuple(t.astype(x.dtype) for t in (n_pool, n_ret, n_sc, n_shift, n_wkv, n_ffn))
    return x, new


def zero_states(b, dtype):
    return (jnp.zeros((b, POOL_PAST, MIX_W), dtype),
            jnp.zeros((b, RET_HEADS, RET_DK, RET_DV), dtype),
            jnp.zeros((b, SC_WIDTH - 1, MIX_W), dtype),
            jnp.zeros((b, RWKV_COLS), dtype),
            jnp.zeros((b, RWKV_HEADS, RWKV_HD, RWKV_HD), dtype),
            jnp.zeros((b, FFN_WIDTH - 1, 2 * D_FF), dtype))


def setup_inputs(seed: int = 0) -> dict:
    key = jax.random.key(seed)
    ks = iter(jax.random.split(key, 64))
    f32 = jnp.float32

    def nrm(shape, s):
        return jax.random.normal(next(ks), shape, f32) * s

    def uni(shape, lo, hi):
        return jax.random.uniform(next(ks), shape, f32, lo, hi)

    L = DEPTH
    D = D_MODEL
    return {
        'x_prompt': nrm((BATCH, SEQ, D), 1.0),
        'x_sample': nrm((DEC_BATCH, DEC_SEQ, D), 1.0),
        'c_prompt': nrm((BATCH, D), 1.0),
        'c_sample': nrm((DEC_BATCH, D), 1.0),
        'state_pool': nrm((L, DEC_BATCH, POOL_PAST, MIX_W), 1.0),
        'state_ret': nrm((L, DEC_BATCH, RET_HEADS, RET_DK, RET_DV), 0.5),
        'state_sconv': nrm((L, DEC_BATCH, SC_WIDTH - 1, MIX_W), 1.0),
        'state_shift': nrm((L, DEC_BATCH, RWKV_COLS), 1.0),
        'state_wkv': nrm((L, DEC_BATCH, RWKV_HEADS, RWKV_HD, RWKV_HD), 0.3),
        'state_ffn': nrm((L, DEC_BATCH, FFN_WIDTH - 1, 2 * D_FF), 1.0),
        'w_ada': nrm((L, D, 6 * D), 0.5 * D ** -0.5),
        'b_ada': nrm((L, 6 * D), 0.02),
        'norm1_g': 1.0 + nrm((L, D), 0.05),
        'norm2_g': 1.0 + nrm((L, D), 0.05),
        'w_in': nrm((L, D, IN_COLS), D ** -0.5),
        'pool_w': nrm((L, POOL_GROUPS, POOL_GW, POOL_GW), POOL_GW ** -0.5),
        'pool_scale': 1.0 + nrm((L, MIX_W), 0.1),
        'sc_w': nrm((L, SC_WIDTH, MIX_W), SC_WIDTH ** -0.5),
        'rw_mu': uni((L, RWKV_COLS), 0.0, 1.0),
        'rw_w0': uni((L, MIX_W), -6.0, -1.0),
        'rw_w_lora': nrm((L, LORA_W, MIX_W), 0.1 * LORA_W ** -0.5),
        'rw_a0': nrm((L, MIX_W), 0.1),
        'rw_a_lora': nrm((L, LORA_A, MIX_W), 0.1 * LORA_A ** -0.5),
        'rw_g_lora': nrm((L, LORA_G, MIX_W), LORA_G ** -0.5),
        'rw_k_k': 0.85 + nrm((L, MIX_W), 0.05),
        'rw_k_a': 1.0 + nrm((L, MIX_W), 0.05),
        'rw_r_k': nrm((L, RWKV_HEADS, RWKV_HD), 0.1),
        'rw_ln_g': 1.0 + nrm((L, MIX_W), 0.05),
        'rw_ln_b': nrm((L, MIX_W), 0.02),
        'w_br': nrm((L, N_BRANCH, MIX_W, D), MIX_W ** -0.5),
        'w_out': nrm((L, D, D), D ** -0.5),
        'w_up': nrm((L, D, 2 * D_FF), D ** -0.5),
        'ffn_w': nrm((L, FFN_WIDTH, 2 * D_FF), FFN_WIDTH ** -0.5),
        'w_down': nrm((L, D_FF, D), D_FF ** -0.5),
        'final_g': 1.0 + nrm((D,), 0.05),
    }


def reference(x_prompt, x_sample, c_prompt, c_sample, state_pool, state_ret, state_sconv, state_shift,
              state_wkv, state_ffn, w_ada, b_ada, norm1_g, norm2_g, w_in, pool_w, pool_scale, sc_w,
              rw_mu, rw_w0, rw_w_lora, rw_a0, rw_a_lora, rw_g_lora, rw_k_k, rw_k_a, rw_r_k, rw_ln_g,
              rw_ln_b, w_br, w_out, w_up, ffn_w, w_down, final_g):
    xp, xs = x_prompt, x_sample
    new_p, new_s = [], []
    for l in range(DEPTH):
        p = {'w_ada': w_ada[l], 'b_ada': b_ada[l], 'norm1_g': norm1_g[l], 'norm2_g': norm2_g[l],
             'w_in': w_in[l], 'pool_w': pool_w[l], 'pool_scale': pool_scale[l], 'sc_w': sc_w[l],
             'rw_mu': rw_mu[l], 'rw_w0': rw_w0[l], 'rw_w_lora': rw_w_lora[l], 'rw_a0': rw_a0[l],
             'rw_a_lora': rw_a_lora[l], 'rw_g_lora': rw_g_lora[l], 'rw_k_k': rw_k_k[l],
             'rw_k_a': rw_k_a[l], 'rw_r_k': rw_r_k[l], 'rw_ln_g': rw_ln_g[l], 'rw_ln_b': rw_ln_b[l],
             'w_br': w_br[l], 'w_out': w_out[l], 'w_up': w_up[l], 'ffn_w': ffn_w[l],
             'w_down': w_down[l]}
        xp, sp = block(xp, c_prompt, 0, zero_states(xp.shape[0], xp.dtype), p)
        xs, ss = block(xs, c_sample, PAST_LEN,
                       (state_pool[l], state_ret[l], state_sconv[l], state_shift[l], state_wkv[l], state_ffn[l]), p)
        new_p.append(sp)
        new_s.append(ss)
    y_prompt = rmsnorm(xp, final_g)
    y_sample = rmsnorm(xs, final_g)
    p_pool = jnp.stack([s[0] for s in new_p])
    p_ret = jnp.stack([s[1] for s in new_p])
    p_sconv = jnp.stack([s[2] for s in new_p])
    p_shift = jnp.stack([s[3] for s in new_p])
    p_wkv = jnp.stack([s[4] for s in new_p])
    p_ffn = jnp.stack([s[5] for s in new_p])
    s_pool = jnp.stack([s[0] for s in new_s])
    s_ret = jnp.stack([s[1] for s in new_s])
    s_sconv = jnp.stack([s[2] for s in new_s])
    s_shift = jnp.stack([s[3] for s in new_s])
    s_wkv = jnp.stack([s[4] for s in new_s])
    s_ffn = jnp.stack([s[5] for s in new_s])
    return (y_prompt, y_sample, p_pool, p_ret, p_sconv, p_shift, p_wkv, p_ffn,
            s_pool, s_ret, s_sconv, s_shift, s_wkv, s_ffn)
```

```python
import contextlib
import math
import numpy as np
import concourse.bass as bass
import concourse.mybir as mybir
from concourse.bass_utils import run_bass_kernel_spmd

F32 = mybir.dt.float32
BF16 = mybir.dt.bfloat16
BF16_DENSE = True
AF = mybir.ActivationFunctionType
ALU = mybir.AluOpType

D = 1024
L = 2
NPT = 2048
NB = 16
NSC = 64
NCOLS = NPT + NSC
SEGS = [256, 384, 384, 384, 384, 256]
WMAX = 384
DFF = 2816
NFC = 44
IN_COLS = 7168
RC = 128
WC = 64
LG = [math.log1p(-(2.0 ** (-5.0 - h))) for h in range(4)]

VOFF = {}
_o = 0
for _n, _w in [("n1g", 8), ("n2g", 8), ("pscale", 2), ("scw", 6), ("mu", 8), ("w0", 2), ("a0", 2), ("kk", 2),
               ("ka", 2), ("rk", 2), ("lng", 2), ("lnb", 2), ("ffw", 132), ("fg", 8), ("bada", 48)]:
    VOFF[_n] = _o
    _o += _w
NV = _o

TOFF = {}
_o = 0
for _n, _w in [("ident", 128), ("ones", 128), ("bones", 128), ("perm", 128), ("dmask", 512), ("qdec", 256),
               ("kdec", 256), ("kdec4", 256), ("cdec", 2), ("cdec4", 2), ("msk64", 320), ("msk4", 20),
               ("invw", 2), ("icnt", 32)]:
    TOFF[_n] = _o
    _o += _w
NTAB = _o


def make_tables():
    t = np.zeros((128, NTAB), np.float64)
    p = np.arange(128)
    t[:, TOFF["ident"]:TOFF["ident"] + 128] = np.eye(128)
    t[:, TOFF["ones"]:TOFF["ones"] + 128] = 1.0
    t[:, TOFF["bones"]:TOFF["bones"] + 128] = (p[:, None] // 64 == p[None, :] // 64)
    partner = np.where(p % 64 < 32, p + 32, p - 32)
    pm = np.zeros((128, 128))
    pm[partner, p] = 1.0
    t[:, TOFF["perm"]:TOFF["perm"] + 128] = pm
    i = np.arange(128)
    for h in range(4):
        g = math.exp(LG[h])
        dm = np.where(i[None, :] >= i[:, None], g ** np.maximum(i[None, :] - i[:, None], 0), 0.0)
        t[:, TOFF["dmask"] + h * 128:TOFF["dmask"] + (h + 1) * 128] = dm
        t[:, TOFF["kdec"] + h * 64:TOFF["kdec"] + (h + 1) * 64] = (g ** (127 - i))[:, None]
        t[0:4, TOFF["kdec4"] + h * 64:TOFF["kdec4"] + (h + 1) * 64] = (g ** (3 - np.arange(4)))[:, None]
    for hp in range(2):
        for h2 in range(2):
            g = math.exp(LG[2 * hp + h2])
            t[h2 * 64:(h2 + 1) * 64, TOFF["qdec"] + hp * 128:TOFF["qdec"] + (hp + 1) * 128] = (g ** (i + 1))[None, :]
            t[h2 * 64:(h2 + 1) * 64, TOFF["cdec"] + hp] = g ** 128
            t[h2 * 64:(h2 + 1) * 64, TOFF["cdec4"] + hp] = g ** 4
    for C, nm in [(64, "msk64"), (4, "msk4")]:
        j = np.arange(C)
        su = (j[:, None] < j[None, :]).astype(np.float64)
        iu = (j[:, None] <= j[None, :]).astype(np.float64)
        sl = (j[:, None] > j[None, :]).astype(np.float64)
        t[0:C, TOFF[nm]:TOFF[nm] + 5 * C] = np.concatenate([su, iu, su, iu, sl], axis=1)
    wins = (2, 4, 8, 16)
    for c in range(2):
        for half in range(2):
            w = wins[2 * c + half]
            t[half * 64:(half + 1) * 64, TOFF["invw"] + c] = 1.0 / w
            t[half * 64:(half + 1) * 64, TOFF["icnt"] + c * 16:TOFF["icnt"] + (c + 1) * 16] = \
                (1.0 / np.minimum(w, np.arange(16) + 1))[None, :]
    return t.astype(np.float32)


def make_rot():
    pos = np.concatenate([np.arange(NPT), np.repeat(16384 + np.arange(4), NB)]).astype(np.float32)
    p = np.arange(128)
    inv = (10000.0 ** (-(p % 32).astype(np.float32) / np.float32(32))).astype(np.float32)
    ang = (pos[None, :] * inv[:, None]).astype(np.float32)
    cos = np.cos(ang.astype(np.float64))
    sin = np.sin(ang.astype(np.float64))
    sgn = np.where(p % 64 < 32, -1.0, 1.0)[:, None]
    return np.stack([cos, sin * sgn]).astype(np.float32)


class Sem:
    def __init__(self, h, inc):
        self.h = h
        self.inc = inc
        self.count = 0


class V:
    def __init__(self, tt, ap):
        self.tt = tt
        self.ap = ap


class TT:
    def __init__(self, t):
        self.t = t
        self.lw = None
        self.rd = {}

    def __getitem__(self, idx):
        return V(self, self.t[idx])


class Prog:
    ENG = ["pe", "act", "dve", "pool", "sp"]

    def __init__(self, nc):
        self.nc = nc
        self.es = contextlib.ExitStack()
        self.esem = {e: Sem(self.es.enter_context(nc.semaphore("s_" + e)), 1) for e in self.ENG}
        self.ops = {e: [] for e in self.ENG}
        self.seen = {e: {} for e in self.ENG}
        self.psr = []
        self.pi = 0
        self.evi = 0
        self.nm = 0
        self.alld = []

    def sb(self, name, shape, dtype=F32):
        return TT(self.es.enter_context(self.nc.sbuf_tensor("sb_" + name, list(shape), dtype)))

    def mkps(self):
        self.psr = [TT(self.es.enter_context(self.nc.psum_tensor("ps%d" % i, [128, 512], F32))) for i in range(8)]

    def nps(self):
        self.pi = (self.pi + 1) % 6
        return self.psr[self.pi]

    def ev(self):
        self.evi ^= 1
        return "act" if self.evi else "dve"

    def dsem(self, name):
        d = Sem(self.es.enter_context(self.nc.semaphore("d_" + name)), 16)
        self.alld.append(d)
        return d

    def _deps(self, eng, reads, writes):
        waits = {}
        own = self.esem[eng]

        def need(sv, raw):
            sem, val = sv
            if sem is own and (eng == "pe" or (not raw and eng != "pool")):
                return
            if waits.get(sem, 0) < val:
                waits[sem] = val

        for r in reads:
            if r.lw is not None:
                need(r.lw, True)
        for w in writes:
            if w.lw is not None:
                need(w.lw, False)
            for s, v in w.rd.items():
                need((s, v), False)
        out = []
        seen = self.seen[eng]
        for s, v in waits.items():
            if seen.get(s, 0) < v:
                seen[s] = v
                out.append((s, v))
        return out

    def _commit(self, sem, reads, writes):
        sem.count += sem.inc
        val = sem.count
        for r in reads:
            if r.rd.get(sem, 0) < val:
                r.rd[sem] = val
        for w in writes:
            w.lw = (sem, val)
            w.rd = {}

    def op(self, eng, fn, reads, writes):
        waits = self._deps(eng, reads, writes)
        sem = self.esem[eng]
        self._commit(sem, reads, writes)
        self.ops[eng].append((waits, fn, sem))

    def dma(self, eng, out, in_, dsem):
        rd = [in_.tt] if isinstance(in_, V) else []
        wr = [out.tt] if isinstance(out, V) else []
        oap = out.ap if isinstance(out, V) else out
        iap = in_.ap if isinstance(in_, V) else in_
        waits = self._deps(eng, rd, wr)
        self._commit(dsem, rd, wr)
        self.ops[eng].append((waits, lambda e: e.dma_start(out=oap, in_=iap), dsem))

    def mm(self, out, pairs):
        n = len(pairs)
        aps = [(l.ap, r.ap) for l, r in pairs]
        oap = out.ap

        def fn(e):
            inst = None
            for i, (l, r) in enumerate(aps):
                inst = e.matmul(oap, l, r, start=(i == 0), stop=(i == n - 1))
            return inst

        rds = []
        for l, r in pairs:
            rds += [l.tt, r.tt]
        self.op("pe", fn, rds, [out.tt])

    def mm1(self, out, l, r, start, stop):
        oap, lap, rap = out.ap, l.ap, r.ap
        self.op("pe", lambda e: e.matmul(oap, lap, rap, start=start, stop=stop), [l.tt, r.tt], [out.tt])

    def tr(self, out, in_, ident):
        oap, iap, dap = out.ap, in_.ap, ident.ap
        self.op("pe", lambda e: e.transpose(out=oap, in_=iap, identity=dap), [in_.tt, ident.tt], [out.tt])

    def trx(self, out, in_, identv, K, M):
        if K == 128 and M == 128:
            self.tr(out, in_, identv(128))
        else:
            self.mm(out, [(in_, identv(K))])

    def tt(self, eng, out, in0, in1, op):
        oap, a, b = out.ap, in0.ap, in1.ap
        self.op(eng, lambda e: e.tensor_tensor(out=oap, in0=a, in1=b, op=op), [in0.tt, in1.tt], [out.tt])

    def ts(self, eng, out, in0, s1, s2, op0, op1=None):
        rds = [in0.tt]
        a1 = s1
        a2 = s2
        if isinstance(s1, V):
            rds.append(s1.tt)
            a1 = s1.ap
        if isinstance(s2, V):
            rds.append(s2.tt)
            a2 = s2.ap
        oap, iap = out.ap, in0.ap
        if op1 is None:
            self.op(eng, lambda e: e.tensor_scalar(out=oap, in0=iap, scalar1=a1, scalar2=None, op0=op0), rds, [out.tt])
        else:
            self.op(eng, lambda e: e.tensor_scalar(out=oap, in0=iap, scalar1=a1, scalar2=a2, op0=op0, op1=op1),
                    rds, [out.tt])

    def stt(self, out, in0, sc, in1, op0, op1):
        rds = [in0.tt, in1.tt]
        a = sc
        if isinstance(sc, V):
            rds.append(sc.tt)
            a = sc.ap
        oap, i0, i1 = out.ap, in0.ap, in1.ap
        self.op("dve", lambda e: e.scalar_tensor_tensor(out=oap, in0=i0, scalar=a, in1=i1, op0=op0, op1=op1),
                rds, [out.tt])

    def act(self, out, in_, func, bias=None, scale=None):
        rds = [in_.tt]
        kw = {}
        if bias is not None:
            if isinstance(bias, V):
                rds.append(bias.tt)
                kw["bias"] = bias.ap
            else:
                kw["bias"] = bias
        if scale is not None:
            if isinstance(scale, V):
                rds.append(scale.tt)
                kw["scale"] = scale.ap
            else:
                kw["scale"] = scale
        oap, iap = out.ap, in_.ap
        self.op("act", lambda e: e.activation(out=oap, in_=iap, func=func, **kw), rds, [out.tt])

    def cp(self, eng, out, in_):
        oap, iap = out.ap, in_.ap
        if eng == "act":
            self.op("act", lambda e: e.activation(out=oap, in_=iap, func=AF.Copy), [in_.tt], [out.tt])
        else:
            self.op(eng, lambda e: e.tensor_copy(out=oap, in_=iap), [in_.tt], [out.tt])

    def recip(self, out, in_):
        oap, iap = out.ap, in_.ap
        self.op("dve", lambda e: e.reciprocal(out=oap, in_=iap), [in_.tt], [out.tt])

    def memset(self, eng, out, val):
        oap = out.ap
        self.op(eng, lambda e: e.memset(oap, val), [], [out.tt])

    def finish(self, dsems):
        waits = [(s, s.count) for s in dsems if s.count > 0]
        self.ops["sp"].append((waits, None, None))

    def emit(self):
        nc = self.nc
        ops = self.ops

        def run(e, lst):
            for waits, fn, sem in lst:
                for s, v in waits:
                    e.wait_ge(s.h, v)
                if fn is not None:
                    fn(e).then_inc(sem.h, sem.inc)

        with nc.Block() as block:
            @block.tensor
            def _(e):
                run(e, ops["pe"])

            @block.scalar
            def _(e):
                run(e, ops["act"])

            @block.vector
            def _(e):
                run(e, ops["dve"])

            @block.gpsimd
            def _(e):
                run(e, ops["pool"])

            @block.sync
            def _(e):
                run(e, ops["sp"])
        self.es.close()


class _Stop(Exception):
    pass


def build_program(nlayers=L, segs=SEGS, stop_after=None):
    nc = bass.Bass("TRN2", target_bir_lowering=False, dynamic_dma_scratch_size=1024)
    P = Prog(nc)

    def din(name, shape):
        return nc.dram_tensor(name, list(shape), F32, kind="ExternalInput").ap()

    def dout(name, shape):
        return nc.dram_tensor(name, list(shape), F32, kind="ExternalOutput").ap()

    xT = din("xT", [D, NCOLS])
    cT = din("cT", [128, 8, 17])
    w_ada = din("w_ada", [L, D, 6 * D])
    w_in = din("w_in", [L, D, IN_COLS])
    w_br = din("w_br", [L, D, D])
    w_out = din("w_out", [L, D, D])
    w_up = din("w_up", [L, D, 2 * DFF])
    w_down = din("w_down", [L, DFF, D])
    vecs = din("vecs", [L, 128, NV])
    pool_bd = din("pool_bd", [L, 128, 2, 128])
    lora_wa = din("lora_wa", [L, 128, 256])
    lora_g = din("lora_g", [L, 128, 256])
    st_pool = din("st_pool", [L, 128, 2, 240])
    st_ret = din("st_ret", [L, 128, NB, 2, 64])
    st_sc = din("st_sc", [L, 128, 2, 32])
    st_sh = din("st_sh", [L, 128, 8, 16])
    st_wkv = din("st_wkv", [L, 128, NB, 2, 64])
    st_ffn = din("st_ffn", [L, 128, NFC, 32])
    tabs_d = din("tabs", [128, NTAB])
    rot_d = din("rot", [2, 128, NCOLS])

    yT = dout("yT", [D, NCOLS])
    o_pool_p = dout("o_pool_p", [L, 128, 2, 15])
    o_pool_s = dout("o_pool_s", [L, 128, 2, 240])
    o_ret_p = dout("o_ret_p", [L, 128, 2, 64])
    o_ret_s = dout("o_ret_s", [L, 128, NB, 2, 64])
    o_sc_p = dout("o_sc_p", [L, 128, 2, 2])
    o_sc_s = dout("o_sc_s", [L, 128, 2, 32])
    o_sh_p = dout("o_sh_p", [L, 128, 8])
    o_sh_s = dout("o_sh_s", [L, 128, 8, 16])
    o_wkv_p = dout("o_wkv_p", [L, 128, 2, 64])
    o_wkv_s = dout("o_wkv_s", [L, 128, NB, 2, 64])
    o_ffn_p = dout("o_ffn_p", [L, 128, NFC, 2])
    o_ffn_s = dout("o_ffn_s", [L, 128, NFC, 32])

    ckc = [0]

    def ck(tag=""):
        ckc[0] += 1
        if stop_after is not None and stop_after == "tag:" + tag:
            print("STOP at tag", tag)
            raise _Stop()
        if stop_after is not None and stop_after.startswith("cnt:") and ckc[0] == int(stop_after[4:]):
            print("STOP at checkpoint", ckc[0], tag)
            raise _Stop()

    P.mkps()
    W0 = WMAX
    ZW = 2 + WMAX + 4
    UW = 576
    tabs = P.sb("tabs", [128, NTAB])
    vt_ = [P.sb("vecs%d" % l, [128, NV]) for l in range(L)]
    pbd = [P.sb("pbd%d" % l, [128, 2, 128]) for l in range(L)]
    lwa = [P.sb("lwa%d" % l, [128, 256]) for l in range(L)]
    lgl = [P.sb("lgl%d" % l, [128, 256]) for l in range(L)]
    omka = [P.sb("omka%d" % l, [128, 2]) for l in range(L)]
    csil = P.sb("csil", [128, 8, 17])
    mod = [P.sb("mod%d" % l, [128, 48, 17]) for l in range(L)]
    A1 = [P.sb("A1_%d" % l, [128, 8, 17]) for l in range(L)]
    A2 = [P.sb("A2_%d" % l, [128, 8, 17]) for l in range(L)]
    rot = P.sb("rot", [128, 2, W0])
    xs = P.sb("xs", [128, 8, W0])
    DD = BF16 if BF16_DENSE else F32
    hs = P.sb("hs", [128, 8, W0], DD)
    csil_b = P.sb("csil_b", [128, 8, 17], DD)
    hbuf = [P.sb("hbuf%d" % i, [128, 2048], BF16) for i in range(3)] if BF16_DENSE else []
    wbi = [0]
    cei = [0]

    wcast_impl = [None]

    def wcast(wv, nk, c0, c1):
        return wcast_impl[0](wv, nk, c0, c1)

    NSLOT = 2
    wsl = [P.sb("wsl%d" % i, [128, 4096]) for i in range(NSLOT)]
    wsem = [P.dsem("w%d" % i) for i in range(NSLOT)]
    arena = [P.sb("ar%d" % i, [128, ZW]) for i in range(22)]
    zr = arena[0:8]
    zsc = arena[8:14]
    zw = arena[14:22]
    yb = [P.sb("yb%d" % i, [128, W0], BF16 if BF16_DENSE else F32) for i in range(8)]
    rx = [P.sb("rx%d" % i, [128, W0]) for i in range(10)]
    ubuf = [P.sb("ubuf%d" % i, [128, UW]) for i in range(2)]
    wide = [P.sb("wide%d" % i, [128, UW]) for i in range(2)]
    fbuf = wide + ubuf
    NSCR = 11
    scr_ = [P.sb("scr%d" % i, [128, W0]) for i in range(NSCR)]
    sci = [0]

    def scr():
        sci[0] = (sci[0] + 1) % NSCR
        return scr_[sci[0]]

    sstate = P.sb("sstate", [128, NB, 2, 64])
    wkvM = [P.sb("wkvM%d" % l, [128, 2, 2, 64]) for l in range(L)]
    smask = [P.sb("smask%d" % i, [128, 2, 2, 64]) for i in range(2)]
    retS = [P.sb("retS%d" % l, [128, 2, 64]) for l in range(L)]
    wkvS = [P.sb("wkvS%d" % l, [128, 2, 64]) for l in range(L)]
    pool_c = [P.sb("poolc%d" % l, [128, 2, 15]) for l in range(L)]
    sc_c = [P.sb("scc%d" % l, [128, 2, 2]) for l in range(L)]
    sh_c = [P.sb("shc%d" % l, [128, 8]) for l in range(L)]
    ffn_c = [P.sb("ffnc%d" % l, [128, NFC, 2]) for l in range(L)]
    sh_s = P.sb("sh_s", [128, 8, 16])
    sc_s = P.sb("sc_s", [128, 2, 32])
    ffn_s = P.sb("ffn_s", [128, NFC, 32])
    vtok = [P.sb("vtok%d" % i, [128, 256]) for i in range(2)]
    ktok = [P.sb("ktok%d" % i, [128, 256]) for i in range(2)]
    sTs = [P.sb("sTs%d" % i, [128, 128]) for i in range(2)]
    NU = 8
    tokm = [P.sb("tokm%d" % i, [64, 768]) for i in range(2)]
    Am = [P.sb("Am%d" % i, [64, 320]) for i in range(NU)]
    Tm = [P.sb("Tm%d" % i, [64, 64]) for i in range(NU)]
    Pq = [P.sb("Pq%d" % i, [64, 128]) for i in range(NU)]
    XTs = [P.sb("XTs%d" % i, [64, 256]) for i in range(1)]
    UTs = [P.sb("UTs%d" % i, [64, 256]) for i in range(1)]
    YTs = [P.sb("YTs%d" % i, [64, 256]) for i in range(1)]
    epsn_t = P.sb("epsn", [128, 4])
    ident = tabs[:, TOFF["ident"]:TOFF["ident"] + 128]
    onesm = tabs[:, TOFF["ones"]:TOFF["ones"] + 128]
    bones = tabs[:, TOFF["bones"]:TOFF["bones"] + 128]
    perm = tabs[:, TOFF["perm"]:TOFF["perm"] + 128]

    def idv(K):
        return tabs[0:K, TOFF["ident"]:TOFF["ident"] + K]

    def tab(name, r0, r1, c0, c1):
        return tabs[r0:r1, TOFF[name] + c0:TOFF[name] + c1]

    ds_const = P.dsem("const")
    ds_x = P.dsem("x")
    ds_rot = P.dsem("rot")
    ds_st = P.dsem("st")
    ds_ub = [P.dsem("ub%d" % i) for i in range(2)]
    ds_scs = P.dsem("scs")
    ds_shs = P.dsem("shs")
    ds_ffs = P.dsem("ffs")
    ds_y = P.dsem("y")
    ds_carry = [P.dsem("carry%d" % i) for i in range(6)]
    out_sems = [ds_y, ds_st, ds_scs, ds_shs, ds_ffs] + ds_ub + ds_carry

    P.dma("sp", tabs[:, :], tabs_d, ds_const)
    for l in range(L):
        P.dma("sp", vt_[l][:, :], vecs[l], ds_const)
        P.dma("sp", pbd[l][:, :, :], pool_bd[l], ds_const)
        P.dma("sp", lwa[l][:, :], lora_wa[l], ds_const)
        P.dma("sp", lgl[l][:, :], lora_g[l], ds_const)
    P.dma("sp", csil[:, :, :], cT, ds_const)
    for t in [tabs, csil] + vt_ + pbd + lwa + lgl:
        t.lw = (ds_const, ds_const.count)
    P.act(csil_b[:, :, :], csil[:, :, :], AF.Silu)
    P.memset("pool", epsn_t[:, 0:1], 1e-6)
    P.memset("pool", epsn_t[:, 1:2], 64e-5)
    epsn = epsn_t[:, 0:1]
    epsln = epsn_t[:, 1:2]
    for l in range(L):
        ka = vt_[l][:, VOFF["ka"]:VOFF["ka"] + 2]
        P.ts("dve", omka[l][:, :], ka, -1.0, 1.0, ALU.mult, ALU.add)
        for t in (retS[l], wkvS[l], pool_c[l], sc_c[l], ffn_c[l]):
            P.memset("pool", t[:, :, :], 0.0)
        P.memset("pool", sh_c[l][:, :], 0.0)
        P.memset("pool", wkvM[l][:, :, :, :], 0.0)
    for i in range(2):
        P.memset("pool", smask[i][:, :, :, :], 0.0)

    def vec(l, name, c):
        o = VOFF[name] + c
        return vt_[l][:, o:o + 1]

    wjobs = []
    wkind = []
    wmeta = []
    WIN_JOBS = [(0, 512), (512, 1024), (1024, 1280), (1280, 1792), (1792, 2048), (2048, 2560), (2560, 3072)]

    def gen_jobs():
        for si in range(len(segs)):
            for l in range(nlayers):
                jl = [0]

                def meta(ada=False, si=si, l=l, jl=jl):
                    if ada:
                        wmeta.append((si, l, None))
                    else:
                        wmeta.append((si, l, jl[0]))
                        jl[0] += 1

                if si == 0:
                    wa = w_ada[l].rearrange("(k p) n -> p k n", p=128)
                    for j in range(12):
                        wjobs.append([(wa[:, :, j * 512:(j + 1) * 512], 8, 512, 0)])
                        wkind.append("bf"); meta(ada=True)
                wi = w_in[l].rearrange("(k p) n -> p k n", p=128)
                for (a, b) in WIN_JOBS:
                    wjobs.append([(wi[:, :, a:b], 8, b - a, 0)])
                    wkind.append("bf"); meta()
                wb = w_br[l].rearrange("(k p) n -> p k n", p=128)
                for n in range(4):
                    for q in range(4):
                        g0 = 3072 + n * 1024 + q * 256
                        wjobs.append([(wi[:, :, g0:g0 + 256], 8, 256, 0),
                                      (wb[:, 2 * n:2 * n + 2, q * 256:(q + 1) * 256], 2, 256, 2048)])
                        wkind.append("bf"); meta()
                wo = w_out[l].rearrange("(k p) n -> p k n", p=128)
                for half in range(2):
                    wjobs.append([(wo[:, :, half * 512:(half + 1) * 512], 8, 512, 0)])
                    wkind.append("bf"); meta()
                wu = w_up[l].rearrange("(k p) n -> p k n", p=128)
                for j in range(11):
                    wjobs.append([(wu[:, :, j * 512:(j + 1) * 512], 8, 512, 0)])
                    wkind.append("bf"); meta()
                wd = w_down[l].rearrange("(k p) n -> p k n", p=128)
                for j in range(8):
                    wjobs.append([(wd[:, 0:11, j * 128:(j + 1) * 128], 11, 128, 0),
                                  (wd[:, 11:22, j * 128:(j + 1) * 128], 11, 128, 1408)])
                    wkind.append("bf"); meta()

    gen_jobs()
    assert len(wkind) == len(wjobs) == len(wmeta), (len(wkind), len(wjobs), len(wmeta))
    NJL = 44
    HBM_BF16 = BF16_DENSE and len(segs) > 1
    if HBM_BF16:
        wscr = nc.dram_tensor("wscr", [L, NJL, 128, 4096], BF16, kind="Internal").ap()
        hst = [P.dsem("hst%d" % i) for i in range(3)]
    first_b = [True]
    wslh = [TT(wsl[q // 2].t) for q in range(4)] if HBM_BF16 else []
    hsem = [P.dsem("hs%d" % q) for q in range(4)] if HBM_BF16 else []
    inherited = set()
    last0 = max([i for i in range(len(wjobs)) if wmeta[i][0] == 0] + [-1])

    def half_of(i):
        q = i % 4
        if q not in inherited:
            inherited.add(q)
            par = wsl[q // 2]
            wslh[q].lw = par.lw
            wslh[q].rd = dict(par.rd)
        return q

    def job_from_scratch(i):
        si_, l_, jl_ = wmeta[i]
        return HBM_BF16 and si_ >= 1 and jl_ is not None

    def unit_off(i, u):
        return sum(k_ * nc2 for (_, k_, _, _, _, nc2) in job_units(i)[:u])
    wstate = {"issued": 0, "next": 0}

    def wissue():
        i = wstate["issued"]
        s = i % NSLOT
        if job_from_scratch(i):
            si_, l_, jl_ = wmeta[i]
            tot = unit_off(i, len(job_units(i)))
            if si_ == 1:
                P.ops["sp"].append(([(d, d.count) for d in hst if d.count > 0], None, None))
            q = half_of(i)
            dst = V(wslh[q], wslh[q].t[:, :].bitcast(BF16)[:, (q % 2) * 4096:(q % 2) * 4096 + tot])
            P.dma("sp", dst, wscr[l_, jl_, :, 0:tot], hsem[q])
        else:
            for (ap, k, n, off) in wjobs[i]:
                dst = V(wsl[s], wsl[s].t[:, off:off + k * n].rearrange("p (k n) -> p k n", k=k))
                P.dma("sp", dst, ap, wsem[s])
        wstate["issued"] = i + 1

    cast_done = {}
    hbi = [0]

    def job_castable(i):
        return BF16_DENSE and wkind[i] == "bf"

    def job_units(i):
        us = []
        for pi, (ap, k, n, off) in enumerate(wjobs[i]):
            for c0 in range(0, n, 256):
                us.append((pi, k, n, off, c0, min(256, n - c0)))
        return us

    def ensure_cast(i, u):
        if (i, u) in cast_done:
            return cast_done[(i, u)]
        pi, k, n, off, c0, nc_ = job_units(i)[u]
        sl = wsl[i % NSLOT]
        if job_from_scratch(i):
            q = i % 4
            uo = (q % 2) * 4096 + unit_off(i, u)
            dst = V(wslh[q], wslh[q].t[:, :].bitcast(BF16)[:, uo:uo + k * nc_].rearrange("p (k n) -> p k n", k=k))
            cast_done[(i, u)] = dst
            return dst
        src = V(sl, sl.t[:, off:off + k * n].rearrange("p (k n) -> p k n", k=k)[:, :, c0:c0 + nc_])
        hbi[0] = (hbi[0] + 1) % 3
        cei[0] = (cei[0] + 1) % 2
        hb = hbuf[hbi[0]]
        dst = V(hb, hb.t[:, 0:k * nc_].rearrange("p (k n) -> p k n", k=k))
        P.cp(("dve", "act")[cei[0]], dst, src)
        if HBM_BF16 and wmeta[i][0] == 0 and wmeta[i][2] is not None:
            uo = unit_off(i, u)
            P.dma("act", wscr[wmeta[i][1], wmeta[i][2], :, uo:uo + k * nc_], V(hb, hb.t[:, 0:k * nc_]), hst[hbi[0]])
        cast_done[(i, u)] = dst
        return dst

    def wcast_real(wv, nk, c0, c1):
        if not BF16_DENSE:
            return V(wv.tt, wv.ap[:, 0:nk, c0:c1])
        i, pi = wv.job, wv.part
        units = job_units(i)
        u = [j for j, uu in enumerate(units) if uu[0] == pi and uu[4] <= c0 < uu[4] + uu[5]][0]
        d = ensure_cast(i, u)
        if u == len(units) - 1 and i + 1 < len(wjobs) and job_castable(i + 1) and wstate["issued"] > i + 1:
            ensure_cast(i + 1, 0)
        cw = c0 - units[u][4]
        return V(d.tt, d.ap[:, 0:nk, cw:cw + (c1 - c0)])

    wcast_impl[0] = wcast_real

    def wget():
        i = wstate["next"]
        if wstate["issued"] <= i:
            wissue()
        s_ = i % NSLOT
        wstate["next"] = i + 1
        vs = []
        for pi, (ap, k, n, off) in enumerate(wjobs[i]):
            v_ = V(wsl[s_], wsl[s_].t[:, off:off + k * n].rearrange("p (k n) -> p k n", k=k))
            v_.job, v_.part = i, pi
            vs.append(v_)
        if job_castable(i):
            for u in range(len(job_units(i))):
                ensure_cast(i, u)
        while wstate["issued"] < len(wjobs):
            j = wstate["issued"]
            if job_from_scratch(j):
                if i < last0 or j > i + 3:
                    break
            elif j > i + 1:
                break
            wissue()
        return vs if len(vs) > 1 else vs[0]

    def wcol(wv, k, c0, c1):
        return V(wv.tt, wv.ap[:, k, c0:c1])

    def rms_rstd(src, W):
        ps = P.nps()
        for dc in range(8):
            sq = scr()
            P.act(sq[:, 0:W], src[:, dc, 0:W], AF.Square)
            P.mm1(ps[:, 0:W], onesm, sq[:, 0:W], dc == 0, dc == 7)
        sd = scr()
        P.act(sd[:, 0:W], ps[:, 0:W], AF.Sqrt, bias=epsn, scale=1.0 / D)
        rs = scr()
        P.recip(rs[:, 0:W], sd[:, 0:W])
        return rs

    def bc_s(v2d):
        return V(v2d.tt, v2d.ap.unsqueeze(1).broadcast_to([128, 4, NB]))

    def s3(vv):
        return V(vv.tt, vv.ap.rearrange("p (t b) -> p t b", b=NB))

    def modulate(l, src_rs, Amod, shift_k, npc, has_s, W):
        for dc in range(8):
            eng = "dve" if dc % 2 == 0 else "pool"
            t = scr()
            P.tt(eng, t[:, 0:W], xs[:, dc, 0:W], src_rs[:, 0:W], ALU.mult)
            P.ts(eng, hs[:, dc, 0:npc], t[:, 0:npc], Amod[:, dc, 0:1], mod[l][:, shift_k * 8 + dc, 0:1],
                 ALU.mult, ALU.add)
            if has_s:
                tv = s3(t[:, npc:npc + NSC])
                hv = s3(hs[:, dc, npc:npc + NSC])
                P.tt(eng, tv, tv, bc_s(Amod[:, dc, 1:17]), ALU.mult)
                P.tt(eng, hv, tv, bc_s(mod[l][:, shift_k * 8 + dc, 1:17]), ALU.add)

    def residual(l, ps, dc, gate_k, npc, has_s):
        P.stt(xs[:, dc, 0:npc], ps[:, 0:npc], mod[l][:, gate_k * 8 + dc, 0:1], xs[:, dc, 0:npc], ALU.mult, ALU.add)
        if has_s:
            t = scr()
            P.tt("dve", s3(t[:, 0:NSC]), s3(ps[:, npc:npc + NSC]), bc_s(mod[l][:, gate_k * 8 + dc, 1:17]), ALU.mult)
            P.tt("pool", xs[:, dc, npc:npc + NSC], xs[:, dc, npc:npc + NSC], t[:, 0:NSC], ALU.add)

    def evac_past(ps, dst, pp, sp, npc, has_s, eng=None):
        P.cp(eng or P.ev(), dst[:, pp:pp + npc], ps[:, 0:npc])
        if has_s:
            sc0 = pp + npc + NB * sp
            P.cp(eng or P.ev(), dst[:, sc0:sc0 + NSC], ps[:, npc:npc + NSC])

    def parts(pp, sp, npc, has_s):
        r = [(pp, npc, 1, 0)]
        if has_s:
            r.append((pp + npc + NB * sp, NSC, NB, npc))
        return r

    try:
        col0 = 0
        for si, npc in enumerate(segs):
            has_s = si == 0
            last = si == len(segs) - 1
            W = npc + (NSC if has_s else 0)
            nrc = npc // RC
            nwc = npc // WC
            xv = xT.rearrange("(k p) n -> p k n", p=128)
            rv = rot_d.rearrange("c p n -> p c n")
            P.dma("sp", xs[:, :, 0:npc], xv[:, :, col0:col0 + npc], ds_x)
            P.dma("sp", rot[:, :, 0:npc], rv[:, :, col0:col0 + npc], ds_rot)
            if has_s:
                P.dma("sp", xs[:, :, npc:W], xv[:, :, NPT:NPT + NSC], ds_x)
                P.dma("sp", rot[:, :, npc:W], rv[:, :, NPT:NPT + NSC], ds_rot)
            cosT = rot[:, 0, 0:W]
            sinT = rot[:, 1, 0:W]

            for l in range(nlayers):
                if si == 0:
                    for j in range(12):
                        wv = wget()
                        for c4 in range(4):
                            fc = j * 4 + c4
                            ps = P.nps()
                            wc = wcast(wv, 8, c4 * 128, (c4 + 1) * 128)
                            P.mm(ps[:, 0:17], [(V(wc.tt, wc.ap[:, k, :]), csil_b[:, k, :]) for k in range(8)])
                            P.ts("dve", mod[l][:, fc, :], ps[:, 0:17], vec(l, "bada", fc), None, ALU.add)
                    for dc in range(8):
                        for (Ax, sk, gn) in ((A1[l], 1, "n1g"), (A2[l], 4, "n2g")):
                            gb = V(vt_[l], vt_[l].t[:, VOFF[gn] + dc:VOFF[gn] + dc + 1].to_broadcast([128, 17]))
                            P.stt(Ax[:, dc, :], mod[l][:, sk * 8 + dc, :], 1.0, gb, ALU.add, ALU.mult)

                if stop_after == 'mod':
                    raise _Stop()
                rs = rms_rstd(xs, W)
                modulate(l, rs, A1[l], 0, npc, has_s, W)

                def zchunk(wv, c4):
                    ps = P.nps()
                    wc = wcast(wv, 8, c4 * 128, (c4 + 1) * 128)
                    P.mm(ps[:, 0:W], [(V(wc.tt, wc.ap[:, k, :]), hs[:, k, 0:W]) for k in range(8)])
                    return ps

                if stop_after == 'norm1':
                    raise _Stop()
                PP, SPP = 15, 15
                wv = wget()
                for c in range(2):
                    ps = zchunk(wv, c)
                    P.cp("pool", ubuf[c][:, 0:PP], pool_c[l][:, c, :])
                    if has_s:
                        P.dma("sp", ubuf[c][:, PP + npc:PP + npc + 240], st_pool[l, :, c, :], ds_ub[c])
                    evac_past(ps, ubuf[c], PP, SPP, npc, has_s)
                if stop_after == 'z_a':
                    raise _Stop()
                for c in range(2):
                    ps = zchunk(wv, 2 + c)
                    P.cp(P.ev(), zr[c][:, 0:W], ps[:, 0:W])
                if stop_after == 'z_b':
                    raise _Stop()
                wv = wget()
                for c in range(4):
                    ps = zchunk(wv, c)
                    P.cp(P.ev(), zr[2 + c][:, 0:W], ps[:, 0:W])
                if stop_after == 'z_c':
                    raise _Stop()
                wv = wget()
                for c in range(2):
                    ps = zchunk(wv, c)
                    P.act(zr[6 + c][:, 0:W], ps[:, 0:W], AF.Silu)
                if stop_after == 'z_d':
                    raise _Stop()
                wv = wget()
                for c in range(4):
                    ps = zchunk(wv, c)
                    P.cp(P.ev(), zsc[c][:, 0:W], ps[:, 0:W])
                if stop_after == 'z_e':
                    raise _Stop()
                wv = wget()
                if has_s:
                    P.dma("sp", sc_s[:, :, :], st_sc[l], ds_scs)
                for c in range(2):
                    ps = zchunk(wv, c)
                    evac_past(ps, zsc[4 + c], 2, 2, npc, has_s)
                if has_s:
                    P.dma("sp", sh_s[:, :, :], st_sh[l], ds_shs)
                if stop_after == 'z_f':
                    raise _Stop()
                for j in range(2):
                    wv = wget()
                    for c in range(4):
                        ps = zchunk(wv, c)
                        evac_past(ps, zw[4 * j + c], 1, 1, npc, has_s)

                if stop_after == 'zproj':
                    raise _Stop()
                for c in range(2):
                    U = ubuf[c]
                    Aw, Bw = wide
                    for (c0, n, st, f0) in parts(PP, SPP, npc, has_s):
                        r0 = c0 - 15 * st
                        e = c0 + n
                        P.tt("pool", Aw[:, r0 + st:e], U[:, r0 + st:e], U[:, r0:e - st], ALU.add)
                        P.tt("pool", Bw[:, r0 + 3 * st:e], Aw[:, r0 + 3 * st:e], Aw[:, r0 + st:e - 2 * st], ALU.add)
                        if c == 1:
                            P.tt("pool", Aw[:, r0 + 7 * st:e], Bw[:, r0 + 7 * st:e], Bw[:, r0 + 3 * st:e - 4 * st], ALU.add)
                            P.tt("pool", Bw[:, r0 + 15 * st:e], Aw[:, r0 + 15 * st:e], Aw[:, r0 + 7 * st:e - 8 * st], ALU.add)
                    m = scr()
                    for (c0, n, st, f0) in parts(PP, SPP, npc, has_s):
                        for half, src in ((0, Aw), (1, Bw)):
                            p0, p1 = half * 64, half * 64 + 64
                            P.stt(m[p0:p1, f0:f0 + n], src[p0:p1, c0:c0 + n], tab("invw", p0, p1, c, c + 1),
                                  U[p0:p1, c0:c0 + n], ALU.mult, ALU.subtract)
                            if si == 0 and st == 1:
                                t = scr()
                                P.tt("dve", t[p0:p1, 0:16], src[p0:p1, c0:c0 + 16], tab("icnt", p0, p1, c * 16, c * 16 + 16),
                                     ALU.mult)
                                P.tt("dve", m[p0:p1, 0:16], t[p0:p1, 0:16], U[p0:p1, c0:c0 + 16], ALU.subtract)
                    ps = P.nps()
                    P.mm(ps[:, 0:W], [(pbd[l][:, c, :], m[:, 0:W])])
                    P.ts("dve", yb[c][:, 0:W], ps[:, 0:W], vec(l, "pscale", c), None, ALU.mult)
                    P.cp("pool", pool_c[l][:, c, :], U[:, PP + npc - 15:PP + npc])
                    if has_s:
                        s0 = PP + npc
                        P.dma("act", o_pool_s[l, :, c, :], U[:, s0 + 64:s0 + 64 + 240], ds_ub[c])
                if last:
                    P.dma("act", o_pool_p[l], pool_c[l][:, :, :], ds_carry[0])

                if stop_after == 'pool':
                    raise _Stop()
                qr = [rx[0], rx[1]]
                kr = [rx[2], rx[3]]
                qd = [rx[4], rx[5]]
                for hp in range(2):
                    for (src, dst, scl) in ((zr[hp], qr[hp], 1.0), (zr[2 + hp], kr[hp], 0.125)):
                        ps = P.nps()
                        P.mm(ps[:, 0:W], [(perm, src[:, 0:W])])
                        t1 = scr()
                        P.stt(t1[:, 0:W], src[:, 0:W], scl, cosT, ALU.mult, ALU.mult)
                        P.stt(dst[:, 0:W], ps[:, 0:W], scl, sinT, ALU.mult, ALU.mult)
                        P.tt("pool", dst[:, 0:W], dst[:, 0:W], t1[:, 0:W], ALU.add)
                    qdt = V(tabs, tabs.t[:, TOFF["qdec"] + hp * 128:TOFF["qdec"] + (hp + 1) * 128]
                            .unsqueeze(1).broadcast_to([128, nrc, RC]))
                    P.tt("dve", V(qd[hp], qd[hp].t[:, 0:npc].rearrange("p (c i) -> p c i", i=RC)),
                         V(qr[hp], qr[hp].t[:, 0:npc].rearrange("p (c i) -> p c i", i=RC)), qdt, ALU.mult)
                    if has_s:
                        qd4 = V(tabs, tabs.t[:, TOFF["qdec"] + hp * 128:TOFF["qdec"] + hp * 128 + 4]
                                .unsqueeze(2).broadcast_to([128, 4, NB]))
                        P.tt("dve", s3(qd[hp][:, npc:W]), s3(qr[hp][:, npc:W]), qd4, ALU.mult)
                if stop_after == 'r_a':
                    raise _Stop()
                ops_ = [P.psr[6], P.psr[7]]
                for j in range(nrc):
                    cs = slice(j * RC, (j + 1) * RC)
                    pv = P.nps()
                    pk = P.nps()
                    for hp in range(2):
                        P.tr(pv[:, hp * 128:(hp + 1) * 128], zr[4 + hp][:, cs], ident)
                        P.tr(pk[:, hp * 128:(hp + 1) * 128], kr[hp][:, cs], ident)
                    vtk = vtok[j % 2]
                    ktk = ktok[j % 2]
                    P.cp("act", vtk[:, :], pv[:, 0:256])
                    P.tt("dve", ktk[:, :], pk[:, 0:256], tab("kdec", 0, 128, 0, 256), ALU.mult)
                    for h in range(4):
                        hp, h2 = h // 2, h % 2
                        p0, p1 = h2 * 64, h2 * 64 + 64
                        hc = slice(h * 64, h * 64 + 64)
                        pss = P.nps()
                        P.mm(pss[:, 0:RC], [(kr[hp][p0:p1, cs], qr[hp][p0:p1, cs])])
                        st_ = sTs[h % 2]
                        P.tt("dve", st_[:, :], pss[:, 0:RC], tab("dmask", 0, 128, h * 128, h * 128 + 128), ALU.mult)
                        P.mm(ops_[hp][p0:p1, cs], [(vtk[:, hc], st_[:, :]), (retS[l][p0:p1, hp, :], qd[hp][p0:p1, cs])])
                        psS = P.nps()
                        P.mm(psS[p0:p1, 0:64], [(ktk[:, hc], vtk[:, hc])])
                        P.stt(retS[l][p0:p1, hp, :], retS[l][p0:p1, hp, :], tab("cdec", p0, p1, hp, hp + 1),
                              psS[p0:p1, 0:64], ALU.mult, ALU.add)
                if stop_after == 'r_b':
                    raise _Stop()
                if last:
                    P.dma("act", o_ret_p[l], retS[l][:, :, :], ds_carry[1])
                if has_s:
                    P.dma("sp", sstate[:, :, :, :], st_ret[l], ds_st)
                    ck("ret sstate dma")
                    for b in range(NB):
                        cb = slice(npc + b, npc + NSC, NB)
                        ob = slice(npc + 4 * b, npc + 4 * b + 4)
                        pt = P.nps()
                        for hp in range(2):
                            P.trx(pt[0:4, hp * 128:(hp + 1) * 128], zr[4 + hp][:, cb], idv, 128, 4)
                            P.trx(pt[0:4, 256 + hp * 128:256 + (hp + 1) * 128], kr[hp][:, cb], idv, 128, 4)
                        vtk = vtok[b % 2]
                        ktk = ktok[b % 2]
                        sm = smask[b % 2]
                        P.cp("pool", sm[0:64, :, 0, :], sstate[0:64, b, :, :])
                        P.cp("pool", sm[64:128, :, 1, :], sstate[64:128, b, :, :])
                        ck("ret s transposes")
                        P.cp("act", vtk[0:4, :], pt[0:4, 0:256])
                        ck("ret s cp act")
                        P.cp("act", ktk[0:4, :], pt[0:4, 256:512])
                        P.tt("dve", ktk[0:4, :], ktk[0:4, :], tab("kdec4", 0, 4, 0, 256), ALU.mult)
                        ck("ret s tt dve")
                        for h in range(4):
                            hp, h2 = h // 2, h % 2
                            p0, p1 = h2 * 64, h2 * 64 + 64
                            hc = slice(h * 64, h * 64 + 64)
                            pss = P.nps()
                            P.mm(pss[0:4, 0:4], [(kr[hp][p0:p1, cb], qr[hp][p0:p1, cb])])
                            ck("ret s scores")
                            st_ = sTs[h % 2]
                            P.cp("act", st_[0:4, 0:4], pss[0:4, 0:4])
                            P.tt("dve", st_[0:4, 0:4], st_[0:4, 0:4], tab("dmask", 0, 4, h * 128, h * 128 + 4), ALU.mult)
                            ck("ret s mask")
                            P.mm(ops_[hp][p0:p1, ob], [(vtk[0:4, hc], st_[0:4, 0:4]), (sm[:, hp, h2, :], qd[hp][:, cb])])
                            ck("ret s o")
                            psS = P.nps()
                            P.mm(psS[p0:p1, 0:64], [(ktk[0:4, hc], vtk[0:4, hc])])
                            ck("ret s S mm")
                            P.stt(sstate[p0:p1, b, hp, :], sstate[p0:p1, b, hp, :], tab("cdec4", p0, p1, hp, hp + 1),
                                  psS[p0:p1, 0:64], ALU.mult, ALU.add)
                    P.dma("act", o_ret_s[l], sstate[:, :, :, :], ds_st)
                if stop_after == 'r_c':
                    raise _Stop()
                for hp in range(2):
                    o = scr()
                    P.cp("act", o[:, 0:npc], ops_[hp][:, 0:npc])
                    if has_s:
                        P.cp("act", s3(o[:, npc:W]), V(ops_[hp], ops_[hp].t[:, npc:W].rearrange("p (b t) -> p t b", t=4)))
                    sq = scr()
                    P.act(sq[:, 0:W], o[:, 0:W], AF.Square)
                    ps = P.nps()
                    P.mm(ps[:, 0:W], [(bones, sq[:, 0:W])])
                    sd = scr()
                    P.act(sd[:, 0:W], ps[:, 0:W], AF.Sqrt, bias=epsn, scale=1.0 / 64)
                    P.recip(sd[:, 0:W], sd[:, 0:W])
                    P.tt("dve", o[:, 0:W], o[:, 0:W], sd[:, 0:W], ALU.mult)
                    P.tt("pool", yb[2 + hp][:, 0:W], o[:, 0:W], zr[6 + hp][:, 0:W], ALU.mult)

                if stop_after == 'ret':
                    raise _Stop()
                for c in range(2):
                    pb = zsc[4 + c]
                    hc_ = zsc[c]
                    Bc = zsc[2 + c]
                    P.cp("pool", pb[:, 0:2], sc_c[l][:, c, :])
                    if has_s:
                        P.cp("pool", pb[:, 2 + npc:2 + npc + 32], sc_s[:, c, :])
                    cv = scr()
                    for (c0, n, st, f0) in parts(2, 2, npc, has_s):
                        P.tt("pool", pb[:, c0:c0 + n], pb[:, c0:c0 + n], hc_[:, f0:f0 + n], ALU.mult)
                    for (c0, n, st, f0) in parts(2, 2, npc, has_s):
                        P.ts("dve", cv[:, f0:f0 + n], pb[:, c0:c0 + n], vec(l, "scw", 4 + c), None, ALU.mult)
                        P.stt(cv[:, f0:f0 + n], pb[:, c0 - st:c0 - st + n], vec(l, "scw", 2 + c), cv[:, f0:f0 + n],
                              ALU.mult, ALU.add)
                        P.stt(cv[:, f0:f0 + n], pb[:, c0 - 2 * st:c0 - 2 * st + n], vec(l, "scw", c), cv[:, f0:f0 + n],
                              ALU.mult, ALU.add)
                    P.tt("pool", yb[4 + c][:, 0:W], cv[:, 0:W], Bc[:, 0:W], ALU.mult)
                    P.cp("pool", sc_c[l][:, c, :], pb[:, npc:npc + 2])
                    if has_s:
                        s0 = 2 + npc + 32
                        P.cp("pool", sc_s[:, c, :], pb[:, s0 + 32:s0 + 64])
                if has_s:
                    P.dma("act", o_sc_s[l], sc_s[:, :, :], ds_scs)
                if last:
                    P.dma("act", o_sc_p[l], sc_c[l][:, :, :], ds_carry[2])

                if stop_after == 'sc':
                    raise _Stop()
                R = [arena[0], arena[1]]
                K = [arena[2], arena[3]]
                Vv = [arena[4], arena[5]]
                C6, C7 = arena[6], arena[7]
                flat = [R[0], R[1], K[0], K[1], Vv[0], Vv[1], C6, C7]
                for c in range(8):
                    Z = zw[c]
                    P.cp("pool", Z[:, 0:1], sh_c[l][:, c:c + 1])
                    if has_s:
                        P.cp("pool", Z[:, 1 + npc:1 + npc + 16], sh_s[:, c, :])
                    for (c0, n, st, f0) in parts(1, 1, npc, has_s):
                        t = scr()
                        P.tt("pool", t[:, 0:n], Z[:, c0 - st:c0 - st + n], Z[:, c0:c0 + n], ALU.subtract)
                        P.stt(flat[c][:, f0:f0 + n], t[:, 0:n], vec(l, "mu", c), Z[:, c0:c0 + n], ALU.mult, ALU.add)
                    P.cp("pool", sh_c[l][:, c:c + 1], Z[:, npc:npc + 1])
                    if has_s:
                        s0 = 1 + npc + 16
                        P.cp("pool", sh_s[:, c, :], Z[:, s0 + 48:s0 + 64])
                if has_s:
                    P.dma("act", o_sh_s[l], sh_s[:, :, :], ds_shs)
                if last:
                    P.dma("act", o_sh_p[l], sh_c[l][:, :], ds_carry[3])
                ck("w_shift")
                LW = [arena[14], arena[15]]
                AS = [arena[16], arena[17]]
                G = [arena[18], arena[19]]
                KK = [arena[20], arena[21]]
                AT = [arena[8], arena[9]]
                RT = [arena[10], arena[11]]
                BT = [arena[12], arena[13]]
                KT = [rx[0], rx[1]]
                BH = [rx[2], rx[3]]
                KH = [rx[4], rx[5]]
                BON = [rx[6], rx[7]]
                Y = [rx[8], rx[9]]
                tw = scr()
                P.act(tw[0:64, 0:W], C6[0:64, 0:W], AF.Tanh)
                sg = scr()
                P.act(sg[:, 0:W], C7[:, 0:W], AF.Sigmoid)
                for hp in range(2):
                    cc = slice(hp * 128, hp * 128 + 128)
                    ps = P.nps()
                    P.mm(ps[:, 0:W], [(lwa[l][0:64, cc], tw[0:64, 0:W])])
                    t = scr()
                    P.act(t[:, 0:W], ps[:, 0:W], AF.Sigmoid, bias=vec(l, "w0", hp))
                    P.ts("dve", LW[hp][:, 0:W], t[:, 0:W], -math.exp(-0.5), None, ALU.mult)
                    ps = P.nps()
                    P.mm(ps[:, 0:W], [(lwa[l][64:128, cc], C6[64:128, 0:W])])
                    P.act(AS[hp][:, 0:W], ps[:, 0:W], AF.Sigmoid, bias=vec(l, "a0", hp))
                    ps = P.nps()
                    P.mm(ps[:, 0:W], [(lgl[l][:, cc], sg[:, 0:W])])
                    P.cp("act", G[hp][:, 0:W], ps[:, 0:W])
                ck("w_lora")
                for hp in range(2):
                    t = scr()
                    P.ts("dve", t[:, 0:W], K[hp][:, 0:W], vec(l, "kk", hp), None, ALU.mult)
                    sq = scr()
                    P.act(sq[:, 0:W], t[:, 0:W], AF.Square)
                    ps = P.nps()
                    P.mm(ps[:, 0:W], [(bones, sq[:, 0:W])])
                    sd = scr()
                    P.act(sd[:, 0:W], ps[:, 0:W], AF.Sqrt)
                    P.ts("dve", sd[:, 0:W], sd[:, 0:W], 1e-12, None, ALU.max)
                    P.recip(sd[:, 0:W], sd[:, 0:W])
                    P.tt("dve", KK[hp][:, 0:W], t[:, 0:W], sd[:, 0:W], ALU.mult)
                    t2 = scr()
                    P.ts("dve", t2[:, 0:W], AS[hp][:, 0:W], vec(l, "ka", hp), omka[l][:, hp:hp + 1], ALU.mult, ALU.add)
                    P.tt("pool", K[hp][:, 0:W], K[hp][:, 0:W], t2[:, 0:W], ALU.mult)
                    t3 = scr()
                    P.stt(t3[:, 0:W], R[hp][:, 0:W], vec(l, "rk", hp), K[hp][:, 0:W], ALU.mult, ALU.mult)
                    ps = P.nps()
                    P.mm(ps[:, 0:W], [(bones, t3[:, 0:W])])
                    P.tt("dve", BON[hp][:, 0:W], ps[:, 0:W], Vv[hp][:, 0:W], ALU.mult)
                    P.tt("pool", AS[hp][:, 0:W], AS[hp][:, 0:W], KK[hp][:, 0:W], ALU.mult)
                    T1, T2 = scr(), scr()
                    seq = [LW[hp], T1, T2, T1, T2, T1, T2]
                    for i, s_ in enumerate((1, 2, 4, 8, 16, 32)):
                        a_, b_ = seq[i], seq[i + 1]
                        av = V(a_, a_.t[:, 0:npc].rearrange("p (c i) -> p c i", i=WC))
                        bv = V(b_, b_.t[:, 0:npc].rearrange("p (c i) -> p c i", i=WC))
                        P.tt("pool", V(b_, bv.ap[:, :, s_:WC]), V(a_, av.ap[:, :, s_:WC]), V(a_, av.ap[:, :, 0:WC - s_]), ALU.add)
                        P.cp("pool", V(b_, bv.ap[:, :, 0:s_]), V(a_, av.ap[:, :, 0:s_]))
                    CUM = T2
                    if has_s:
                        l3 = s3(LW[hp][:, npc:W])
                        a3 = s3(T1[:, npc:W])
                        c3 = s3(CUM[:, npc:W])
                        P.tt("pool", V(T1, a3.ap[:, 1:4, :]), V(LW[hp], l3.ap[:, 1:4, :]), V(LW[hp], l3.ap[:, 0:3, :]), ALU.add)
                        P.cp("pool", V(T1, a3.ap[:, 0:1, :]), V(LW[hp], l3.ap[:, 0:1, :]))
                        P.tt("pool", V(CUM, c3.ap[:, 2:4, :]), V(T1, a3.ap[:, 2:4, :]), V(T1, a3.ap[:, 0:2, :]), ALU.add)
                        P.cp("pool", V(CUM, c3.ap[:, 0:2, :]), V(T1, a3.ap[:, 0:2, :]))
                    EC = scr()
                    P.act(EC[:, 0:W], CUM[:, 0:W], AF.Exp)
                    P.tt("dve", RT[hp][:, 0:W], R[hp][:, 0:W], EC[:, 0:W], ALU.mult)
                    EN = scr()
                    P.act(EN[:, 0:W], CUM[:, 0:W], AF.Exp, scale=-1.0)
                    P.tt("dve", BT[hp][:, 0:W], AS[hp][:, 0:W], EN[:, 0:W], ALU.mult)
                    P.tt("pool", KT[hp][:, 0:W], K[hp][:, 0:W], EN[:, 0:W], ALU.mult)
                    cex = scr()
                    P.tt("dve", cex[:, 0:W], CUM[:, 0:W], LW[hp][:, 0:W], ALU.subtract)
                    P.act(cex[:, 0:W], cex[:, 0:W], AF.Exp)
                    P.stt(AT[hp][:, 0:W], KK[hp][:, 0:W], -1.0, cex[:, 0:W], ALU.mult, ALU.mult)
                    dl = scr()
                    cv_ = V(CUM, CUM.t[:, 0:npc].rearrange("p (c i) -> p c i", i=WC))
                    P.tt("dve", V(dl, dl.t[:, 0:npc].rearrange("p (c i) -> p c i", i=WC)),
                         V(CUM, cv_.ap[:, :, WC - 1:WC].broadcast_to([128, nwc, WC])), cv_, ALU.subtract)
                    if has_s:
                        c3 = s3(CUM[:, npc:W])
                        P.tt("dve", s3(dl[:, npc:W]), V(CUM, c3.ap[:, 3:4, :].broadcast_to([128, 4, NB])), c3, ALU.subtract)
                    P.act(dl[:, 0:W], dl[:, 0:W], AF.Exp)
                    P.tt("dve", BH[hp][:, 0:W], AS[hp][:, 0:W], dl[:, 0:W], ALU.mult)
                    P.tt("pool", KH[hp][:, 0:W], K[hp][:, 0:W], dl[:, 0:W], ALU.mult)
                    P.cp("pool", LW[hp][:, 0:W], EC[:, 0:W])
                ck("w_prep")
                ECs = LW

                def rwkv_group(chunks):
                    units = []
                    for ci_, ch in enumerate(chunks):
                        C = ch["C"]
                        cs = ch["cols"]
                        pa_ = P.nps()
                        pb_ = P.nps()
                        for hp in range(2):
                            P.trx(pa_[0:C, hp * 128:(hp + 1) * 128], Vv[hp][:, cs], idv, 128, C)
                            P.trx(pa_[0:C, 256 + hp * 128:256 + (hp + 1) * 128], BH[hp][:, cs], idv, 128, C)
                            P.trx(pb_[0:C, hp * 128:(hp + 1) * 128], KH[hp][:, cs], idv, 128, C)
                        tk = tokm[ch["tok"]]
                        P.cp("act", tk[0:C, 0:512], pa_[0:C, 0:512])
                        P.cp("dve" if C >= 128 else "act", tk[0:C, 512:768], pb_[0:C, 0:256])
                        ch["tk"] = tk
                        for h in range(4):
                            units.append((ch, h, ci_ * 4 + h))
                    ck("w_tok")
                    for (ch, h, u) in units:
                        C, cs = ch["C"], ch["cols"]
                        hp, h2 = h // 2, h % 2
                        p0, p1 = h2 * 64, h2 * 64 + 64
                        at_, rt_, bt_, kt_ = AT[hp][p0:p1, cs], RT[hp][p0:p1, cs], BT[hp][p0:p1, cs], KT[hp][p0:p1, cs]
                        pa_ = P.nps()
                        P.mm(pa_[0:C, 0:C], [(bt_, at_)])
                        P.mm(pa_[0:C, C:2 * C], [(bt_, rt_)])
                        P.mm(pa_[0:C, 2 * C:3 * C], [(kt_, at_)])
                        P.mm(pa_[0:C, 3 * C:4 * C], [(kt_, rt_)])
                        P.mm(pa_[0:C, 4 * C:5 * C], [(at_, bt_)])
                        mk = tab("msk64" if C == 64 else "msk4", 0, C, 0, 5 * C)
                        if C >= 128:
                            P.tt("dve", Am[u][0:C, 0:5 * C], pa_[0:C, 0:5 * C], mk, ALU.mult)
                        else:
                            P.cp("act", Am[u][0:C, 0:5 * C], pa_[0:C, 0:5 * C])
                            P.tt("dve", Am[u][0:C, 0:5 * C], Am[u][0:C, 0:5 * C], mk, ALU.mult)
                        P.tt("pool", Tm[u][0:C, 0:C], Am[u][0:C, 0:C], tab("ident", 0, C, 0, C), ALU.add)
                    ck("w_amat")
                    nlev = 5 if chunks[0]["C"] == 64 else 1
                    cur = {u: (Am[u][0:ch["C"], 0:ch["C"]], Am[u][0:ch["C"], 4 * ch["C"]:5 * ch["C"]]) for (ch, h, u) in units}
                    for lev in range(nlev):
                        pend = {}
                        for (ch, h, u) in units:
                            C = ch["C"]
                            cp_, cpt_ = cur[u]
                            pp_ = P.nps()
                            P.mm(pp_[0:C, C:2 * C], [(cp_, cpt_)])
                            if lev < nlev - 1:
                                P.mm(pp_[0:C, 0:C], [(cpt_, cp_)])
                            buf = Pq[u]
                            eve = P.ev() if C >= 128 else "act"
                            if lev < nlev - 1:
                                P.cp(eve, buf[0:C, 0:2 * C], pp_[0:C, 0:2 * C])
                            else:
                                P.cp(eve, buf[0:C, C:2 * C], pp_[0:C, C:2 * C])
                            cur[u] = (buf[0:C, 0:C], buf[0:C, C:2 * C])
                        for (ch, h, u) in units:
                            C = ch["C"]
                            pt_ = P.nps()
                            P.mm(pt_[0:C, 0:C], [(cur[u][1], Tm[u][0:C, 0:C])])
                            if C >= 128:
                                P.tt("dve", Tm[u][0:C, 0:C], Tm[u][0:C, 0:C], pt_[0:C, 0:C], ALU.add)
                            else:
                                P.cp("act", Am[u][0:C, 4 * C:5 * C], pt_[0:C, 0:C])
                                P.tt("dve", Tm[u][0:C, 0:C], Tm[u][0:C, 0:C], Am[u][0:C, 4 * C:5 * C], ALU.add)
                    ck("w_lev")
                    for ci_, ch in enumerate(chunks):
                        C, cs, tk = ch["C"], ch["cols"], ch["tk"]
                        xt, ut, yt = XTs[0], UTs[0], YTs[0]
                        hd = []
                        for h in range(4):
                            hp, h2 = h // 2, h % 2
                            p0, p1 = h2 * 64, h2 * 64 + 64
                            hd.append((h, hp, p0, p1, slice(h * 64, h * 64 + 64), ci_ * 4 + h))
                        px = P.nps()
                        for (h, hp, p0, p1, hc, u) in hd:
                            P.mm(px[0:C, hc], [(AT[hp][:, cs], ch["stm"](h % 2, hp)),
                                               (Am[u][0:C, 2 * C:3 * C], tk[0:C, hc])])
                        P.cp("act", xt[0:C, :], px[0:C, 0:256])
                        ck("w_xt")
                        pu = P.nps()
                        for (h, hp, p0, p1, hc, u) in hd:
                            P.mm(pu[0:C, hc], [(Tm[u][0:C, 0:C], xt[0:C, hc])])
                        P.cp("dve" if C >= 128 else "act", ut[0:C, :], pu[0:C, 0:256])
                        ck("w_ut")
                        py = P.nps()
                        for (h, hp, p0, p1, hc, u) in hd:
                            P.mm(py[0:C, hc], [(RT[hp][:, cs], ch["stm"](h % 2, hp)),
                                               (Am[u][0:C, C:2 * C], ut[0:C, hc]),
                                               (Am[u][0:C, 3 * C:4 * C], tk[0:C, hc])])
                        P.cp("act", yt[0:C, :], py[0:C, 0:256])
                        ck("w_yt")
                        pS = P.nps()
                        for (h, hp, p0, p1, hc, u) in hd:
                            P.mm(pS[p0:p1, hp * 64:(hp + 1) * 64], [(tk[0:C, 256 + h * 64:256 + (h + 1) * 64], ut[0:C, hc]),
                                                                    (tk[0:C, 512 + h * 64:512 + (h + 1) * 64], tk[0:C, hc])])
                        for hp in range(2):
                            stv = ch["st"](0, 128, hp)
                            P.stt(stv, stv, ch["pc"](hp), pS[:, hp * 64:(hp + 1) * 64], ALU.mult, ALU.add)
                        ch["refresh"]()
                        ck("w_st")
                        pyb = P.nps()
                        for hp in range(2):
                            P.trx(pyb[:, hp * C:(hp + 1) * C], yt[0:C, hp * 128:(hp + 1) * 128], idv, C, 128)
                        for hp in range(2):
                            P.cp(P.ev(), ch["ydst"](hp), pyb[:, hp * C:(hp + 1) * C])

                def refresh_p():
                    P.cp("pool", wkvM[l][0:64, :, 0, :], wkvS[l][0:64, :, :])
                    P.cp("pool", wkvM[l][64:128, :, 1, :], wkvS[l][64:128, :, :])

                for g0 in range(0, nwc, 2):
                    chs = []
                    for k_ in range(g0, min(g0 + 2, nwc)):
                        cs = slice(k_ * WC, (k_ + 1) * WC)
                        chs.append(dict(
                            cols=cs, C=WC, tok=(k_ % 2),
                            st=(lambda p0, p1, hp: wkvS[l][p0:p1, hp, :]),
                            stm=(lambda h2, hp: wkvM[l][:, hp, h2, :]),
                            refresh=refresh_p,
                            pc=(lambda hp, k_=k_: ECs[hp][:, (k_ + 1) * WC - 1:(k_ + 1) * WC]),
                            ydst=(lambda hp, cs=cs: Y[hp][:, cs])))
                    rwkv_group(chs)
                ck("w_prompt")
                if last:
                    P.dma("act", o_wkv_p[l], wkvS[l][:, :, :], ds_carry[4])
                if has_s:
                    P.dma("sp", sstate[:, :, :, :], st_wkv[l], ds_st)
                    for g0 in range(0, NB, 2):
                        chs = []
                        for b in range(g0, g0 + 2):
                            cb = slice(npc + b, npc + NSC, NB)
                            P.cp("pool", smask[b % 2][0:64, :, 0, :], sstate[0:64, b, :, :])
                            P.cp("pool", smask[b % 2][64:128, :, 1, :], sstate[64:128, b, :, :])
                            chs.append(dict(
                                cols=cb, C=4, tok=(b % 2),
                                st=(lambda p0, p1, hp, b=b: sstate[p0:p1, b, hp, :]),
                                stm=(lambda h2, hp, b=b: smask[b % 2][:, hp, h2, :]),
                                refresh=(lambda: None),
                                pc=(lambda hp, b=b: ECs[hp][:, npc + 48 + b:npc + 48 + b + 1]),
                                ydst=(lambda hp, cb=cb: Y[hp][:, cb])))
                        rwkv_group(chs)
                    P.dma("act", o_wkv_s[l], sstate[:, :, :, :], ds_st)
                ck("w_sample")
                for hp in range(2):
                    ps = P.nps()
                    P.mm(ps[:, 0:W], [(bones, Y[hp][:, 0:W])])
                    yc = scr()
                    P.stt(yc[:, 0:W], ps[:, 0:W], -1.0 / 64, Y[hp][:, 0:W], ALU.mult, ALU.add)
                    sq = scr()
                    P.act(sq[:, 0:W], yc[:, 0:W], AF.Square)
                    ps = P.nps()
                    P.mm(ps[:, 0:W], [(bones, sq[:, 0:W])])
                    sd = scr()
                    P.act(sd[:, 0:W], ps[:, 0:W], AF.Sqrt, bias=epsln, scale=1.0 / 64)
                    P.recip(sd[:, 0:W], sd[:, 0:W])
                    P.tt("dve", yc[:, 0:W], yc[:, 0:W], sd[:, 0:W], ALU.mult)
                    P.ts("dve", yc[:, 0:W], yc[:, 0:W], vec(l, "lng", hp), vec(l, "lnb", hp), ALU.mult, ALU.add)
                    P.tt("pool", yc[:, 0:W], yc[:, 0:W], BON[hp][:, 0:W], ALU.add)
                    P.tt("pool", yb[6 + hp][:, 0:W], yc[:, 0:W], G[hp][:, 0:W], ALU.mult)

                if stop_after == 'rwkv':
                    raise _Stop()
                merged = arena[0:8]
                if BF16_DENSE:
                    mbf = [V(arena[8 + dc], arena[8 + dc].t[:, :].bitcast(BF16)[:, 0:W]) for dc in range(8)]
                else:
                    mbf = [merged[dc][:, 0:W] for dc in range(8)]
                for n in range(4):
                    for q in range(4):
                        wg, wb = wget()
                        for c4 in range(2):
                            dc = q * 2 + c4
                            ps = zchunk(wg, c4)
                            sgt = scr()
                            P.act(sgt[:, 0:W], ps[:, 0:W], AF.Sigmoid)
                            pp_ = P.nps()
                            wc2 = wcast(wb, 2, c4 * 128, (c4 + 1) * 128)
                            P.mm(pp_[:, 0:W], [(V(wc2.tt, wc2.ap[:, kk_, :]), yb[2 * n + kk_][:, 0:W]) for kk_ in range(2)])
                            if n == 0:
                                P.tt("dve", merged[dc][:, 0:W], sgt[:, 0:W], pp_[:, 0:W], ALU.mult)
                            else:
                                t = scr()
                                P.tt("dve", t[:, 0:W], sgt[:, 0:W], pp_[:, 0:W], ALU.mult)
                                if BF16_DENSE and n == 3:
                                    P.tt("pool", mbf[dc], merged[dc][:, 0:W], t[:, 0:W], ALU.add)
                                else:
                                    P.tt("pool", merged[dc][:, 0:W], merged[dc][:, 0:W], t[:, 0:W], ALU.add)
                for half in range(2):
                    wv = wget()
                    for c4 in range(4):
                        dc = half * 4 + c4
                        ps = P.nps()
                        wc = wcast(wv, 8, c4 * 128, (c4 + 1) * 128)
                        P.mm(ps[:, 0:W], [(V(wc.tt, wc.ap[:, k, :]), mbf[k]) for k in range(8)])
                        residual(l, ps, dc, 2, npc, has_s)

                if stop_after == 'mix':
                    continue
                rs = rms_rstd(xs, W)
                modulate(l, rs, A2[l], 3, npc, has_s, W)
                if has_s:
                    P.dma("sp", ffn_s[:, :, :], st_ffn[l], ds_ffs)
                actt = arena
                if BF16_DENSE:
                    abf = []
                    for i_ in range(22):
                        if i_ < 20:
                            tl = rx[i_ // 2]
                            abf.append(V(tl, tl.t[:, :].bitcast(BF16)[:, (i_ % 2) * W0:(i_ % 2) * W0 + W]))
                        else:
                            abf.append(yb[i_ - 20][:, 0:W])
                else:
                    abf = [actt[i_][:, 0:W] for i_ in range(22)]
                for j in range(11):
                    wv = wget()
                    for c4 in range(4):
                        ci = 4 * j + c4
                        ps = zchunk(wv, c4)
                        fb = fbuf[ci % 4]
                        P.cp("pool", fb[:, 0:2], ffn_c[l][:, ci, :])
                        if has_s:
                            P.cp("pool", fb[:, 2 + npc:2 + npc + 32], ffn_s[:, ci, :])
                        evac_past(ps, fb, 2, 2, npc, has_s, eng="act")
                        cv = scr()
                        for (c0, n, st, f0) in parts(2, 2, npc, has_s):
                            P.act(cv[:, f0:f0 + n], fb[:, c0:c0 + n], AF.Copy, scale=vec(l, "ffw", 2 * NFC + ci))
                            P.stt(cv[:, f0:f0 + n], fb[:, c0 - st:c0 - st + n], vec(l, "ffw", NFC + ci), cv[:, f0:f0 + n],
                                  ALU.mult, ALU.add)
                            P.stt(cv[:, f0:f0 + n], fb[:, c0 - 2 * st:c0 - 2 * st + n], vec(l, "ffw", ci), cv[:, f0:f0 + n],
                                  ALU.mult, ALU.add)
                        if ci < 22:
                            P.act(actt[ci][:, 0:W], cv[:, 0:W], AF.Silu)
                        else:
                            P.tt("pool", abf[ci - 22], actt[ci - 22][:, 0:W], cv[:, 0:W], ALU.mult)
                        P.cp("pool", ffn_c[l][:, ci, :], fb[:, npc:npc + 2])
                        if has_s:
                            s0 = 2 + npc + 32
                            P.cp("pool", ffn_s[:, ci, :], fb[:, s0 + 32:s0 + 64])
                if has_s:
                    P.dma("act", o_ffn_s[l], ffn_s[:, :, :], ds_ffs)
                if last:
                    P.dma("act", o_ffn_p[l], ffn_c[l][:, :, :], ds_carry[5])
                for j in range(8):
                    wva, wvb = wget()
                    wca = wcast(wva, 11, 0, 128)
                    wcb = wcast(wvb, 11, 0, 128)
                    ps = P.nps()
                    P.mm(ps[:, 0:W], [(V(wca.tt, wca.ap[:, ci, :]), abf[ci]) for ci in range(11)] +
                                     [(V(wcb.tt, wcb.ap[:, ci, :]), abf[11 + ci]) for ci in range(11)])
                    residual(l, ps, j, 5, npc, has_s)

            rs = rms_rstd(xs, W)
            for dc in range(8):
                P.tt("dve", xs[:, dc, 0:W], xs[:, dc, 0:W], rs[:, 0:W], ALU.mult)
                P.ts("pool", xs[:, dc, 0:W], xs[:, dc, 0:W], vec(0, "fg", dc), None, ALU.mult)
            yv = yT.rearrange("(k p) n -> p k n", p=128)
            P.dma("act", yv[:, :, col0:col0 + npc], xs[:, :, 0:npc], ds_y)
            if has_s:
                P.dma("act", yv[:, :, NPT:NPT + NSC], xs[:, :, npc:W], ds_y)
            col0 += npc
    except _Stop:
        pass

    P.finish(P.alld)
    P.emit()
    return nc


_CACHE = {}


def _fm(v):
    v = np.asarray(v, np.float32).reshape(-1, 128)
    return np.ascontiguousarray(v.T)


def prepare(**inp):
    f = lambda k: np.asarray(inp[k], np.float32)
    x_prompt, x_sample = f("x_prompt"), f("x_sample")
    c_prompt, c_sample = f("c_prompt"), f("c_sample")
    ncores = 8
    vec_l = []
    for l in range(L):
        cols = [_fm(f("norm1_g")[l]), _fm(f("norm2_g")[l]), _fm(f("pool_scale")[l])]
        cols += [_fm(f("sc_w")[l][k]) for k in range(3)]
        cols += [_fm(f("rw_mu")[l]), _fm(f("rw_w0")[l]), _fm(f("rw_a0")[l]), _fm(f("rw_k_k")[l]), _fm(f("rw_k_a")[l]),
                 _fm(f("rw_r_k")[l].reshape(-1)), _fm(f("rw_ln_g")[l]), _fm(f("rw_ln_b")[l])]
        cols += [_fm(f("ffn_w")[l][k]) for k in range(3)]
        cols += [_fm(f("final_g")), _fm(f("b_ada")[l])]
        vec_l.append(np.concatenate(cols, axis=1))
    vecs = np.ascontiguousarray(np.stack(vec_l))
    assert vecs.shape == (L, 128, NV), vecs.shape
    pool_bd = np.zeros((L, 128, 2, 128), np.float32)
    pw = f("pool_w")
    for l in range(L):
        for g in range(4):
            c, half = g // 2, g % 2
            pool_bd[l, half * 64:(half + 1) * 64, c, half * 64:(half + 1) * 64] = pw[l, g]
    lora_wa = np.ascontiguousarray(np.concatenate([f("rw_w_lora"), f("rw_a_lora")], axis=1))
    shared = {
        "w_ada": f("w_ada"), "w_in": f("w_in"), "w_br": np.ascontiguousarray(f("w_br").reshape(L, D, D)),
        "w_out": f("w_out"), "w_up": f("w_up"), "w_down": f("w_down"), "vecs": vecs, "pool_bd": pool_bd,
        "lora_wa": lora_wa, "lora_g": f("rw_g_lora"), "tabs": make_tables(), "rot": make_rot(),
    }
    in_maps = []
    for i in range(ncores):
        bs = slice(NB * i, NB * (i + 1))
        xs_ = x_sample[bs]
        xT = np.concatenate([x_prompt[i].T, xs_.transpose(2, 1, 0).reshape(D, NSC)], axis=1)
        c17 = np.concatenate([c_prompt[i:i + 1], c_sample[bs]], axis=0)
        cT = c17.T.reshape(8, 128, 17).transpose(1, 0, 2)
        sp = f("state_pool")[:, bs]
        st_pool = sp.transpose(0, 3, 2, 1).reshape(L, 2, 128, 240).transpose(0, 2, 1, 3)
        sr = f("state_ret")[:, bs]
        st_ret = sr.reshape(L, NB, 2, 2, 64, 64).transpose(0, 3, 4, 1, 2, 5).reshape(L, 128, NB, 2, 64)
        ssc = f("state_sconv")[:, bs]
        st_sc = ssc.transpose(0, 3, 2, 1).reshape(L, 2, 128, 32).transpose(0, 2, 1, 3)
        ssh = f("state_shift")[:, bs]
        st_sh = ssh.transpose(0, 2, 1).reshape(L, 8, 128, NB).transpose(0, 2, 1, 3)
        sw = f("state_wkv")[:, bs]
        st_wkv = sw.reshape(L, NB, 2, 2, 64, 64).transpose(0, 3, 5, 1, 2, 4).reshape(L, 128, NB, 2, 64)
        sf = f("state_ffn")[:, bs]
        st_ffn = sf.transpose(0, 3, 2, 1).reshape(L, NFC, 128, 32).transpose(0, 2, 1, 3)
        m = dict(shared)
        m.update({"xT": xT, "cT": cT, "st_pool": st_pool, "st_ret": st_ret, "st_sc": st_sc, "st_sh": st_sh,
                  "st_wkv": st_wkv, "st_ffn": st_ffn})
        in_maps.append({k: np.ascontiguousarray(v, dtype=np.float32) for k, v in m.items()})
    return in_maps


def kernel(**inp):
    in_maps = prepare(**inp)
    if "nc" not in _CACHE:
        _CACHE["nc"] = build_program()
    res = run_bass_kernel_spmd(_CACHE["nc"], in_maps, core_ids=list(range(8)))
    return assemble(res.results)


def assemble(R, cores=None, npt=NPT):
    B = 8
    cores = list(range(B)) if cores is None else cores
    y_p = np.zeros((B, NPT, D), np.float32)
    y_s = np.zeros((B * NB, 4, D), np.float32)
    p_pool = np.zeros((L, B, 15, 256), np.float32)
    p_ret = np.zeros((L, B, 4, 64, 64), np.float32)
    p_sc = np.zeros((L, B, 2, 256), np.float32)
    p_sh = np.zeros((L, B, 1024), np.float32)
    p_wkv = np.zeros((L, B, 4, 64, 64), np.float32)
    p_ffn = np.zeros((L, B, 2, 2 * DFF), np.float32)
    s_pool = np.zeros((L, B * NB, 15, 256), np.float32)
    s_ret = np.zeros((L, B * NB, 4, 64, 64), np.float32)
    s_sc = np.zeros((L, B * NB, 2, 256), np.float32)
    s_sh = np.zeros((L, B * NB, 1024), np.float32)
    s_wkv = np.zeros((L, B * NB, 4, 64, 64), np.float32)
    s_ffn = np.zeros((L, B * NB, 2, 2 * DFF), np.float32)
    for i in cores:
        r = R[i]
        bs = slice(NB * i, NB * (i + 1))
        yT = np.asarray(r["yT"])
        y_p[i, :npt] = yT[:, :npt].T
        y_s[bs] = yT[:, NPT:].reshape(D, 4, NB).transpose(2, 1, 0)
        p_pool[:, i] = np.asarray(r["o_pool_p"]).transpose(0, 3, 2, 1).reshape(L, 15, 256)
        s_pool[:, bs] = np.asarray(r["o_pool_s"]).reshape(L, 128, 2, 15, NB).transpose(0, 4, 3, 2, 1).reshape(L, NB, 15, 256)
        p_ret[:, i] = np.asarray(r["o_ret_p"]).reshape(L, 2, 64, 2, 64).transpose(0, 3, 1, 2, 4).reshape(L, 4, 64, 64)
        s_ret[:, bs] = np.asarray(r["o_ret_s"]).reshape(L, 2, 64, NB, 2, 64).transpose(0, 3, 4, 1, 2, 5).reshape(L, NB, 4, 64, 64)
        p_sc[:, i] = np.asarray(r["o_sc_p"]).transpose(0, 3, 2, 1).reshape(L, 2, 256)
        s_sc[:, bs] = np.asarray(r["o_sc_s"]).reshape(L, 128, 2, 2, NB).transpose(0, 4, 3, 2, 1).reshape(L, NB, 2, 256)
        p_sh[:, i] = np.asarray(r["o_sh_p"]).transpose(0, 2, 1).reshape(L, 1024)
        s_sh[:, bs] = np.asarray(r["o_sh_s"]).transpose(0, 3, 2, 1).reshape(L, NB, 1024)
        p_wkv[:, i] = np.asarray(r["o_wkv_p"]).reshape(L, 2, 64, 2, 64).transpose(0, 3, 1, 4, 2).reshape(L, 4, 64, 64)
        s_wkv[:, bs] = np.asarray(r["o_wkv_s"]).reshape(L, 2, 64, NB, 2, 64).transpose(0, 3, 4, 1, 5, 2).reshape(L, NB, 4, 64, 64)
        p_ffn[:, i] = np.asarray(r["o_ffn_p"]).transpose(0, 3, 2, 1).reshape(L, 2, 2 * DFF)
        s_ffn[:, bs] = np.asarray(r["o_ffn_s"]).reshape(L, 128, NFC, 2, NB).transpose(0, 4, 3, 2, 1).reshape(L, NB, 2, 2 * DFF)
    return (y_p, y_s, p_pool, p_ret, p_sc, p_sh, p_wkv, p_ffn, s_pool, s_ret, s_sc, s_sh, s_wkv, s_ffn)
```

```python
import contextlib
import math
import numpy as np
import concourse.bass as bass
import concourse.mybir as mybir
from concourse.bass_utils import run_bass_kernel_spmd

F32 = mybir.dt.float32
BF16 = mybir.dt.bfloat16
BF16_DENSE = True
AF = mybir.ActivationFunctionType
ALU = mybir.AluOpType

D = 1024
L = 2
NPT = 2048
NB = 16
NSC = 64
NCOLS = NPT + NSC
SEGS = [256, 384, 384, 384, 384, 256]
WMAX = 384
DFF = 2816
NFC = 44
IN_COLS = 7168
RC = 128
WC = 64
LG = [math.log1p(-(2.0 ** (-5.0 - h))) for h in range(4)]

VOFF = {}
_o = 0
for _n, _w in [("n1g", 8), ("n2g", 8), ("pscale", 2), ("scw", 6), ("mu", 8), ("w0", 2), ("a0", 2), ("kk", 2),
               ("ka", 2), ("rk", 2), ("lng", 2), ("lnb", 2), ("ffw", 132), ("fg", 8), ("bada", 48)]:
    VOFF[_n] = _o
    _o += _w
NV = _o

TOFF = {}
_o = 0
for _n, _w in [("ident", 128), ("ones", 128), ("bones", 128), ("perm", 128), ("dmask", 512), ("qdec", 256),
               ("kdec", 256), ("kdec4", 256), ("cdec", 2), ("cdec4", 2), ("msk64", 320), ("msk4", 20),
               ("invw", 2), ("icnt", 32)]:
    TOFF[_n] = _o
    _o += _w
NTAB = _o


def make_tables():
    t = np.zeros((128, NTAB), np.float64)
    p = np.arange(128)
    t[:, TOFF["ident"]:TOFF["ident"] + 128] = np.eye(128)
    t[:, TOFF["ones"]:TOFF["ones"] + 128] = 1.0
    t[:, TOFF["bones"]:TOFF["bones"] + 128] = (p[:, None] // 64 == p[None, :] // 64)
    partner = np.where(p % 64 < 32, p + 32, p - 32)
    pm = np.zeros((128, 128))
    pm[partner, p] = 1.0
    t[:, TOFF["perm"]:TOFF["perm"] + 128] = pm
    i = np.arange(128)
    for h in range(4):
        g = math.exp(LG[h])
        dm = np.where(i[None, :] >= i[:, None], g ** np.maximum(i[None, :] - i[:, None], 0), 0.0)
        t[:, TOFF["dmask"] + h * 128:TOFF["dmask"] + (h + 1) * 128] = dm
        t[:, TOFF["kdec"] + h * 64:TOFF["kdec"] + (h + 1) * 64] = (g ** (127 - i))[:, None]
        t[0:4, TOFF["kdec4"] + h * 64:TOFF["kdec4"] + (h + 1) * 64] = (g ** (3 - np.arange(4)))[:, None]
    for hp in range(2):
        for h2 in range(2):
            g = math.exp(LG[2 * hp + h2])
            t[h2 * 64:(h2 + 1) * 64, TOFF["qdec"] + hp * 128:TOFF["qdec"] + (hp + 1) * 128] = (g ** (i + 1))[None, :]
            t[h2 * 64:(h2 + 1) * 64, TOFF["cdec"] + hp] = g ** 128
            t[h2 * 64:(h2 + 1) * 64, TOFF["cdec4"] + hp] = g ** 4
    for C, nm in [(64, "msk64"), (4, "msk4")]:
        j = np.arange(C)
        su = (j[:, None] < j[None, :]).astype(np.float64)
        iu = (j[:, None] <= j[None, :]).astype(np.float64)
        sl = (j[:, None] > j[None, :]).astype(np.float64)
        t[0:C, TOFF[nm]:TOFF[nm] + 5 * C] = np.concatenate([su, iu, su, iu, sl], axis=1)
    wins = (2, 4, 8, 16)
    for c in range(2):
        for half in range(2):
            w = wins[2 * c + half]
            t[half * 64:(half + 1) * 64, TOFF["invw"] + c] = 1.0 / w
            t[half * 64:(half + 1) * 64, TOFF["icnt"] + c * 16:TOFF["icnt"] + (c + 1) * 16] = \
                (1.0 / np.minimum(w, np.arange(16) + 1))[None, :]
    return t.astype(np.float32)


def make_rot():
    pos = np.concatenate([np.arange(NPT), np.repeat(16384 + np.arange(4), NB)]).astype(np.float32)
    p = np.arange(128)
    inv = (10000.0 ** (-(p % 32).astype(np.float32) / np.float32(32))).astype(np.float32)
    ang = (pos[None, :] * inv[:, None]).astype(np.float32)
    cos = np.cos(ang.astype(np.float64))
    sin = np.sin(ang.astype(np.float64))
    sgn = np.where(p % 64 < 32, -1.0, 1.0)[:, None]
    return np.stack([cos, sin * sgn]).astype(np.float32)


class Sem:
    def __init__(self, h, inc):
        self.h = h
        self.inc = inc
        self.count = 0


class V:
    def __init__(self, tt, ap):
        self.tt = tt
        self.ap = ap


class TT:
    def __init__(self, t):
        self.t = t
        self.lw = None
        self.rd = {}

    def __getitem__(self, idx):
        return V(self, self.t[idx])


class Prog:
    ENG = ["pe", "act", "dve", "pool", "sp"]

    def __init__(self, nc):
        self.nc = nc
        self.es = contextlib.ExitStack()
        self.esem = {e: Sem(self.es.enter_context(nc.semaphore("s_" + e)), 1) for e in self.ENG}
        self.ops = {e: [] for e in self.ENG}
        self.seen = {e: {} for e in self.ENG}
        self.psr = []
        self.pi = 0
        self.evi = 0
        self.nm = 0
        self.alld = []

    def sb(self, name, shape, dtype=F32):
        return TT(self.es.enter_context(self.nc.sbuf_tensor("sb_" + name, list(shape), dtype)))

    def mkps(self):
        self.psr = [TT(self.es.enter_context(self.nc.psum_tensor("ps%d" % i, [128, 512], F32))) for i in range(8)]

    def nps(self):
        self.pi = (self.pi + 1) % 6
        return self.psr[self.pi]

    def ev(self):
        self.evi ^= 1
        return "act" if self.evi else "dve"

    def dsem(self, name):
        d = Sem(self.es.enter_context(self.nc.semaphore("d_" + name)), 16)
        self.alld.append(d)
        return d

    def _deps(self, eng, reads, writes):
        waits = {}
        own = self.esem[eng]

        def need(sv, raw):
            sem, val = sv
            if sem is own and (eng == "pe" or (not raw and eng != "pool")):
                return
            if waits.get(sem, 0) < val:
                waits[sem] = val

        for r in reads:
            if r.lw is not None:
                need(r.lw, True)
        for w in writes:
            if w.lw is not None:
                need(w.lw, False)
            for s, v in w.rd.items():
                need((s, v), False)
        out = []
        seen = self.seen[eng]
        for s, v in waits.items():
            if seen.get(s, 0) < v:
                seen[s] = v
                out.append((s, v))
        return out

    def _commit(self, sem, reads, writes):
        sem.count += sem.inc
        val = sem.count
        for r in reads:
            if r.rd.get(sem, 0) < val:
                r.rd[sem] = val
        for w in writes:
            w.lw = (sem, val)
            w.rd = {}

    def op(self, eng, fn, reads, writes):
        waits = self._deps(eng, reads, writes)
        sem = self.esem[eng]
        self._commit(sem, reads, writes)
        self.ops[eng].append((waits, fn, sem))

    def dma(self, eng, out, in_, dsem):
        rd = [in_.tt] if isinstance(in_, V) else []
        wr = [out.tt] if isinstance(out, V) else []
        oap = out.ap if isinstance(out, V) else out
        iap = in_.ap if isinstance(in_, V) else in_
        waits = self._deps(eng, rd, wr)
        self._commit(dsem, rd, wr)
        self.ops[eng].append((waits, lambda e: e.dma_start(out=oap, in_=iap), dsem))

    def mm(self, out, pairs):
        n = len(pairs)
        aps = [(l.ap, r.ap) for l, r in pairs]
        oap = out.ap

        def fn(e):
            inst = None
            for i, (l, r) in enumerate(aps):
                inst = e.matmul(oap, l, r, start=(i == 0), stop=(i == n - 1))
            return inst

        rds = []
        for l, r in pairs:
            rds += [l.tt, r.tt]
        self.op("pe", fn, rds, [out.tt])

    def mm1(self, out, l, r, start, stop):
        oap, lap, rap = out.ap, l.ap, r.ap
        self.op("pe", lambda e: e.matmul(oap, lap, rap, start=start, stop=stop), [l.tt, r.tt], [out.tt])

    def tr(self, out, in_, ident):
        oap, iap, dap = out.ap, in_.ap, ident.ap
        self.op("pe", lambda e: e.transpose(out=oap, in_=iap, identity=dap), [in_.tt, ident.tt], [out.tt])

    def trx(self, out, in_, identv, K, M):
        if K == 128 and M == 128:
            self.tr(out, in_, identv(128))
        else:
            self.mm(out, [(in_, identv(K))])

    def tt(self, eng, out, in0, in1, op):
        oap, a, b = out.ap, in0.ap, in1.ap
        self.op(eng, lambda e: e.tensor_tensor(out=oap, in0=a, in1=b, op=op), [in0.tt, in1.tt], [out.tt])

    def ts(self, eng, out, in0, s1, s2, op0, op1=None):
        rds = [in0.tt]
        a1 = s1
        a2 = s2
        if isinstance(s1, V):
            rds.append(s1.tt)
            a1 = s1.ap
        if isinstance(s2, V):
            rds.append(s2.tt)
            a2 = s2.ap
        oap, iap = out.ap, in0.ap
        if op1 is None:
            self.op(eng, lambda e: e.tensor_scalar(out=oap, in0=iap, scalar1=a1, scalar2=None, op0=op0), rds, [out.tt])
        else:
            self.op(eng, lambda e: e.tensor_scalar(out=oap, in0=iap, scalar1=a1, scalar2=a2, op0=op0, op1=op1),
                    rds, [out.tt])

    def stt(self, out, in0, sc, in1, op0, op1):
        rds = [in0.tt, in1.tt]
        a = sc
        if isinstance(sc, V):
            rds.append(sc.tt)
            a = sc.ap
        oap, i0, i1 = out.ap, in0.ap, in1.ap
        self.op("dve", lambda e: e.scalar_tensor_tensor(out=oap, in0=i0, scalar=a, in1=i1, op0=op0, op1=op1),
                rds, [out.tt])

    def act(self, out, in_, func, bias=None, scale=None):
        rds = [in_.tt]
        kw = {}
        if bias is not None:
            if isinstance(bias, V):
                rds.append(bias.tt)
                kw["bias"] = bias.ap
            else:
                kw["bias"] = bias
        if scale is not None:
            if isinstance(scale, V):
                rds.append(scale.tt)
                kw["scale"] = scale.ap
            else:
                kw["scale"] = scale
        oap, iap = out.ap, in_.ap
        self.op("act", lambda e: e.activation(out=oap, in_=iap, func=func, **kw), rds, [out.tt])

    def cp(self, eng, out, in_):
        oap, iap = out.ap, in_.ap
        if eng == "act":
            self.op("act", lambda e: e.activation(out=oap, in_=iap, func=AF.Copy), [in_.tt], [out.tt])
        else:
            self.op(eng, lambda e: e.tensor_copy(out=oap, in_=iap), [in_.tt], [out.tt])

    def recip(self, out, in_):
        oap, iap = out.ap, in_.ap
        self.op("dve", lambda e: e.reciprocal(out=oap, in_=iap), [in_.tt], [out.tt])

    def memset(self, eng, out, val):
        oap = out.ap
        self.op(eng, lambda e: e.memset(oap, val), [], [out.tt])

    def finish(self, dsems):
        waits = [(s, s.count) for s in dsems if s.count > 0]
        self.ops["sp"].append((waits, None, None))

    def emit(self):
        nc = self.nc
        ops = self.ops

        def run(e, lst):
            for waits, fn, sem in lst:
                for s, v in waits:
                    e.wait_ge(s.h, v)
                if fn is not None:
                    fn(e).then_inc(sem.h, sem.inc)

        with nc.Block() as block:
            @block.tensor
            def _(e):
                run(e, ops["pe"])

            @block.scalar
            def _(e):
                run(e, ops["act"])

            @block.vector
            def _(e):
                run(e, ops["dve"])

            @block.gpsimd
            def _(e):
                run(e, ops["pool"])

            @block.sync
            def _(e):
                run(e, ops["sp"])
        self.es.close()


class _Stop(Exception):
    pass


def build_program(nlayers=L, segs=SEGS, stop_after=None):
    nc = bass.Bass("TRN2", target_bir_lowering=False, dynamic_dma_scratch_size=1024)
    P = Prog(nc)

    def din(name, shape):
        return nc.dram_tensor(name, list(shape), F32, kind="ExternalInput").ap()

    def dout(name, shape):
        return nc.dram_tensor(name, list(shape), F32, kind="ExternalOutput").ap()

    xT = din("xT", [D, NCOLS])
    cT = din("cT", [128, 8, 17])
    w_ada = din("w_ada", [L, D, 6 * D])
    w_in = din("w_in", [L, D, IN_COLS])
    w_br = din("w_br", [L, D, D])
    w_out = din("w_out", [L, D, D])
    w_up = din("w_up", [L, D, 2 * DFF])
    w_down = din("w_down", [L, DFF, D])
    vecs = din("vecs", [L, 128, NV])
    pool_bd = din("pool_bd", [L, 128, 2, 128])
    lora_wa = din("lora_wa", [L, 128, 256])
    lora_g = din("lora_g", [L, 128, 256])
    st_pool = din("st_pool", [L, 128, 2, 240])
    st_ret = din("st_ret", [L, 128, NB, 2, 64])
    st_sc = din("st_sc", [L, 128, 2, 32])
    st_sh = din("st_sh", [L, 128, 8, 16])
    st_wkv = din("st_wkv", [L, 128, NB, 2, 64])
    st_ffn = din("st_ffn", [L, 128, NFC, 32])
    tabs_d = din("tabs", [128, NTAB])
    rot_d = din("rot", [2, 128, NCOLS])

    yT = dout("yT", [D, NCOLS])
    o_pool_p = dout("o_pool_p", [L, 128, 2, 15])
    o_pool_s = dout("o_pool_s", [L, 128, 2, 240])
    o_ret_p = dout("o_ret_p", [L, 128, 2, 64])
    o_ret_s = dout("o_ret_s", [L, 128, NB, 2, 64])
    o_sc_p = dout("o_sc_p", [L, 128, 2, 2])
    o_sc_s = dout("o_sc_s", [L, 128, 2, 32])
    o_sh_p = dout("o_sh_p", [L, 128, 8])
    o_sh_s = dout("o_sh_s", [L, 128, 8, 16])
    o_wkv_p = dout("o_wkv_p", [L, 128, 2, 64])
    o_wkv_s = dout("o_wkv_s", [L, 128, NB, 2, 64])
    o_ffn_p = dout("o_ffn_p", [L, 128, NFC, 2])
    o_ffn_s = dout("o_ffn_s", [L, 128, NFC, 32])

    ckc = [0]

    def ck(tag=""):
        ckc[0] += 1
        if stop_after is not None and stop_after == "tag:" + tag:
            print("STOP at tag", tag)
            raise _Stop()
        if stop_after is not None and stop_after.startswith("cnt:") and ckc[0] == int(stop_after[4:]):
            print("STOP at checkpoint", ckc[0], tag)
            raise _Stop()

    P.mkps()
    W0 = WMAX
    ZW = 2 + WMAX + 4
    UW = 576
    tabs = P.sb("tabs", [128, NTAB])
    vt_ = [P.sb("vecs%d" % l, [128, NV]) for l in range(L)]
    pbd = [P.sb("pbd%d" % l, [128, 2, 128]) for l in range(L)]
    lwa = [P.sb("lwa%d" % l, [128, 256]) for l in range(L)]
    lgl = [P.sb("lgl%d" % l, [128, 256]) for l in range(L)]
    omka = [P.sb("omka%d" % l, [128, 2]) for l in range(L)]
    csil = P.sb("csil", [128, 8, 17])
    mod = [P.sb("mod%d" % l, [128, 48, 17]) for l in range(L)]
    A1 = [P.sb("A1_%d" % l, [128, 8, 17]) for l in range(L)]
    A2 = [P.sb("A2_%d" % l, [128, 8, 17]) for l in range(L)]
    rot = P.sb("rot", [128, 2, W0])
    xs = P.sb("xs", [128, 8, W0])
    DD = BF16 if BF16_DENSE else F32
    hs = P.sb("hs", [128, 8, W0], DD)
    csil_b = P.sb("csil_b", [128, 8, 17], DD)
    hbuf = [P.sb("hbuf%d" % i, [128, 2048], BF16) for i in range(3)] if BF16_DENSE else []
    wbi = [0]
    cei = [0]

    wcast_impl = [None]

    def wcast(wv, nk, c0, c1):
        return wcast_impl[0](wv, nk, c0, c1)

    NSLOT = 2
    wsl = [P.sb("wsl%d" % i, [128, 4096]) for i in range(NSLOT)]
    wsem = [P.dsem("w%d" % i) for i in range(NSLOT)]
    arena = [P.sb("ar%d" % i, [128, ZW]) for i in range(22)]
    zr = arena[0:8]
    zsc = arena[8:14]
    zw = arena[14:22]
    yb = [P.sb("yb%d" % i, [128, W0], BF16 if BF16_DENSE else F32) for i in range(8)]
    rx = [P.sb("rx%d" % i, [128, W0]) for i in range(10)]
    ubuf = [P.sb("ubuf%d" % i, [128, UW]) for i in range(2)]
    wide = [P.sb("wide%d" % i, [128, UW]) for i in range(2)]
    fbuf = wide + ubuf
    NSCR = 11
    scr_ = [P.sb("scr%d" % i, [128, W0]) for i in range(NSCR)]
    sci = [0]

    def scr():
        sci[0] = (sci[0] + 1) % NSCR
        return scr_[sci[0]]

    sstate = P.sb("sstate", [128, NB, 2, 64])
    wkvM = [P.sb("wkvM%d" % l, [128, 2, 2, 64]) for l in range(L)]
    smask = [P.sb("smask%d" % i, [128, 2, 2, 64]) for i in range(2)]
    retS = [P.sb("retS%d" % l, [128, 2, 64]) for l in range(L)]
    wkvS = [P.sb("wkvS%d" % l, [128, 2, 64]) for l in range(L)]
    pool_c = [P.sb("poolc%d" % l, [128, 2, 15]) for l in range(L)]
    sc_c = [P.sb("scc%d" % l, [128, 2, 2]) for l in range(L)]
    sh_c = [P.sb("shc%d" % l, [128, 8]) for l in range(L)]
    ffn_c = [P.sb("ffnc%d" % l, [128, NFC, 2]) for l in range(L)]
    sh_s = P.sb("sh_s", [128, 8, 16])
    sc_s = P.sb("sc_s", [128, 2, 32])
    ffn_s = P.sb("ffn_s", [128, NFC, 32])
    vtok = [P.sb("vtok%d" % i, [128, 256]) for i in range(2)]
    ktok = [P.sb("ktok%d" % i, [128, 256]) for i in range(2)]
    sTs = [P.sb("sTs%d" % i, [128, 128]) for i in range(2)]
    NU = 8
    tokm = [P.sb("tokm%d" % i, [64, 768]) for i in range(2)]
    Am = [P.sb("Am%d" % i, [64, 320]) for i in range(NU)]
    Tm = [P.sb("Tm%d" % i, [64, 64]) for i in range(NU)]
    Pq = [P.sb("Pq%d" % i, [64, 128]) for i in range(NU)]
    XTs = [P.sb("XTs%d" % i, [64, 256]) for i in range(1)]
    UTs = [P.sb("UTs%d" % i, [64, 256]) for i in range(1)]
    YTs = [P.sb("YTs%d" % i, [64, 256]) for i in range(1)]
    epsn_t = P.sb("epsn", [128, 4])
    ident = tabs[:, TOFF["ident"]:TOFF["ident"] + 128]
    onesm = tabs[:, TOFF["ones"]:TOFF["ones"] + 128]
    bones = tabs[:, TOFF["bones"]:TOFF["bones"] + 128]
    perm = tabs[:, TOFF["perm"]:TOFF["perm"] + 128]

    def idv(K):
        return tabs[0:K, TOFF["ident"]:TOFF["ident"] + K]

    def tab(name, r0, r1, c0, c1):
        return tabs[r0:r1, TOFF[name] + c0:TOFF[name] + c1]

    ds_const = P.dsem("const")
    ds_x = P.dsem("x")
    ds_rot = P.dsem("rot")
    ds_st = P.dsem("st")
    ds_ub = [P.dsem("ub%d" % i) for i in range(2)]
    ds_scs = P.dsem("scs")
    ds_shs = P.dsem("shs")
    ds_ffs = P.dsem("ffs")
    ds_y = P.dsem("y")
    ds_carry = [P.dsem("carry%d" % i) for i in range(6)]
    out_sems = [ds_y, ds_st, ds_scs, ds_shs, ds_ffs] + ds_ub + ds_carry

    P.dma("sp", tabs[:, :], tabs_d, ds_const)
    for l in range(L):
        P.dma("sp", vt_[l][:, :], vecs[l], ds_const)
        P.dma("sp", pbd[l][:, :, :], pool_bd[l], ds_const)
        P.dma("sp", lwa[l][:, :], lora_wa[l], ds_const)
        P.dma("sp", lgl[l][:, :], lora_g[l], ds_const)
    P.dma("sp", csil[:, :, :], cT, ds_const)
    for t in [tabs, csil] + vt_ + pbd + lwa + lgl:
        t.lw = (ds_const, ds_const.count)
    P.act(csil_b[:, :, :], csil[:, :, :], AF.Silu)
    P.memset("pool", epsn_t[:, 0:1], 1e-6)
    P.memset("pool", epsn_t[:, 1:2], 64e-5)
    epsn = epsn_t[:, 0:1]
    epsln = epsn_t[:, 1:2]
    for l in range(L):
        ka = vt_[l][:, VOFF["ka"]:VOFF["ka"] + 2]
        P.ts("dve", omka[l][:, :], ka, -1.0, 1.0, ALU.mult, ALU.add)
        for t in (retS[l], wkvS[l], pool_c[l], sc_c[l], ffn_c[l]):
            P.memset("pool", t[:, :, :], 0.0)
        P.memset("pool", sh_c[l][:, :], 0.0)
        P.memset("pool", wkvM[l][:, :, :, :], 0.0)
    for i in range(2):
        P.memset("pool", smask[i][:, :, :, :], 0.0)

    def vec(l, name, c):
        o = VOFF[name] + c
        return vt_[l][:, o:o + 1]

    wjobs = []
    wkind = []
    wmeta = []
    WIN_JOBS = [(0, 512), (512, 1024), (1024, 1280), (1280, 1792), (1792, 2048), (2048, 2560), (2560, 3072)]

    def gen_jobs():
        for si in range(len(segs)):
            for l in range(nlayers):
                jl = [0]

                def meta(ada=False, si=si, l=l, jl=jl):
                    if ada:
                        wmeta.append((si, l, None))
                    else:
                        wmeta.append((si, l, jl[0]))
                        jl[0] += 1

                if si == 0:
                    wa = w_ada[l].rearrange("(k p) n -> p k n", p=128)
                    for j in range(12):
                        wjobs.append([(wa[:, :, j * 512:(j + 1) * 512], 8, 512, 0)])
                        wkind.append("bf"); meta(ada=True)
                wi = w_in[l].rearrange("(k p) n -> p k n", p=128)
                for (a, b) in WIN_JOBS:
                    wjobs.append([(wi[:, :, a:b], 8, b - a, 0)])
                    wkind.append("bf"); meta()
                wb = w_br[l].rearrange("(k p) n -> p k n", p=128)
                for n in range(4):
                    for q in range(4):
                        g0 = 3072 + n * 1024 + q * 256
                        wjobs.append([(wi[:, :, g0:g0 + 256], 8, 256, 0),
                                      (wb[:, 2 * n:2 * n + 2, q * 256:(q + 1) * 256], 2, 256, 2048)])
                        wkind.append("bf"); meta()
                wo = w_out[l].rearrange("(k p) n -> p k n", p=128)
                for half in range(2):
                    wjobs.append([(wo[:, :, half * 512:(half + 1) * 512], 8, 512, 0)])
                    wkind.append("bf"); meta()
                wu = w_up[l].rearrange("(k p) n -> p k n", p=128)
                for j in range(11):
                    wjobs.append([(wu[:, :, j * 512:(j + 1) * 512], 8, 512, 0)])
                    wkind.append("bf"); meta()
                wd = w_down[l].rearrange("(k p) n -> p k n", p=128)
                for j in range(8):
                    wjobs.append([(wd[:, 0:11, j * 128:(j + 1) * 128], 11, 128, 0),
                                  (wd[:, 11:22, j * 128:(j + 1) * 128], 11, 128, 1408)])
                    wkind.append("bf"); meta()

    gen_jobs()
    assert len(wkind) == len(wjobs) == len(wmeta), (len(wkind), len(wjobs), len(wmeta))
    NJL = 44
    HBM_BF16 = BF16_DENSE and len(segs) > 1
    if HBM_BF16:
        wscr = nc.dram_tensor("wscr", [L, NJL, 128, 4096], BF16, kind="Internal").ap()
        hst = [P.dsem("hst%d" % i) for i in range(3)]
    first_b = [True]
    wslh = [TT(wsl[q // 2].t) for q in range(4)] if HBM_BF16 else []
    hsem = [P.dsem("hs%d" % q) for q in range(4)] if HBM_BF16 else []
    inherited = set()
    last0 = max([i for i in range(len(wjobs)) if wmeta[i][0] == 0] + [-1])

    def half_of(i):
        q = i % 4
        if q not in inherited:
            inherited.add(q)
            par = wsl[q // 2]
            wslh[q].lw = par.lw
            wslh[q].rd = dict(par.rd)
        return q

    def job_from_scratch(i):
        si_, l_, jl_ = wmeta[i]
        return HBM_BF16 and si_ >= 1 and jl_ is not None

    def unit_off(i, u):
        return sum(k_ * nc2 for (_, k_, _, _, _, nc2) in job_units(i)[:u])
    wstate = {"issued": 0, "next": 0}

    def wissue():
        i = wstate["issued"]
        s = i % NSLOT
        if job_from_scratch(i):
            si_, l_, jl_ = wmeta[i]
            tot = unit_off(i, len(job_units(i)))
            if si_ == 1:
                P.ops["sp"].append(([(d, d.count) for d in hst if d.count > 0], None, None))
            q = half_of(i)
            dst = V(wslh[q], wslh[q].t[:, :].bitcast(BF16)[:, (q % 2) * 4096:(q % 2) * 4096 + tot])
            P.dma("sp", dst, wscr[l_, jl_, :, 0:tot], hsem[q])
        else:
            for (ap, k, n, off) in wjobs[i]:
                dst = V(wsl[s], wsl[s].t[:, off:off + k * n].rearrange("p (k n) -> p k n", k=k))
                P.dma("sp", dst, ap, wsem[s])
        wstate["issued"] = i + 1

    cast_done = {}
    hbi = [0]

    def job_castable(i):
        return BF16_DENSE and wkind[i] == "bf"

    def job_units(i):
        us = []
        for pi, (ap, k, n, off) in enumerate(wjobs[i]):
            for c0 in range(0, n, 256):
                us.append((pi, k, n, off, c0, min(256, n - c0)))
        return us

    def ensure_cast(i, u):
        if (i, u) in cast_done:
            return cast_done[(i, u)]
        pi, k, n, off, c0, nc_ = job_units(i)[u]
        sl = wsl[i % NSLOT]
        if job_from_scratch(i):
            q = i % 4
            uo = (q % 2) * 4096 + unit_off(i, u)
            dst = V(wslh[q], wslh[q].t[:, :].bitcast(BF16)[:, uo:uo + k * nc_].rearrange("p (k n) -> p k n", k=k))
            cast_done[(i, u)] = dst
            return dst
        src = V(sl, sl.t[:, off:off + k * n].rearrange("p (k n) -> p k n", k=k)[:, :, c0:c0 + nc_])
        hbi[0] = (hbi[0] + 1) % 3
        cei[0] = (cei[0] + 1) % 2
        hb = hbuf[hbi[0]]
        dst = V(hb, hb.t[:, 0:k * nc_].rearrange("p (k n) -> p k n", k=k))
        P.cp(("dve", "act")[cei[0]], dst, src)
        if HBM_BF16 and wmeta[i][0] == 0 and wmeta[i][2] is not None:
            uo = unit_off(i, u)
            P.dma("act", wscr[wmeta[i][1], wmeta[i][2], :, uo:uo + k * nc_], V(hb, hb.t[:, 0:k * nc_]), hst[hbi[0]])
        cast_done[(i, u)] = dst
        return dst

    def wcast_real(wv, nk, c0, c1):
        if not BF16_DENSE:
            return V(wv.tt, wv.ap[:, 0:nk, c0:c1])
        i, pi = wv.job, wv.part
        units = job_units(i)
        u = [j for j, uu in enumerate(units) if uu[0] == pi and uu[4] <= c0 < uu[4] + uu[5]][0]
        d = ensure_cast(i, u)
        if u == len(units) - 1 and i + 1 < len(wjobs) and job_castable(i + 1) and wstate["issued"] > i + 1:
            ensure_cast(i + 1, 0)
        cw = c0 - units[u][4]
        return V(d.tt, d.ap[:, 0:nk, cw:cw + (c1 - c0)])

    wcast_impl[0] = wcast_real

    def wget():
        i = wstate["next"]
        if wstate["issued"] <= i:
            wissue()
        s_ = i % NSLOT
        wstate["next"] = i + 1
        vs = []
        for pi, (ap, k, n, off) in enumerate(wjobs[i]):
            v_ = V(wsl[s_], wsl[s_].t[:, off:off + k * n].rearrange("p (k n) -> p k n", k=k))
            v_.job, v_.part = i, pi
            vs.append(v_)
        if job_castable(i):
            for u in range(len(job_units(i))):
                ensure_cast(i, u)
        while wstate["issued"] < len(wjobs):
            j = wstate["issued"]
            if job_from_scratch(j):
                if i < last0 or j > i + 3:
                    break
            elif j > i + 1:
                break
            wissue()
        return vs if len(vs) > 1 else vs[0]

    def wcol(wv, k, c0, c1):
        return V(wv.tt, wv.ap[:, k, c0:c1])

    def rms_rstd(src, W):
        ps = P.nps()
        for dc in range(8):
            sq = scr()
            P.act(sq[:, 0:W], src[:, dc, 0:W], AF.Square)
            P.mm1(ps[:, 0:W], onesm, sq[:, 0:W], dc == 0, dc == 7)
        sd = scr()
        P.act(sd[:, 0:W], ps[:, 0:W], AF.Sqrt, bias=epsn, scale=1.0 / D)
        rs = scr()
        P.recip(rs[:, 0:W], sd[:, 0:W])
        return rs

    def bc_s(v2d):
        return V(v2d.tt, v2d.ap.unsqueeze(1).broadcast_to([128, 4, NB]))

    def s3(vv):
        return V(vv.tt, vv.ap.rearrange("p (t b) -> p t b", b=NB))

    def modulate(l, src_rs, Amod, shift_k, npc, has_s, W):
        for dc in range(8):
            eng = "dve" if dc % 2 == 0 else "pool"
            t = scr()
            P.tt(eng, t[:, 0:W], xs[:, dc, 0:W], src_rs[:, 0:W], ALU.mult)
            P.ts(eng, hs[:, dc, 0:npc], t[:, 0:npc], Amod[:, dc, 0:1], mod[l][:, shift_k * 8 + dc, 0:1],
                 ALU.mult, ALU.add)
            if has_s:
                tv = s3(t[:, npc:npc + NSC])
                hv = s3(hs[:, dc, npc:npc + NSC])
                P.tt(eng, tv, tv, bc_s(Amod[:, dc, 1:17]), ALU.mult)
                P.tt(eng, hv, tv, bc_s(mod[l][:, shift_k * 8 + dc, 1:17]), ALU.add)

    def residual(l, ps, dc, gate_k, npc, has_s):
        P.stt(xs[:, dc, 0:npc], ps[:, 0:npc], mod[l][:, gate_k * 8 + dc, 0:1], xs[:, dc, 0:npc], ALU.mult, ALU.add)
        if has_s:
            t = scr()
            P.tt("dve", s3(t[:, 0:NSC]), s3(ps[:, npc:npc + NSC]), bc_s(mod[l][:, gate_k * 8 + dc, 1:17]), ALU.mult)
            P.tt("pool", xs[:, dc, npc:npc + NSC], xs[:, dc, npc:npc + NSC], t[:, 0:NSC], ALU.add)

    def evac_past(ps, dst, pp, sp, npc, has_s, eng=None):
        P.cp(eng or P.ev(), dst[:, pp:pp + npc], ps[:, 0:npc])
        if has_s:
            sc0 = pp + npc + NB * sp
            P.cp(eng or P.ev(), dst[:, sc0:sc0 + NSC], ps[:, npc:npc + NSC])

    def parts(pp, sp, npc, has_s):
        r = [(pp, npc, 1, 0)]
        if has_s:
            r.append((pp + npc + NB * sp, NSC, NB, npc))
        return r

    try:
        col0 = 0
        for si, npc in enumerate(segs):
            has_s = si == 0
            last = si == len(segs) - 1
            W = npc + (NSC if has_s else 0)
            nrc = npc // RC
            nwc = npc // WC
            xv = xT.rearrange("(k p) n -> p k n", p=128)
            rv = rot_d.rearrange("c p n -> p c n")
            P.dma("sp", xs[:, :, 0:npc], xv[:, :, col0:col0 + npc], ds_x)
            P.dma("sp", rot[:, :, 0:npc], rv[:, :, col0:col0 + npc], ds_rot)
            if has_s:
                P.dma("sp", xs[:, :, npc:W], xv[:, :, NPT:NPT + NSC], ds_x)
                P.dma("sp", rot[:, :, npc:W], rv[:, :, NPT:NPT + NSC], ds_rot)
            cosT = rot[:, 0, 0:W]
            sinT = rot[:, 1, 0:W]

            for l in range(nlayers):
                if si == 0:
                    for j in range(12):
                        wv = wget()
                        for c4 in range(4):
                            fc = j * 4 + c4
                            ps = P.nps()
                            wc = wcast(wv, 8, c4 * 128, (c4 + 1) * 128)
                            P.mm(ps[:, 0:17], [(V(wc.tt, wc.ap[:, k, :]), csil_b[:, k, :]) for k in range(8)])
                            P.ts("dve", mod[l][:, fc, :], ps[:, 0:17], vec(l, "bada", fc), None, ALU.add)
                    for dc in range(8):
                        for (Ax, sk, gn) in ((A1[l], 1, "n1g"), (A2[l], 4, "n2g")):
                            gb = V(vt_[l], vt_[l].t[:, VOFF[gn] + dc:VOFF[gn] + dc + 1].to_broadcast([128, 17]))
                            P.stt(Ax[:, dc, :], mod[l][:, sk * 8 + dc, :], 1.0, gb, ALU.add, ALU.mult)

                if stop_after == 'mod':
                    raise _Stop()
                rs = rms_rstd(xs, W)
                modulate(l, rs, A1[l], 0, npc, has_s, W)

                def zchunk(wv, c4):
                    ps = P.nps()
                    wc = wcast(wv, 8, c4 * 128, (c4 + 1) * 128)
                    P.mm(ps[:, 0:W], [(V(wc.tt, wc.ap[:, k, :]), hs[:, k, 0:W]) for k in range(8)])
                    return ps

                if stop_after == 'norm1':
                    raise _Stop()
                PP, SPP = 15, 15
                wv = wget()
                for c in range(2):
                    ps = zchunk(wv, c)
                    P.cp("pool", ubuf[c][:, 0:PP], pool_c[l][:, c, :])
                    if has_s:
                        P.dma("sp", ubuf[c][:, PP + npc:PP + npc + 240], st_pool[l, :, c, :], ds_ub[c])
                    evac_past(ps, ubuf[c], PP, SPP, npc, has_s)
                if stop_after == 'z_a':
                    raise _Stop()
                for c in range(2):
                    ps = zchunk(wv, 2 + c)
                    P.cp(P.ev(), zr[c][:, 0:W], ps[:, 0:W])
                if stop_after == 'z_b':
                    raise _Stop()
                wv = wget()
                for c in range(4):
                    ps = zchunk(wv, c)
                    P.cp(P.ev(), zr[2 + c][:, 0:W], ps[:, 0:W])
                if stop_after == 'z_c':
                    raise _Stop()
                wv = wget()
                for c in range(2):
                    ps = zchunk(wv, c)
                    P.act(zr[6 + c][:, 0:W], ps[:, 0:W], AF.Silu)
                if stop_after == 'z_d':
                    raise _Stop()
                wv = wget()
                for c in range(4):
                    ps = zchunk(wv, c)
                    P.cp(P.ev(), zsc[c][:, 0:W], ps[:, 0:W])
                if stop_after == 'z_e':
                    raise _Stop()
                wv = wget()
                if has_s:
                    P.dma("sp", sc_s[:, :, :], st_sc[l], ds_scs)
                for c in range(2):
                    ps = zchunk(wv, c)
                    evac_past(ps, zsc[4 + c], 2, 2, npc, has_s)
                if has_s:
                    P.dma("sp", sh_s[:, :, :], st_sh[l], ds_shs)
                if stop_after == 'z_f':
                    raise _Stop()
                for j in range(2):
                    wv = wget()
                    for c in range(4):
                        ps = zchunk(wv, c)
                        evac_past(ps, zw[4 * j + c], 1, 1, npc, has_s)

                if stop_after == 'zproj':
                    raise _Stop()
                for c in range(2):
                    U = ubuf[c]
                    Aw, Bw = wide
                    for (c0, n, st, f0) in parts(PP, SPP, npc, has_s):
                        r0 = c0 - 15 * st
                        e = c0 + n
                        P.tt("pool", Aw[:, r0 + st:e], U[:, r0 + st:e], U[:, r0:e - st], ALU.add)
                        P.tt("pool", Bw[:, r0 + 3 * st:e], Aw[:, r0 + 3 * st:e], Aw[:, r0 + st:e - 2 * st], ALU.add)
                        if c == 1:
                            P.tt("pool", Aw[:, r0 + 7 * st:e], Bw[:, r0 + 7 * st:e], Bw[:, r0 + 3 * st:e - 4 * st], ALU.add)
                            P.tt("pool", Bw[:, r0 + 15 * st:e], Aw[:, r0 + 15 * st:e], Aw[:, r0 + 7 * st:e - 8 * st], ALU.add)
                    m = scr()
                    for (c0, n, st, f0) in parts(PP, SPP, npc, has_s):
                        for half, src in ((0, Aw), (1, Bw)):
                            p0, p1 = half * 64, half * 64 + 64
                            P.stt(m[p0:p1, f0:f0 + n], src[p0:p1, c0:c0 + n], tab("invw", p0, p1, c, c + 1),
                                  U[p0:p1, c0:c0 + n], ALU.mult, ALU.subtract)
                            if si == 0 and st == 1:
                                t = scr()
                                P.tt("dve", t[p0:p1, 0:16], src[p0:p1, c0:c0 + 16], tab("icnt", p0, p1, c * 16, c * 16 + 16),
                                     ALU.mult)
                                P.tt("dve", m[p0:p1, 0:16], t[p0:p1, 0:16], U[p0:p1, c0:c0 + 16], ALU.subtract)
                    ps = P.nps()
                    P.mm(ps[:, 0:W], [(pbd[l][:, c, :], m[:, 0:W])])
                    P.ts("dve", yb[c][:, 0:W], ps[:, 0:W], vec(l, "pscale", c), None, ALU.mult)
                    P.cp("pool", pool_c[l][:, c, :], U[:, PP + npc - 15:PP + npc])
                    if has_s:
                        s0 = PP + npc
                        P.dma("act", o_pool_s[l, :, c, :], U[:, s0 + 64:s0 + 64 + 240], ds_ub[c])
                if last:
                    P.dma("act", o_pool_p[l], pool_c[l][:, :, :], ds_carry[0])

                if stop_after == 'pool':
                    raise _Stop()
                qr = [rx[0], rx[1]]
                kr = [rx[2], rx[3]]
                qd = [rx[4], rx[5]]
                for hp in range(2):
                    for (src, dst, scl) in ((zr[hp], qr[hp], 1.0), (zr[2 + hp], kr[hp], 0.125)):
                        ps = P.nps()
                        P.mm(ps[:, 0:W], [(perm, src[:, 0:W])])
                        t1 = scr()
                        P.stt(t1[:, 0:W], src[:, 0:W], scl, cosT, ALU.mult, ALU.mult)
                        P.stt(dst[:, 0:W], ps[:, 0:W], scl, sinT, ALU.mult, ALU.mult)
                        P.tt("pool", dst[:, 0:W], dst[:, 0:W], t1[:, 0:W], ALU.add)
                    qdt = V(tabs, tabs.t[:, TOFF["qdec"] + hp * 128:TOFF["qdec"] + (hp + 1) * 128]
                            .unsqueeze(1).broadcast_to([128, nrc, RC]))
                    P.tt("dve", V(qd[hp], qd[hp].t[:, 0:npc].rearrange("p (c i) -> p c i", i=RC)),
                         V(qr[hp], qr[hp].t[:, 0:npc].rearrange("p (c i) -> p c i", i=RC)), qdt, ALU.mult)
                    if has_s:
                        qd4 = V(tabs, tabs.t[:, TOFF["qdec"] + hp * 128:TOFF["qdec"] + hp * 128 + 4]
                                .unsqueeze(2).broadcast_to([128, 4, NB]))
                        P.tt("dve", s3(qd[hp][:, npc:W]), s3(qr[hp][:, npc:W]), qd4, ALU.mult)
                if stop_after == 'r_a':
                    raise _Stop()
                ops_ = [P.psr[6], P.psr[7]]
                for j in range(nrc):
                    cs = slice(j * RC, (j + 1) * RC)
                    pv = P.nps()
                    pk = P.nps()
                    for hp in range(2):
                        P.tr(pv[:, hp * 128:(hp + 1) * 128], zr[4 + hp][:, cs], ident)
                        P.tr(pk[:, hp * 128:(hp + 1) * 128], kr[hp][:, cs], ident)
                    vtk = vtok[j % 2]
                    ktk = ktok[j % 2]
                    P.cp("act", vtk[:, :], pv[:, 0:256])
                    P.tt("dve", ktk[:, :], pk[:, 0:256], tab("kdec", 0, 128, 0, 256), ALU.mult)
                    for h in range(4):
                        hp, h2 = h // 2, h % 2
                        p0, p1 = h2 * 64, h2 * 64 + 64
                        hc = slice(h * 64, h * 64 + 64)
                        pss = P.nps()
                        P.mm(pss[:, 0:RC], [(kr[hp][p0:p1, cs], qr[hp][p0:p1, cs])])
                        st_ = sTs[h % 2]
                        P.tt("dve", st_[:, :], pss[:, 0:RC], tab("dmask", 0, 128, h * 128, h * 128 + 128), ALU.mult)
                        P.mm(ops_[hp][p0:p1, cs], [(vtk[:, hc], st_[:, :]), (retS[l][p0:p1, hp, :], qd[hp][p0:p1, cs])])
                        psS = P.nps()
                        P.mm(psS[p0:p1, 0:64], [(ktk[:, hc], vtk[:, hc])])
                        P.stt(retS[l][p0:p1, hp, :], retS[l][p0:p1, hp, :], tab("cdec", p0, p1, hp, hp + 1),
                              psS[p0:p1, 0:64], ALU.mult, ALU.add)
                if stop_after == 'r_b':
                    raise _Stop()
                if last:
                    P.dma("act", o_ret_p[l], retS[l][:, :, :], ds_carry[1])
                if has_s:
                    P.dma("sp", sstate[:, :, :, :], st_ret[l], ds_st)
                    ck("ret sstate dma")
                    for b in range(NB):
                        cb = slice(npc + b, npc + NSC, NB)
                        ob = slice(npc + 4 * b, npc + 4 * b + 4)
                        pt = P.nps()
                        for hp in range(2):
                            P.trx(pt[0:4, hp * 128:(hp + 1) * 128], zr[4 + hp][:, cb], idv, 128, 4)
                            P.trx(pt[0:4, 256 + hp * 128:256 + (hp + 1) * 128], kr[hp][:, cb], idv, 128, 4)
                        vtk = vtok[b % 2]
                        ktk = ktok[b % 2]
                        sm = smask[b % 2]
                        P.cp("pool", sm[0:64, :, 0, :], sstate[0:64, b, :, :])
                        P.cp("pool", sm[64:128, :, 1, :], sstate[64:128, b, :, :])
                        ck("ret s transposes")
                        P.cp("act", vtk[0:4, :], pt[0:4, 0:256])
                        ck("ret s cp act")
                        P.cp("act", ktk[0:4, :], pt[0:4, 256:512])
                        P.tt("dve", ktk[0:4, :], ktk[0:4, :], tab("kdec4", 0, 4, 0, 256), ALU.mult)
                        ck("ret s tt dve")
                        for h in range(4):
                            hp, h2 = h // 2, h % 2
                            p0, p1 = h2 * 64, h2 * 64 + 64
                            hc = slice(h * 64, h * 64 + 64)
                            pss = P.nps()
                            P.mm(pss[0:4, 0:4], [(kr[hp][p0:p1, cb], qr[hp][p0:p1, cb])])
                            ck("ret s scores")
                            st_ = sTs[h % 2]
                            P.cp("act", st_[0:4, 0:4], pss[0:4, 0:4])
                            P.tt("dve", st_[0:4, 0:4], st_[0:4, 0:4], tab("dmask", 0, 4, h * 128, h * 128 + 4), ALU.mult)
                            ck("ret s mask")
                            P.mm(ops_[hp][p0:p1, ob], [(vtk[0:4, hc], st_[0:4, 0:4]), (sm[:, hp, h2, :], qd[hp][:, cb])])
                            ck("ret s o")
                            psS = P.nps()
                            P.mm(psS[p0:p1, 0:64], [(ktk[0:4, hc], vtk[0:4, hc])])
                            ck("ret s S mm")
                            P.stt(sstate[p0:p1, b, hp, :], sstate[p0:p1, b, hp, :], tab("cdec4", p0, p1, hp, hp + 1),
                                  psS[p0:p1, 0:64], ALU.mult, ALU.add)
                    P.dma("act", o_ret_s[l], sstate[:, :, :, :], ds_st)
                if stop_after == 'r_c':
                    raise _Stop()
                for hp in range(2):
                    o = scr()
                    P.cp("act", o[:, 0:npc], ops_[hp][:, 0:npc])
                    if has_s:
                        P.cp("act", s3(o[:, npc:W]), V(ops_[hp], ops_[hp].t[:, npc:W].rearrange("p (b t) -> p t b", t=4)))
                    sq = scr()
                    P.act(sq[:, 0:W], o[:, 0:W], AF.Square)
                    ps = P.nps()
                    P.mm(ps[:, 0:W], [(bones, sq[:, 0:W])])
                    sd = scr()
                    P.act(sd[:, 0:W], ps[:, 0:W], AF.Sqrt, bias=epsn, scale=1.0 / 64)
                    P.recip(sd[:, 0:W], sd[:, 0:W])
                    P.tt("dve", o[:, 0:W], o[:, 0:W], sd[:, 0:W], ALU.mult)
                    P.tt("pool", yb[2 + hp][:, 0:W], o[:, 0:W], zr[6 + hp][:, 0:W], ALU.mult)

                if stop_after == 'ret':
                    raise _Stop()
                for c in range(2):
                    pb = zsc[4 + c]
                    hc_ = zsc[c]
                    Bc = zsc[2 + c]
                    P.cp("pool", pb[:, 0:2], sc_c[l][:, c, :])
                    if has_s:
                        P.cp("pool", pb[:, 2 + npc:2 + npc + 32], sc_s[:, c, :])
                    cv = scr()
                    for (c0, n, st, f0) in parts(2, 2, npc, has_s):
                        P.tt("pool", pb[:, c0:c0 + n], pb[:, c0:c0 + n], hc_[:, f0:f0 + n], ALU.mult)
                    for (c0, n, st, f0) in parts(2, 2, npc, has_s):
                        P.ts("dve", cv[:, f0:f0 + n], pb[:, c0:c0 + n], vec(l, "scw", 4 + c), None, ALU.mult)
                        P.stt(cv[:, f0:f0 + n], pb[:, c0 - st:c0 - st + n], vec(l, "scw", 2 + c), cv[:, f0:f0 + n],
                              ALU.mult, ALU.add)
                        P.stt(cv[:, f0:f0 + n], pb[:, c0 - 2 * st:c0 - 2 * st + n], vec(l, "scw", c), cv[:, f0:f0 + n],
                              ALU.mult, ALU.add)
                    P.tt("pool", yb[4 + c][:, 0:W], cv[:, 0:W], Bc[:, 0:W], ALU.mult)
                    P.cp("pool", sc_c[l][:, c, :], pb[:, npc:npc + 2])
                    if has_s:
                        s0 = 2 + npc + 32
                        P.cp("pool", sc_s[:, c, :], pb[:, s0 + 32:s0 + 64])
                if has_s:
                    P.dma("act", o_sc_s[l], sc_s[:, :, :], ds_scs)
                if last:
                    P.dma("act", o_sc_p[l], sc_c[l][:, :, :], ds_carry[2])

                if stop_after == 'sc':
                    raise _Stop()
                R = [arena[0], arena[1]]
                K = [arena[2], arena[3]]
                Vv = [arena[4], arena[5]]
                C6, C7 = arena[6], arena[7]
                flat = [R[0], R[1], K[0], K[1], Vv[0], Vv[1], C6, C7]
                for c in range(8):
                    Z = zw[c]
                    P.cp("pool", Z[:, 0:1], sh_c[l][:, c:c + 1])
                    if has_s:
                        P.cp("pool", Z[:, 1 + npc:1 + npc + 16], sh_s[:, c, :])
                    for (c0, n, st, f0) in parts(1, 1, npc, has_s):
                        t = scr()
                        P.tt("pool", t[:, 0:n], Z[:, c0 - st:c0 - st + n], Z[:, c0:c0 + n], ALU.subtract)
                        P.stt(flat[c][:, f0:f0 + n], t[:, 0:n], vec(l, "mu", c), Z[:, c0:c0 + n], ALU.mult, ALU.add)
                    P.cp("pool", sh_c[l][:, c:c + 1], Z[:, npc:npc + 1])
                    if has_s:
                        s0 = 1 + npc + 16
                        P.cp("pool", sh_s[:, c, :], Z[:, s0 + 48:s0 + 64])
                if has_s:
                    P.dma("act", o_sh_s[l], sh_s[:, :, :], ds_shs)
                if last:
                    P.dma("act", o_sh_p[l], sh_c[l][:, :], ds_carry[3])
                ck("w_shift")
                LW = [arena[14], arena[15]]
                AS = [arena[16], arena[17]]
                G = [arena[18], arena[19]]
                KK = [arena[20], arena[21]]
                AT = [arena[8], arena[9]]
                RT = [arena[10], arena[11]]
                BT = [arena[12], arena[13]]
                KT = [rx[0], rx[1]]
                BH = [rx[2], rx[3]]
                KH = [rx[4], rx[5]]
                BON = [rx[6], rx[7]]
                Y = [rx[8], rx[9]]
                tw = scr()
                P.act(tw[0:64, 0:W], C6[0:64, 0:W], AF.Tanh)
                sg = scr()
                P.act(sg[:, 0:W], C7[:, 0:W], AF.Sigmoid)
                for hp in range(2):
                    cc = slice(hp * 128, hp * 128 + 128)
                    ps = P.nps()
                    P.mm(ps[:, 0:W], [(lwa[l][0:64, cc], tw[0:64, 0:W])])
                    t = scr()
                    P.act(t[:, 0:W], ps[:, 0:W], AF.Sigmoid, bias=vec(l, "w0", hp))
                    P.ts("dve", LW[hp][:, 0:W], t[:, 0:W], -math.exp(-0.5), None, ALU.mult)
                    ps = P.nps()
                    P.mm(ps[:, 0:W], [(lwa[l][64:128, cc], C6[64:128, 0:W])])
                    P.act(AS[hp][:, 0:W], ps[:, 0:W], AF.Sigmoid, bias=vec(l, "a0", hp))
                    ps = P.nps()
                    P.mm(ps[:, 0:W], [(lgl[l][:, cc], sg[:, 0:W])])
                    P.cp("act", G[hp][:, 0:W], ps[:, 0:W])
                ck("w_lora")
                for hp in range(2):
                    t = scr()
                    P.ts("dve", t[:, 0:W], K[hp][:, 0:W], vec(l, "kk", hp), None, ALU.mult)
                    sq = scr()
                    P.act(sq[:, 0:W], t[:, 0:W], AF.Square)
                    ps = P.nps()
                    P.mm(ps[:, 0:W], [(bones, sq[:, 0:W])])
                    sd = scr()
                    P.act(sd[:, 0:W], ps[:, 0:W], AF.Sqrt)
                    P.ts("dve", sd[:, 0:W], sd[:, 0:W], 1e-12, None, ALU.max)
                    P.recip(sd[:, 0:W], sd[:, 0:W])
                    P.tt("dve", KK[hp][:, 0:W], t[:, 0:W], sd[:, 0:W], ALU.mult)
                    t2 = scr()
                    P.ts("dve", t2[:, 0:W], AS[hp][:, 0:W], vec(l, "ka", hp), omka[l][:, hp:hp + 1], ALU.mult, ALU.add)
                    P.tt("pool", K[hp][:, 0:W], K[hp][:, 0:W], t2[:, 0:W], ALU.mult)
                    t3 = scr()
                    P.stt(t3[:, 0:W], R[hp][:, 0:W], vec(l, "rk", hp), K[hp][:, 0:W], ALU.mult, ALU.mult)
                    ps = P.nps()
                    P.mm(ps[:, 0:W], [(bones, t3[:, 0:W])])
                    P.tt("dve", BON[hp][:, 0:W], ps[:, 0:W], Vv[hp][:, 0:W], ALU.mult)
                    P.tt("pool", AS[hp][:, 0:W], AS[hp][:, 0:W], KK[hp][:, 0:W], ALU.mult)
                    T1, T2 = scr(), scr()
                    seq = [LW[hp], T1, T2, T1, T2, T1, T2]
                    for i, s_ in enumerate((1, 2, 4, 8, 16, 32)):
                        a_, b_ = seq[i], seq[i + 1]
                        av = V(a_, a_.t[:, 0:npc].rearrange("p (c i) -> p c i", i=WC))
                        bv = V(b_, b_.t[:, 0:npc].rearrange("p (c i) -> p c i", i=WC))
                        P.tt("pool", V(b_, bv.ap[:, :, s_:WC]), V(a_, av.ap[:, :, s_:WC]), V(a_, av.ap[:, :, 0:WC - s_]), ALU.add)
                        P.cp("pool", V(b_, bv.ap[:, :, 0:s_]), V(a_, av.ap[:, :, 0:s_]))
                    CUM = T2
                    if has_s:
                        l3 = s3(LW[hp][:, npc:W])
                        a3 = s3(T1[:, npc:W])
                        c3 = s3(CUM[:, npc:W])
                        P.tt("pool", V(T1, a3.ap[:, 1:4, :]), V(LW[hp], l3.ap[:, 1:4, :]), V(LW[hp], l3.ap[:, 0:3, :]), ALU.add)
                        P.cp("pool", V(T1, a3.ap[:, 0:1, :]), V(LW[hp], l3.ap[:, 0:1, :]))
                        P.tt("pool", V(CUM, c3.ap[:, 2:4, :]), V(T1, a3.ap[:, 2:4, :]), V(T1, a3.ap[:, 0:2, :]), ALU.add)
                        P.cp("pool", V(CUM, c3.ap[:, 0:2, :]), V(T1, a3.ap[:, 0:2, :]))
                    EC = scr()
                    P.act(EC[:, 0:W], CUM[:, 0:W], AF.Exp)
                    P.tt("dve", RT[hp][:, 0:W], R[hp][:, 0:W], EC[:, 0:W], ALU.mult)
                    EN = scr()
                    P.act(EN[:, 0:W], CUM[:, 0:W], AF.Exp, scale=-1.0)
                    P.tt("dve", BT[hp][:, 0:W], AS[hp][:, 0:W], EN[:, 0:W], ALU.mult)
                    P.tt("pool", KT[hp][:, 0:W], K[hp][:, 0:W], EN[:, 0:W], ALU.mult)
                    cex = scr()
                    P.tt("dve", cex[:, 0:W], CUM[:, 0:W], LW[hp][:, 0:W], ALU.subtract)
                    P.act(cex[:, 0:W], cex[:, 0:W], AF.Exp)
                    P.stt(AT[hp][:, 0:W], KK[hp][:, 0:W], -1.0, cex[:, 0:W], ALU.mult, ALU.mult)
                    dl = scr()
                    cv_ = V(CUM, CUM.t[:, 0:npc].rearrange("p (c i) -> p c i", i=WC))
                    P.tt("dve", V(dl, dl.t[:, 0:npc].rearrange("p (c i) -> p c i", i=WC)),
                         V(CUM, cv_.ap[:, :, WC - 1:WC].broadcast_to([128, nwc, WC])), cv_, ALU.subtract)
                    if has_s:
                        c3 = s3(CUM[:, npc:W])
                        P.tt("dve", s3(dl[:, npc:W]), V(CUM, c3.ap[:, 3:4, :].broadcast_to([128, 4, NB])), c3, ALU.subtract)
                    P.act(dl[:, 0:W], dl[:, 0:W], AF.Exp)
                    P.tt("dve", BH[hp][:, 0:W], AS[hp][:, 0:W], dl[:, 0:W], ALU.mult)
                    P.tt("pool", KH[hp][:, 0:W], K[hp][:, 0:W], dl[:, 0:W], ALU.mult)
                    P.cp("pool", LW[hp][:, 0:W], EC[:, 0:W])
                ck("w_prep")
                ECs = LW

                def rwkv_group(chunks):
                    units = []
                    for ci_, ch in enumerate(chunks):
                        C = ch["C"]
                        cs = ch["cols"]
                        pa_ = P.nps()
                        pb_ = P.nps()
                        for hp in range(2):
                            P.trx(pa_[0:C, hp * 128:(hp + 1) * 128], Vv[hp][:, cs], idv, 128, C)
                            P.trx(pa_[0:C, 256 + hp * 128:256 + (hp + 1) * 128], BH[hp][:, cs], idv, 128, C)
                            P.trx(pb_[0:C, hp * 128:(hp + 1) * 128], KH[hp][:, cs], idv, 128, C)
                        tk = tokm[ch["tok"]]
                        P.cp("act", tk[0:C, 0:512], pa_[0:C, 0:512])
                        P.cp("dve" if C >= 128 else "act", tk[0:C, 512:768], pb_[0:C, 0:256])
                        ch["tk"] = tk
                        for h in range(4):
                            units.append((ch, h, ci_ * 4 + h))
                    ck("w_tok")
                    for (ch, h, u) in units:
                        C, cs = ch["C"], ch["cols"]
                        hp, h2 = h // 2, h % 2
                        p0, p1 = h2 * 64, h2 * 64 + 64
                        at_, rt_, bt_, kt_ = AT[hp][p0:p1, cs], RT[hp][p0:p1, cs], BT[hp][p0:p1, cs], KT[hp][p0:p1, cs]
                        pa_ = P.nps()
                        P.mm(pa_[0:C, 0:C], [(bt_, at_)])
                        P.mm(pa_[0:C, C:2 * C], [(bt_, rt_)])
                        P.mm(pa_[0:C, 2 * C:3 * C], [(kt_, at_)])
                        P.mm(pa_[0:C, 3 * C:4 * C], [(kt_, rt_)])
                        P.mm(pa_[0:C, 4 * C:5 * C], [(at_, bt_)])
                        mk = tab("msk64" if C == 64 else "msk4", 0, C, 0, 5 * C)
                        if C >= 128:
                            P.tt("dve", Am[u][0:C, 0:5 * C], pa_[0:C, 0:5 * C], mk, ALU.mult)
                        else:
                            P.cp("act", Am[u][0:C, 0:5 * C], pa_[0:C, 0:5 * C])
                            P.tt("dve", Am[u][0:C, 0:5 * C], Am[u][0:C, 0:5 * C], mk, ALU.mult)
                        P.tt("pool", Tm[u][0:C, 0:C], Am[u][0:C, 0:C], tab("ident", 0, C, 0, C), ALU.add)
                    ck("w_amat")
                    nlev = 5 if chunks[0]["C"] == 64 else 1
                    cur = {u: (Am[u][0:ch["C"], 0:ch["C"]], Am[u][0:ch["C"], 4 * ch["C"]:5 * ch["C"]]) for (ch, h, u) in units}
                    for lev in range(nlev):
                        pend = {}
                        for (ch, h, u) in units:
                            C = ch["C"]
                            cp_, cpt_ = cur[u]
                            pp_ = P.nps()
                            P.mm(pp_[0:C, C:2 * C], [(cp_, cpt_)])
                            if lev < nlev - 1:
                                P.mm(pp_[0:C, 0:C], [(cpt_, cp_)])
                            buf = Pq[u]
                            eve = P.ev() if C >= 128 else "act"
                            if lev < nlev - 1:
                                P.cp(eve, buf[0:C, 0:2 * C], pp_[0:C, 0:2 * C])
                            else:
                                P.cp(eve, buf[0:C, C:2 * C], pp_[0:C, C:2 * C])
                            cur[u] = (buf[0:C, 0:C], buf[0:C, C:2 * C])
                        for (ch, h, u) in units:
                            C = ch["C"]
                            pt_ = P.nps()
                            P.mm(pt_[0:C, 0:C], [(cur[u][1], Tm[u][0:C, 0:C])])
                            if C >= 128:
                                P.tt("dve", Tm[u][0:C, 0:C], Tm[u][0:C, 0:C], pt_[0:C, 0:C], ALU.add)
                            else:
                                P.cp("act", Am[u][0:C, 4 * C:5 * C], pt_[0:C, 0:C])
                                P.tt("dve", Tm[u][0:C, 0:C], Tm[u][0:C, 0:C], Am[u][0:C, 4 * C:5 * C], ALU.add)
                    ck("w_lev")
                    for ci_, ch in enumerate(chunks):
                        C, cs, tk = ch["C"], ch["cols"], ch["tk"]
                        xt, ut, yt = XTs[0], UTs[0], YTs[0]
                        hd = []
                        for h in range(4):
                            hp, h2 = h // 2, h % 2
                            p0, p1 = h2 * 64, h2 * 64 + 64
                            hd.append((h, hp, p0, p1, slice(h * 64, h * 64 + 64), ci_ * 4 + h))
                        px = P.nps()
                        for (h, hp, p0, p1, hc, u) in hd:
                            P.mm(px[0:C, hc], [(AT[hp][:, cs], ch["stm"](h % 2, hp)),
                                               (Am[u][0:C, 2 * C:3 * C], tk[0:C, hc])])
                        P.cp("act", xt[0:C, :], px[0:C, 0:256])
                        ck("w_xt")
                        pu = P.nps()
                        for (h, hp, p0, p1, hc, u) in hd:
                            P.mm(pu[0:C, hc], [(Tm[u][0:C, 0:C], xt[0:C, hc])])
                        P.cp("dve" if C >= 128 else "act", ut[0:C, :], pu[0:C, 0:256])
                        ck("w_ut")
                        py = P.nps()
                        for (h, hp, p0, p1, hc, u) in hd:
                            P.mm(py[0:C, hc], [(RT[hp][:, cs], ch["stm"](h % 2, hp)),
                                               (Am[u][0:C, C:2 * C], ut[0:C, hc]),
                                               (Am[u][0:C, 3 * C:4 * C], tk[0:C, hc])])
                        P.cp("act", yt[0:C, :], py[0:C, 0:256])
                        ck("w_yt")
                        pS = P.nps()
                        for (h, hp, p0, p1, hc, u) in hd:
                            P.mm(pS[p0:p1, hp * 64:(hp + 1) * 64], [(tk[0:C, 256 + h * 64:256 + (h + 1) * 64], ut[0:C, hc]),
                                                                    (tk[0:C, 512 + h * 64:512 + (h + 1) * 64], tk[0:C, hc])])
                        for hp in range(2):
                            stv = ch["st"](0, 128, hp)
                            P.stt(stv, stv, ch["pc"](hp), pS[:, hp * 64:(hp + 1) * 64], ALU.mult, ALU.add)
                        ch["refresh"]()
                        ck("w_st")
                        pyb = P.nps()
                        for hp in range(2):
                            P.trx(pyb[:, hp * C:(hp + 1) * C], yt[0:C, hp * 128:(hp + 1) * 128], idv, C, 128)
                        for hp in range(2):
                            P.cp(P.ev(), ch["ydst"](hp), pyb[:, hp * C:(hp + 1) * C])

                def refresh_p():
                    P.cp("pool", wkvM[l][0:64, :, 0, :], wkvS[l][0:64, :, :])
                    P.cp("pool", wkvM[l][64:128, :, 1, :], wkvS[l][64:128, :, :])

                for g0 in range(0, nwc, 2):
                    chs = []
                    for k_ in range(g0, min(g0 + 2, nwc)):
                        cs = slice(k_ * WC, (k_ + 1) * WC)
                        chs.append(dict(
                            cols=cs, C=WC, tok=(k_ % 2),
                            st=(lambda p0, p1, hp: wkvS[l][p0:p1, hp, :]),
                            stm=(lambda h2, hp: wkvM[l][:, hp, h2, :]),
                            refresh=refresh_p,
                            pc=(lambda hp, k_=k_: ECs[hp][:, (k_ + 1) * WC - 1:(k_ + 1) * WC]),
                            ydst=(lambda hp, cs=cs: Y[hp][:, cs])))
                    rwkv_group(chs)
                ck("w_prompt")
                if last:
                    P.dma("act", o_wkv_p[l], wkvS[l][:, :, :], ds_carry[4])
                if has_s:
                    P.dma("sp", sstate[:, :, :, :], st_wkv[l], ds_st)
                    for g0 in range(0, NB, 2):
                        chs = []
                        for b in range(g0, g0 + 2):
                            cb = slice(npc + b, npc + NSC, NB)
                            P.cp("pool", smask[b % 2][0:64, :, 0, :], sstate[0:64, b, :, :])
                            P.cp("pool", smask[b % 2][64:128, :, 1, :], sstate[64:128, b, :, :])
                            chs.append(dict(
                                cols=cb, C=4, tok=(b % 2),
                                st=(lambda p0, p1, hp, b=b: sstate[p0:p1, b, hp, :]),
                                stm=(lambda h2, hp, b=b: smask[b % 2][:, hp, h2, :]),
                                refresh=(lambda: None),
                                pc=(lambda hp, b=b: ECs[hp][:, npc + 48 + b:npc + 48 + b + 1]),
                                ydst=(lambda hp, cb=cb: Y[hp][:, cb])))
                        rwkv_group(chs)
                    P.dma("act", o_wkv_s[l], sstate[:, :, :, :], ds_st)
                ck("w_sample")
                for hp in range(2):
                    ps = P.nps()
                    P.mm(ps[:, 0:W], [(bones, Y[hp][:, 0:W])])
                    yc = scr()
                    P.stt(yc[:, 0:W], ps[:, 0:W], -1.0 / 64, Y[hp][:, 0:W], ALU.mult, ALU.add)
                    sq = scr()
                    P.act(sq[:, 0:W], yc[:, 0:W], AF.Square)
                    ps = P.nps()
                    P.mm(ps[:, 0:W], [(bones, sq[:, 0:W])])
                    sd = scr()
                    P.act(sd[:, 0:W], ps[:, 0:W], AF.Sqrt, bias=epsln, scale=1.0 / 64)
                    P.recip(sd[:, 0:W], sd[:, 0:W])
                    P.tt("dve", yc[:, 0:W], yc[:, 0:W], sd[:, 0:W], ALU.mult)
                    P.ts("dve", yc[:, 0:W], yc[:, 0:W], vec(l, "lng", hp), vec(l, "lnb", hp), ALU.mult, ALU.add)
                    P.tt("pool", yc[:, 0:W], yc[:, 0:W], BON[hp][:, 0:W], ALU.add)
                    P.tt("pool", yb[6 + hp][:, 0:W], yc[:, 0:W], G[hp][:, 0:W], ALU.mult)

                if stop_after == 'rwkv':
                    raise _Stop()
                merged = arena[0:8]
                if BF16_DENSE:
                    mbf = [V(arena[8 + dc], arena[8 + dc].t[:, :].bitcast(BF16)[:, 0:W]) for dc in range(8)]
                else:
                    mbf = [merged[dc][:, 0:W] for dc in range(8)]
                for n in range(4):
                    for q in range(4):
                        wg, wb = wget()
                        for c4 in range(2):
                            dc = q * 2 + c4
                            ps = zchunk(wg, c4)
                            sgt = scr()
                            P.act(sgt[:, 0:W], ps[:, 0:W], AF.Sigmoid)
                            pp_ = P.nps()
                            wc2 = wcast(wb, 2, c4 * 128, (c4 + 1) * 128)
                            P.mm(pp_[:, 0:W], [(V(wc2.tt, wc2.ap[:, kk_, :]), yb[2 * n + kk_][:, 0:W]) for kk_ in range(2)])
                            if n == 0:
                                P.tt("dve", merged[dc][:, 0:W], sgt[:, 0:W], pp_[:, 0:W], ALU.mult)
                            else:
                                t = scr()
                                P.tt("dve", t[:, 0:W], sgt[:, 0:W], pp_[:, 0:W], ALU.mult)
                                if BF16_DENSE and n == 3:
                                    P.tt("pool", mbf[dc], merged[dc][:, 0:W], t[:, 0:W], ALU.add)
                                else:
                                    P.tt("pool", merged[dc][:, 0:W], merged[dc][:, 0:W], t[:, 0:W], ALU.add)
                for half in range(2):
                    wv = wget()
                    for c4 in range(4):
                        dc = half * 4 + c4
                        ps = P.nps()
                        wc = wcast(wv, 8, c4 * 128, (c4 + 1) * 128)
                        P.mm(ps[:, 0:W], [(V(wc.tt, wc.ap[:, k, :]), mbf[k]) for k in range(8)])
                        residual(l, ps, dc, 2, npc, has_s)

                if stop_after == 'mix':
                    continue
                rs = rms_rstd(xs, W)
                modulate(l, rs, A2[l], 3, npc, has_s, W)
                if has_s:
                    P.dma("sp", ffn_s[:, :, :], st_ffn[l], ds_ffs)
                actt = arena
                if BF16_DENSE:
                    abf = []
                    for i_ in range(22):
                        if i_ < 20:
                            tl = rx[i_ // 2]
                            abf.append(V(tl, tl.t[:, :].bitcast(BF16)[:, (i_ % 2) * W0:(i_ % 2) * W0 + W]))
                        else:
                            abf.append(yb[i_ - 20][:, 0:W])
                else:
                    abf = [actt[i_][:, 0:W] for i_ in range(22)]
                for j in range(11):
                    wv = wget()
                    for c4 in range(4):
                        ci = 4 * j + c4
                        ps = zchunk(wv, c4)
                        fb = fbuf[ci % 4]
                        P.cp("pool", fb[:, 0:2], ffn_c[l][:, ci, :])
                        if has_s:
                            P.cp("pool", fb[:, 2 + npc:2 + npc + 32], ffn_s[:, ci, :])
                        evac_past(ps, fb, 2, 2, npc, has_s, eng="dve")
                        cv = scr()
                        for (c0, n, st, f0) in parts(2, 2, npc, has_s):
                            P.ts("dve", cv[:, f0:f0 + n], fb[:, c0:c0 + n], vec(l, "ffw", 2 * NFC + ci), None, ALU.mult)
                            P.stt(cv[:, f0:f0 + n], fb[:, c0 - st:c0 - st + n], vec(l, "ffw", NFC + ci), cv[:, f0:f0 + n],
                                  ALU.mult, ALU.add)
                            P.stt(cv[:, f0:f0 + n], fb[:, c0 - 2 * st:c0 - 2 * st + n], vec(l, "ffw", ci), cv[:, f0:f0 + n],
                                  ALU.mult, ALU.add)
                        if ci < 22:
                            P.act(actt[ci][:, 0:W], cv[:, 0:W], AF.Silu)
                        else:
                            P.tt("pool", abf[ci - 22], actt[ci - 22][:, 0:W], cv[:, 0:W], ALU.mult)
                        P.cp("pool", ffn_c[l][:, ci, :], fb[:, npc:npc + 2])
                        if has_s:
                            s0 = 2 + npc + 32
                            P.cp("pool", ffn_s[:, ci, :], fb[:, s0 + 32:s0 + 64])
                if has_s:
                    P.dma("act", o_ffn_s[l], ffn_s[:, :, :], ds_ffs)
                if last:
                    P.dma("act", o_ffn_p[l], ffn_c[l][:, :, :], ds_carry[5])
                for j in range(8):
                    wva, wvb = wget()
                    wca = wcast(wva, 11, 0, 128)
                    wcb = wcast(wvb, 11, 0, 128)
                    ps = P.nps()
                    P.mm(ps[:, 0:W], [(V(wca.tt, wca.ap[:, ci, :]), abf[ci]) for ci in range(11)] +
                                     [(V(wcb.tt, wcb.ap[:, ci, :]), abf[11 + ci]) for ci in range(11)])
                    residual(l, ps, j, 5, npc, has_s)

            rs = rms_rstd(xs, W)
            for dc in range(8):
                P.tt("dve", xs[:, dc, 0:W], xs[:, dc, 0:W], rs[:, 0:W], ALU.mult)
                P.ts("pool", xs[:, dc, 0:W], xs[:, dc, 0:W], vec(0, "fg", dc), None, ALU.mult)
            yv = yT.rearrange("(k p) n -> p k n", p=128)
            P.dma("act", yv[:, :, col0:col0 + npc], xs[:, :, 0:npc], ds_y)
            if has_s:
                P.dma("act", yv[:, :, NPT:NPT + NSC], xs[:, :, npc:W], ds_y)
            col0 += npc
    except _Stop:
        pass

    P.finish(P.alld)
    P.emit()
    return nc


_CACHE = {}


def _fm(v):
    v = np.asarray(v, np.float32).reshape(-1, 128)
    return np.ascontiguousarray(v.T)


def prepare(**inp):
    f = lambda k: np.asarray(inp[k], np.float32)
    x_prompt, x_sample = f("x_prompt"), f("x_sample")
    c_prompt, c_sample = f("c_prompt"), f("c_sample")
    ncores = 8
    vec_l = []
    for l in range(L):
        cols = [_fm(f("norm1_g")[l]), _fm(f("norm2_g")[l]), _fm(f("pool_scale")[l])]
        cols += [_fm(f("sc_w")[l][k]) for k in range(3)]
        cols += [_fm(f("rw_mu")[l]), _fm(f("rw_w0")[l]), _fm(f("rw_a0")[l]), _fm(f("rw_k_k")[l]), _fm(f("rw_k_a")[l]),
                 _fm(f("rw_r_k")[l].reshape(-1)), _fm(f("rw_ln_g")[l]), _fm(f("rw_ln_b")[l])]
        cols += [_fm(f("ffn_w")[l][k]) for k in range(3)]
        cols += [_fm(f("final_g")), _fm(f("b_ada")[l])]
        vec_l.append(np.concatenate(cols, axis=1))
    vecs = np.ascontiguousarray(np.stack(vec_l))
    assert vecs.shape == (L, 128, NV), vecs.shape
    pool_bd = np.zeros((L, 128, 2, 128), np.float32)
    pw = f("pool_w")
    for l in range(L):
        for g in range(4):
            c, half = g // 2, g % 2
            pool_bd[l, half * 64:(half + 1) * 64, c, half * 64:(half + 1) * 64] = pw[l, g]
    lora_wa = np.ascontiguousarray(np.concatenate([f("rw_w_lora"), f("rw_a_lora")], axis=1))
    shared = {
        "w_ada": f("w_ada"), "w_in": f("w_in"), "w_br": np.ascontiguousarray(f("w_br").reshape(L, D, D)),
        "w_out": f("w_out"), "w_up": f("w_up"), "w_down": f("w_down"), "vecs": vecs, "pool_bd": pool_bd,
        "lora_wa": lora_wa, "lora_g": f("rw_g_lora"), "tabs": make_tables(), "rot": make_rot(),
    }
    in_maps = []
    for i in range(ncores):
        bs = slice(NB * i, NB * (i + 1))
        xs_ = x_sample[bs]
        xT = np.concatenate([x_prompt[i].T, xs_.transpose(2, 1, 0).reshape(D, NSC)], axis=1)
        c17 = np.concatenate([c_prompt[i:i + 1], c_sample[bs]], axis=0)
        cT = c17.T.reshape(8, 128, 17).transpose(1, 0, 2)
        sp = f("state_pool")[:, bs]
        st_pool = sp.transpose(0, 3, 2, 1).reshape(L, 2, 128, 240).transpose(0, 2, 1, 3)
        sr = f("state_ret")[:, bs]
        st_ret = sr.reshape(L, NB, 2, 2, 64, 64).transpose(0, 3, 4, 1, 2, 5).reshape(L, 128, NB, 2, 64)
        ssc = f("state_sconv")[:, bs]
        st_sc = ssc.transpose(0, 3, 2, 1).reshape(L, 2, 128, 32).transpose(0, 2, 1, 3)
        ssh = f("state_shift")[:, bs]
        st_sh = ssh.transpose(0, 2, 1).reshape(L, 8, 128, NB).transpose(0, 2, 1, 3)
        sw = f("state_wkv")[:, bs]
        st_wkv = sw.reshape(L, NB, 2, 2, 64, 64).transpose(0, 3, 5, 1, 2, 4).reshape(L, 128, NB, 2, 64)
        sf = f("state_ffn")[:, bs]
        st_ffn = sf.transpose(0, 3, 2, 1).reshape(L, NFC, 128, 32).transpose(0, 2, 1, 3)
        m = dict(shared)
        m.update({"xT": xT, "cT": cT, "st_pool": st_pool, "st_ret": st_ret, "st_sc": st_sc, "st_sh": st_sh,
                  "st_wkv": st_wkv, "st_ffn": st_ffn})
        in_maps.append({k: np.ascontiguousarray(v, dtype=np.float32) for k, v in m.items()})
    return in_maps


def kernel(**inp):
    in_maps = prepare(**inp)
    if "nc" not in _CACHE:
        _CACHE["nc"] = build_program()
    res = run_bass_kernel_spmd(_CACHE["nc"], in_maps, core_ids=list(range(8)))
    return assemble(res.results)


def assemble(R, cores=None, npt=NPT):
    B = 8
    cores = list(range(B)) if cores is None else cores
    y_p = np.zeros((B, NPT, D), np.float32)
    y_s = np.zeros((B * NB, 4, D), np.float32)
    p_pool = np.zeros((L, B, 15, 256), np.float32)
    p_ret = np.zeros((L, B, 4, 64, 64), np.float32)
    p_sc = np.zeros((L, B, 2, 256), np.float32)
    p_sh = np.zeros((L, B, 1024), np.float32)
    p_wkv = np.zeros((L, B, 4, 64, 64), np.float32)
    p_ffn = np.zeros((L, B, 2, 2 * DFF), np.float32)
    s_pool = np.zeros((L, B * NB, 15, 256), np.float32)
    s_ret = np.zeros((L, B * NB, 4, 64, 64), np.float32)
    s_sc = np.zeros((L, B * NB, 2, 256), np.float32)
    s_sh = np.zeros((L, B * NB, 1024), np.float32)
    s_wkv = np.zeros((L, B * NB, 4, 64, 64), np.float32)
    s_ffn = np.zeros((L, B * NB, 2, 2 * DFF), np.float32)
    for i in cores:
        r = R[i]
        bs = slice(NB * i, NB * (i + 1))
        yT = np.asarray(r["yT"])
        y_p[i, :npt] = yT[:, :npt].T
        y_s[bs] = yT[:, NPT:].reshape(D, 4, NB).transpose(2, 1, 0)
        p_pool[:, i] = np.asarray(r["o_pool_p"]).transpose(0, 3, 2, 1).reshape(L, 15, 256)
        s_pool[:, bs] = np.asarray(r["o_pool_s"]).reshape(L, 128, 2, 15, NB).transpose(0, 4, 3, 2, 1).reshape(L, NB, 15, 256)
        p_ret[:, i] = np.asarray(r["o_ret_p"]).reshape(L, 2, 64, 2, 64).transpose(0, 3, 1, 2, 4).reshape(L, 4, 64, 64)
        s_ret[:, bs] = np.asarray(r["o_ret_s"]).reshape(L, 2, 64, NB, 2, 64).transpose(0, 3, 4, 1, 2, 5).reshape(L, NB, 4, 64, 64)
        p_sc[:, i] = np.asarray(r["o_sc_p"]).transpose(0, 3, 2, 1).reshape(L, 2, 256)
        s_sc[:, bs] = np.asarray(r["o_sc_s"]).reshape(L, 128, 2, 2, NB).transpose(0, 4, 3, 2, 1).reshape(L, NB, 2, 256)
        p_sh[:, i] = np.asarray(r["o_sh_p"]).transpose(0, 2, 1).reshape(L, 1024)
        s_sh[:, bs] = np.asarray(r["o_sh_s"]).transpose(0, 3, 2, 1).reshape(L, NB, 1024)
        p_wkv[:, i] = np.asarray(r["o_wkv_p"]).reshape(L, 2, 64, 2, 64).transpose(0, 3, 1, 4, 2).reshape(L, 4, 64, 64)
        s_wkv[:, bs] = np.asarray(r["o_wkv_s"]).reshape(L, 2, 64, NB, 2, 64).transpose(0, 3, 4, 1, 5, 2).reshape(L, NB, 4, 64, 64)
        p_ffn[:, i] = np.asarray(r["o_ffn_p"]).transpose(0, 3, 2, 1).reshape(L, 2, 2 * DFF)
        s_ffn[:, bs] = np.asarray(r["o_ffn_s"]).reshape(L, 128, NFC, 2, NB).transpose(0, 4, 3, 2, 1).reshape(L, NB, 2, 2 * DFF)
    return (y_p, y_s, p_pool, p_ret, p_sc, p_sh, p_wkv, p_ffn, s_pool, s_ret, s_sc, s_sh, s_wkv, s_ffn)
```

```python
import contextlib
import math
import numpy as np
import concourse.bass as bass
import concourse.mybir as mybir
from concourse.bass_utils import run_bass_kernel_spmd

F32 = mybir.dt.float32
BF16 = mybir.dt.bfloat16
BF16_DENSE = True
AF = mybir.ActivationFunctionType
ALU = mybir.AluOpType

D = 1024
L = 2
NPT = 2048
NB = 16
NSC = 64
NCOLS = NPT + NSC
SEGS = [256, 384, 384, 384, 384, 256]
WMAX = 384
DFF = 2816
NFC = 44
IN_COLS = 7168
RC = 128
WC = 64
LG = [math.log1p(-(2.0 ** (-5.0 - h))) for h in range(4)]

VOFF = {}
_o = 0
for _n, _w in [("n1g", 8), ("n2g", 8), ("pscale", 2), ("scw", 6), ("mu", 8), ("w0", 2), ("a0", 2), ("kk", 2),
               ("ka", 2), ("rk", 2), ("lng", 2), ("lnb", 2), ("ffw", 132), ("fg", 8), ("bada", 48)]:
    VOFF[_n] = _o
    _o += _w
NV = _o

TOFF = {}
_o = 0
for _n, _w in [("ident", 128), ("ones", 128), ("bones", 128), ("perm", 128), ("dmask", 512), ("qdec", 256),
               ("kdec", 256), ("kdec4", 256), ("cdec", 2), ("cdec4", 2), ("msk64", 320), ("msk4", 20),
               ("invw", 2), ("icnt", 32)]:
    TOFF[_n] = _o
    _o += _w
NTAB = _o


def make_tables():
    t = np.zeros((128, NTAB), np.float64)
    p = np.arange(128)
    t[:, TOFF["ident"]:TOFF["ident"] + 128] = np.eye(128)
    t[:, TOFF["ones"]:TOFF["ones"] + 128] = 1.0
    t[:, TOFF["bones"]:TOFF["bones"] + 128] = (p[:, None] // 64 == p[None, :] // 64)
    partner = np.where(p % 64 < 32, p + 32, p - 32)
    pm = np.zeros((128, 128))
    pm[partner, p] = 1.0
    t[:, TOFF["perm"]:TOFF["perm"] + 128] = pm
    i = np.arange(128)
    for h in range(4):
        g = math.exp(LG[h])
        dm = np.where(i[None, :] >= i[:, None], g ** np.maximum(i[None, :] - i[:, None], 0), 0.0)
        t[:, TOFF["dmask"] + h * 128:TOFF["dmask"] + (h + 1) * 128] = dm
        t[:, TOFF["kdec"] + h * 64:TOFF["kdec"] + (h + 1) * 64] = (g ** (127 - i))[:, None]
        t[0:4, TOFF["kdec4"] + h * 64:TOFF["kdec4"] + (h + 1) * 64] = (g ** (3 - np.arange(4)))[:, None]
    for hp in range(2):
        for h2 in range(2):
            g = math.exp(LG[2 * hp + h2])
            t[h2 * 64:(h2 + 1) * 64, TOFF["qdec"] + hp * 128:TOFF["qdec"] + (hp + 1) * 128] = (g ** (i + 1))[None, :]
            t[h2 * 64:(h2 + 1) * 64, TOFF["cdec"] + hp] = g ** 128
            t[h2 * 64:(h2 + 1) * 64, TOFF["cdec4"] + hp] = g ** 4
    for C, nm in [(64, "msk64"), (4, "msk4")]:
        j = np.arange(C)
        su = (j[:, None] < j[None, :]).astype(np.float64)
        iu = (j[:, None] <= j[None, :]).astype(np.float64)
        sl = (j[:, None] > j[None, :]).astype(np.float64)
        t[0:C, TOFF[nm]:TOFF[nm] + 5 * C] = np.concatenate([su, iu, su, iu, sl], axis=1)
    wins = (2, 4, 8, 16)
    for c in range(2):
        for half in range(2):
            w = wins[2 * c + half]
            t[half * 64:(half + 1) * 64, TOFF["invw"] + c] = 1.0 / w
            t[half * 64:(half + 1) * 64, TOFF["icnt"] + c * 16:TOFF["icnt"] + (c + 1) * 16] = \
                (1.0 / np.minimum(w, np.arange(16) + 1))[None, :]
    return t.astype(np.float32)


def make_rot():
    pos = np.concatenate([np.arange(NPT), np.repeat(16384 + np.arange(4), NB)]).astype(np.float32)
    p = np.arange(128)
    inv = (10000.0 ** (-(p % 32).astype(np.float32) / np.float32(32))).astype(np.float32)
    ang = (pos[None, :] * inv[:, None]).astype(np.float32)
    cos = np.cos(ang.astype(np.float64))
    sin = np.sin(ang.astype(np.float64))
    sgn = np.where(p % 64 < 32, -1.0, 1.0)[:, None]
    return np.stack([cos, sin * sgn]).astype(np.float32)


class Sem:
    def __init__(self, h, inc):
        self.h = h
        self.inc = inc
        self.count = 0


class V:
    def __init__(self, tt, ap):
        self.tt = tt
        self.ap = ap


class TT:
    def __init__(self, t):
        self.t = t
        self.lw = None
        self.rd = {}

    def __getitem__(self, idx):
        return V(self, self.t[idx])


class Prog:
    ENG = ["pe", "act", "dve", "pool", "sp"]

    def __init__(self, nc):
        self.nc = nc
        self.es = contextlib.ExitStack()
        self.esem = {e: Sem(self.es.enter_context(nc.semaphore("s_" + e)), 1) for e in self.ENG}
        self.ops = {e: [] for e in self.ENG}
        self.seen = {e: {} for e in self.ENG}
        self.psr = []
        self.pi = 0
        self.evi = 0
        self.nm = 0
        self.alld = []

    def sb(self, name, shape, dtype=F32):
        return TT(self.es.enter_context(self.nc.sbuf_tensor("sb_" + name, list(shape), dtype)))

    def mkps(self):
        self.psr = [TT(self.es.enter_context(self.nc.psum_tensor("ps%d" % i, [128, 512], F32))) for i in range(8)]

    def nps(self):
        self.pi = (self.pi + 1) % 6
        return self.psr[self.pi]

    def ev(self):
        return "dve"

    def dsem(self, name):
        d = Sem(self.es.enter_context(self.nc.semaphore("d_" + name)), 16)
        self.alld.append(d)
        return d

    def _deps(self, eng, reads, writes):
        waits = {}
        own = self.esem[eng]

        def need(sv, raw):
            sem, val = sv
            if sem is own and (eng == "pe" or (not raw and eng != "pool")):
                return
            if waits.get(sem, 0) < val:
                waits[sem] = val

        for r in reads:
            if r.lw is not None:
                need(r.lw, True)
        for w in writes:
            if w.lw is not None:
                need(w.lw, False)
            for s, v in w.rd.items():
                need((s, v), False)
        out = []
        seen = self.seen[eng]
        for s, v in waits.items():
            if seen.get(s, 0) < v:
                seen[s] = v
                out.append((s, v))
        return out

    def _commit(self, sem, reads, writes):
        sem.count += sem.inc
        val = sem.count
        for r in reads:
            if r.rd.get(sem, 0) < val:
                r.rd[sem] = val
        for w in writes:
            w.lw = (sem, val)
            w.rd = {}

    def op(self, eng, fn, reads, writes):
        waits = self._deps(eng, reads, writes)
        sem = self.esem[eng]
        self._commit(sem, reads, writes)
        self.ops[eng].append((waits, fn, sem))

    def dma(self, eng, out, in_, dsem):
        rd = [in_.tt] if isinstance(in_, V) else []
        wr = [out.tt] if isinstance(out, V) else []
        oap = out.ap if isinstance(out, V) else out
        iap = in_.ap if isinstance(in_, V) else in_
        waits = self._deps(eng, rd, wr)
        self._commit(dsem, rd, wr)
        self.ops[eng].append((waits, lambda e: e.dma_start(out=oap, in_=iap), dsem))

    def mm(self, out, pairs):
        n = len(pairs)
        aps = [(l.ap, r.ap) for l, r in pairs]
        oap = out.ap

        def fn(e):
            inst = None
            for i, (l, r) in enumerate(aps):
                inst = e.matmul(oap, l, r, start=(i == 0), stop=(i == n - 1))
            return inst

        rds = []
        for l, r in pairs:
            rds += [l.tt, r.tt]
        self.op("pe", fn, rds, [out.tt])

    def mm1(self, out, l, r, start, stop):
        oap, lap, rap = out.ap, l.ap, r.ap
        self.op("pe", lambda e: e.matmul(oap, lap, rap, start=start, stop=stop), [l.tt, r.tt], [out.tt])

    def tr(self, out, in_, ident):
        oap, iap, dap = out.ap, in_.ap, ident.ap
        self.op("pe", lambda e: e.transpose(out=oap, in_=iap, identity=dap), [in_.tt, ident.tt], [out.tt])

    def trx(self, out, in_, identv, K, M):
        if K == 128 and M == 128:
            self.tr(out, in_, identv(128))
        else:
            self.mm(out, [(in_, identv(K))])

    def tt(self, eng, out, in0, in1, op):
        oap, a, b = out.ap, in0.ap, in1.ap
        self.op(eng, lambda e: e.tensor_tensor(out=oap, in0=a, in1=b, op=op), [in0.tt, in1.tt], [out.tt])

    def ts(self, eng, out, in0, s1, s2, op0, op1=None):
        rds = [in0.tt]
        a1 = s1
        a2 = s2
        if isinstance(s1, V):
            rds.append(s1.tt)
            a1 = s1.ap
        if isinstance(s2, V):
            rds.append(s2.tt)
            a2 = s2.ap
        oap, iap = out.ap, in0.ap
        if op1 is None:
            self.op(eng, lambda e: e.tensor_scalar(out=oap, in0=iap, scalar1=a1, scalar2=None, op0=op0), rds, [out.tt])
        else:
            self.op(eng, lambda e: e.tensor_scalar(out=oap, in0=iap, scalar1=a1, scalar2=a2, op0=op0, op1=op1),
                    rds, [out.tt])

    def stt(self, out, in0, sc, in1, op0, op1):
        rds = [in0.tt, in1.tt]
        a = sc
        if isinstance(sc, V):
            rds.append(sc.tt)
            a = sc.ap
        oap, i0, i1 = out.ap, in0.ap, in1.ap
        self.op("dve", lambda e: e.scalar_tensor_tensor(out=oap, in0=i0, scalar=a, in1=i1, op0=op0, op1=op1),
                rds, [out.tt])

    def act(self, out, in_, func, bias=None, scale=None):
        rds = [in_.tt]
        kw = {}
        if bias is not None:
            if isinstance(bias, V):
                rds.append(bias.tt)
                kw["bias"] = bias.ap
            else:
                kw["bias"] = bias
        if scale is not None:
            if isinstance(scale, V):
                rds.append(scale.tt)
                kw["scale"] = scale.ap
            else:
                kw["scale"] = scale
        oap, iap = out.ap, in_.ap
        self.op("act", lambda e: e.activation(out=oap, in_=iap, func=func, **kw), rds, [out.tt])

    def cp(self, eng, out, in_):
        oap, iap = out.ap, in_.ap
        if eng == "act":
            self.op("act", lambda e: e.activation(out=oap, in_=iap, func=AF.Copy), [in_.tt], [out.tt])
        else:
            self.op(eng, lambda e: e.tensor_copy(out=oap, in_=iap), [in_.tt], [out.tt])

    def recip(self, out, in_):
        oap, iap = out.ap, in_.ap
        self.op("dve", lambda e: e.reciprocal(out=oap, in_=iap), [in_.tt], [out.tt])

    def memset(self, eng, out, val):
        oap = out.ap
        self.op(eng, lambda e: e.memset(oap, val), [], [out.tt])

    def finish(self, dsems):
        waits = [(s, s.count) for s in dsems if s.count > 0]
        self.ops["sp"].append((waits, None, None))

    def emit(self):
        nc = self.nc
        ops = self.ops

        def run(e, lst):
            for waits, fn, sem in lst:
                for s, v in waits:
                    e.wait_ge(s.h, v)
                if fn is not None:
                    fn(e).then_inc(sem.h, sem.inc)

        with nc.Block() as block:
            @block.tensor
            def _(e):
                run(e, ops["pe"])

            @block.scalar
            def _(e):
                run(e, ops["act"])

            @block.vector
            def _(e):
                run(e, ops["dve"])

            @block.gpsimd
            def _(e):
                run(e, ops["pool"])

            @block.sync
            def _(e):
                run(e, ops["sp"])
        self.es.close()


class _Stop(Exception):
    pass


def build_program(nlayers=L, segs=SEGS, stop_after=None):
    nc = bass.Bass("TRN2", target_bir_lowering=False, dynamic_dma_scratch_size=1024)
    P = Prog(nc)

    def din(name, shape):
        return nc.dram_tensor(name, list(shape), F32, kind="ExternalInput").ap()

    def dout(name, shape):
        return nc.dram_tensor(name, list(shape), F32, kind="ExternalOutput").ap()

    xT = din("xT", [D, NCOLS])
    cT = din("cT", [128, 8, 17])
    w_ada = din("w_ada", [L, D, 6 * D])
    w_in = din("w_in", [L, D, IN_COLS])
    w_br = din("w_br", [L, D, D])
    w_out = din("w_out", [L, D, D])
    w_up = din("w_up", [L, D, 2 * DFF])
    w_down = din("w_down", [L, DFF, D])
    vecs = din("vecs", [L, 128, NV])
    pool_bd = din("pool_bd", [L, 128, 2, 128])
    lora_wa = din("lora_wa", [L, 128, 256])
    lora_g = din("lora_g", [L, 128, 256])
    st_pool = din("st_pool", [L, 128, 2, 240])
    st_ret = din("st_ret", [L, 128, NB, 2, 64])
    st_sc = din("st_sc", [L, 128, 2, 32])
    st_sh = din("st_sh", [L, 128, 8, 16])
    st_wkv = din("st_wkv", [L, 128, NB, 2, 64])
    st_ffn = din("st_ffn", [L, 128, NFC, 32])
    tabs_d = din("tabs", [128, NTAB])
    rot_d = din("rot", [2, 128, NCOLS])

    yT = dout("yT", [D, NCOLS])
    o_pool_p = dout("o_pool_p", [L, 128, 2, 15])
    o_pool_s = dout("o_pool_s", [L, 128, 2, 240])
    o_ret_p = dout("o_ret_p", [L, 128, 2, 64])
    o_ret_s = dout("o_ret_s", [L, 128, NB, 2, 64])
    o_sc_p = dout("o_sc_p", [L, 128, 2, 2])
    o_sc_s = dout("o_sc_s", [L, 128, 2, 32])
    o_sh_p = dout("o_sh_p", [L, 128, 8])
    o_sh_s = dout("o_sh_s", [L, 128, 8, 16])
    o_wkv_p = dout("o_wkv_p", [L, 128, 2, 64])
    o_wkv_s = dout("o_wkv_s", [L, 128, NB, 2, 64])
    o_ffn_p = dout("o_ffn_p", [L, 128, NFC, 2])
    o_ffn_s = dout("o_ffn_s", [L, 128, NFC, 32])

    ckc = [0]

    def ck(tag=""):
        ckc[0] += 1
        if stop_after is not None and stop_after == "tag:" + tag:
            print("STOP at tag", tag)
            raise _Stop()
        if stop_after is not None and stop_after.startswith("cnt:") and ckc[0] == int(stop_after[4:]):
            print("STOP at checkpoint", ckc[0], tag)
            raise _Stop()

    P.mkps()
    W0 = WMAX
    ZW = 2 + WMAX + 4
    UW = 576
    tabs = P.sb("tabs", [128, NTAB])
    vt_ = [P.sb("vecs%d" % l, [128, NV]) for l in range(L)]
    pbd = [P.sb("pbd%d" % l, [128, 2, 128]) for l in range(L)]
    lwa = [P.sb("lwa%d" % l, [128, 256]) for l in range(L)]
    lgl = [P.sb("lgl%d" % l, [128, 256]) for l in range(L)]
    omka = [P.sb("omka%d" % l, [128, 2]) for l in range(L)]
    csil = P.sb("csil", [128, 8, 17])
    mod = [P.sb("mod%d" % l, [128, 48, 17]) for l in range(L)]
    A1 = [P.sb("A1_%d" % l, [128, 8, 17]) for l in range(L)]
    A2 = [P.sb("A2_%d" % l, [128, 8, 17]) for l in range(L)]
    rot = P.sb("rot", [128, 2, W0])
    xs = P.sb("xs", [128, 8, W0])
    DD = BF16 if BF16_DENSE else F32
    hs = P.sb("hs", [128, 8, W0], DD)
    csil_b = P.sb("csil_b", [128, 8, 17], DD)
    hbuf = [P.sb("hbuf%d" % i, [128, 2048], BF16) for i in range(3)] if BF16_DENSE else []
    wbi = [0]
    cei = [0]

    wcast_impl = [None]

    def wcast(wv, nk, c0, c1):
        return wcast_impl[0](wv, nk, c0, c1)

    NSLOT = 2
    wsl = [P.sb("wsl%d" % i, [128, 4096]) for i in range(NSLOT)]
    wsem = [P.dsem("w%d" % i) for i in range(NSLOT)]
    arena = [P.sb("ar%d" % i, [128, ZW]) for i in range(22)]
    zr = arena[0:8]
    zsc = arena[8:14]
    zw = arena[14:22]
    yb = [P.sb("yb%d" % i, [128, W0], BF16 if BF16_DENSE else F32) for i in range(8)]
    rx = [P.sb("rx%d" % i, [128, W0]) for i in range(10)]
    ubuf = [P.sb("ubuf%d" % i, [128, UW]) for i in range(2)]
    wide = [P.sb("wide%d" % i, [128, UW]) for i in range(2)]
    fbuf = wide + ubuf
    NSCR = 11
    scr_ = [P.sb("scr%d" % i, [128, W0]) for i in range(NSCR)]
    sci = [0]

    def scr():
        sci[0] = (sci[0] + 1) % NSCR
        return scr_[sci[0]]

    sstate = P.sb("sstate", [128, NB, 2, 64])
    wkvM = [P.sb("wkvM%d" % l, [128, 2, 2, 64]) for l in range(L)]
    smask = [P.sb("smask%d" % i, [128, 2, 2, 64]) for i in range(2)]
    retS = [P.sb("retS%d" % l, [128, 2, 64]) for l in range(L)]
    wkvS = [P.sb("wkvS%d" % l, [128, 2, 64]) for l in range(L)]
    pool_c = [P.sb("poolc%d" % l, [128, 2, 15]) for l in range(L)]
    sc_c = [P.sb("scc%d" % l, [128, 2, 2]) for l in range(L)]
    sh_c = [P.sb("shc%d" % l, [128, 8]) for l in range(L)]
    ffn_c = [P.sb("ffnc%d" % l, [128, NFC, 2]) for l in range(L)]
    sh_s = P.sb("sh_s", [128, 8, 16])
    sc_s = P.sb("sc_s", [128, 2, 32])
    ffn_s = P.sb("ffn_s", [128, NFC, 32])
    vtok = [P.sb("vtok%d" % i, [128, 256]) for i in range(2)]
    ktok = [P.sb("ktok%d" % i, [128, 256]) for i in range(2)]
    sTs = [P.sb("sTs%d" % i, [128, 128]) for i in range(2)]
    NU = 8
    tokm = [P.sb("tokm%d" % i, [64, 768]) for i in range(2)]
    Am = [P.sb("Am%d" % i, [64, 320]) for i in range(NU)]
    Tm = [P.sb("Tm%d" % i, [64, 64]) for i in range(NU)]
    Pq = [P.sb("Pq%d" % i, [64, 128]) for i in range(NU)]
    XTs = [P.sb("XTs%d" % i, [64, 256]) for i in range(1)]
    UTs = [P.sb("UTs%d" % i, [64, 256]) for i in range(1)]
    YTs = [P.sb("YTs%d" % i, [64, 256]) for i in range(1)]
    epsn_t = P.sb("epsn", [128, 4])
    ident = tabs[:, TOFF["ident"]:TOFF["ident"] + 128]
    onesm = tabs[:, TOFF["ones"]:TOFF["ones"] + 128]
    bones = tabs[:, TOFF["bones"]:TOFF["bones"] + 128]
    perm = tabs[:, TOFF["perm"]:TOFF["perm"] + 128]

    def idv(K):
        return tabs[0:K, TOFF["ident"]:TOFF["ident"] + K]

    def tab(name, r0, r1, c0, c1):
        return tabs[r0:r1, TOFF[name] + c0:TOFF[name] + c1]

    ds_const = P.dsem("const")
    ds_x = P.dsem("x")
    ds_rot = P.dsem("rot")
    ds_st = P.dsem("st")
    ds_ub = [P.dsem("ub%d" % i) for i in range(2)]
    ds_scs = P.dsem("scs")
    ds_shs = P.dsem("shs")
    ds_ffs = P.dsem("ffs")
    ds_y = P.dsem("y")
    ds_carry = [P.dsem("carry%d" % i) for i in range(6)]
    out_sems = [ds_y, ds_st, ds_scs, ds_shs, ds_ffs] + ds_ub + ds_carry

    P.dma("sp", tabs[:, :], tabs_d, ds_const)
    for l in range(L):
        P.dma("sp", vt_[l][:, :], vecs[l], ds_const)
        P.dma("sp", pbd[l][:, :, :], pool_bd[l], ds_const)
        P.dma("sp", lwa[l][:, :], lora_wa[l], ds_const)
        P.dma("sp", lgl[l][:, :], lora_g[l], ds_const)
    P.dma("sp", csil[:, :, :], cT, ds_const)
    for t in [tabs, csil] + vt_ + pbd + lwa + lgl:
        t.lw = (ds_const, ds_const.count)
    P.act(csil_b[:, :, :], csil[:, :, :], AF.Silu)
    P.memset("pool", epsn_t[:, 0:1], 1e-6)
    P.memset("pool", epsn_t[:, 1:2], 64e-5)
    epsn = epsn_t[:, 0:1]
    epsln = epsn_t[:, 1:2]
    for l in range(L):
        ka = vt_[l][:, VOFF["ka"]:VOFF["ka"] + 2]
        P.ts("dve", omka[l][:, :], ka, -1.0, 1.0, ALU.mult, ALU.add)
        for t in (retS[l], wkvS[l], pool_c[l], sc_c[l], ffn_c[l]):
            P.memset("pool", t[:, :, :], 0.0)
        P.memset("pool", sh_c[l][:, :], 0.0)
        P.memset("pool", wkvM[l][:, :, :, :], 0.0)
    for i in range(2):
        P.memset("pool", smask[i][:, :, :, :], 0.0)

    def vec(l, name, c):
        o = VOFF[name] + c
        return vt_[l][:, o:o + 1]

    wjobs = []
    wkind = []
    wmeta = []
    WIN_JOBS = [(0, 512), (512, 1024), (1024, 1280), (1280, 1792), (1792, 2048), (2048, 2560), (2560, 3072)]

    def gen_jobs():
        for si in range(len(segs)):
            for l in range(nlayers):
                jl = [0]

                def meta(ada=False, si=si, l=l, jl=jl):
                    if ada:
                        wmeta.append((si, l, None))
                    else:
                        wmeta.append((si, l, jl[0]))
                        jl[0] += 1

                if si == 0:
                    wa = w_ada[l].rearrange("(k p) n -> p k n", p=128)
                    for j in range(12):
                        wjobs.append([(wa[:, :, j * 512:(j + 1) * 512], 8, 512, 0)])
                        wkind.append("bf"); meta(ada=True)
                wi = w_in[l].rearrange("(k p) n -> p k n", p=128)
                for (a, b) in WIN_JOBS:
                    wjobs.append([(wi[:, :, a:b], 8, b - a, 0)])
                    wkind.append("bf"); meta()
                wb = w_br[l].rearrange("(k p) n -> p k n", p=128)
                for n in range(4):
                    for q in range(4):
                        g0 = 3072 + n * 1024 + q * 256
                        wjobs.append([(wi[:, :, g0:g0 + 256], 8, 256, 0),
                                      (wb[:, 2 * n:2 * n + 2, q * 256:(q + 1) * 256], 2, 256, 2048)])
                        wkind.append("bf"); meta()
                wo = w_out[l].rearrange("(k p) n -> p k n", p=128)
                for half in range(2):
                    wjobs.append([(wo[:, :, half * 512:(half + 1) * 512], 8, 512, 0)])
                    wkind.append("bf"); meta()
                wu = w_up[l].rearrange("(k p) n -> p k n", p=128)
                for j in range(11):
                    wjobs.append([(wu[:, :, j * 512:(j + 1) * 512], 8, 512, 0)])
                    wkind.append("bf"); meta()
                wd = w_down[l].rearrange("(k p) n -> p k n", p=128)
                for j in range(8):
                    wjobs.append([(wd[:, 0:11, j * 128:(j + 1) * 128], 11, 128, 0),
                                  (wd[:, 11:22, j * 128:(j + 1) * 128], 11, 128, 1408)])
                    wkind.append("bf"); meta()

    gen_jobs()
    assert len(wkind) == len(wjobs) == len(wmeta), (len(wkind), len(wjobs), len(wmeta))
    NJL = 44
    HBM_BF16 = BF16_DENSE and len(segs) > 1
    if HBM_BF16:
        wscr = nc.dram_tensor("wscr", [L, NJL, 128, 4096], BF16, kind="Internal").ap()
        hst = [P.dsem("hst%d" % i) for i in range(3)]
    first_b = [True]
    wslh = [TT(wsl[q // 2].t) for q in range(4)] if HBM_BF16 else []
    hsem = [P.dsem("hs%d" % q) for q in range(4)] if HBM_BF16 else []
    inherited = set()
    last0 = max([i for i in range(len(wjobs)) if wmeta[i][0] == 0] + [-1])

    def half_of(i):
        q = i % 4
        if q not in inherited:
            inherited.add(q)
            par = wsl[q // 2]
            wslh[q].lw = par.lw
            wslh[q].rd = dict(par.rd)
        return q

    def job_from_scratch(i):
        si_, l_, jl_ = wmeta[i]
        return HBM_BF16 and si_ >= 1 and jl_ is not None

    def unit_off(i, u):
        return sum(k_ * nc2 for (_, k_, _, _, _, nc2) in job_units(i)[:u])
    wstate = {"issued": 0, "next": 0}

    def wissue():
        i = wstate["issued"]
        s = i % NSLOT
        if job_from_scratch(i):
            si_, l_, jl_ = wmeta[i]
            tot = unit_off(i, len(job_units(i)))
            if si_ == 1:
                P.ops["sp"].append(([(d, d.count) for d in hst if d.count > 0], None, None))
            q = half_of(i)
            dst = V(wslh[q], wslh[q].t[:, :].bitcast(BF16)[:, (q % 2) * 4096:(q % 2) * 4096 + tot])
            P.dma("sp", dst, wscr[l_, jl_, :, 0:tot], hsem[q])
        else:
            for (ap, k, n, off) in wjobs[i]:
                dst = V(wsl[s], wsl[s].t[:, off:off + k * n].rearrange("p (k n) -> p k n", k=k))
                P.dma("sp", dst, ap, wsem[s])
        wstate["issued"] = i + 1

    cast_done = {}
    hbi = [0]

    def job_castable(i):
        return BF16_DENSE and wkind[i] == "bf"

    def job_units(i):
        us = []
        for pi, (ap, k, n, off) in enumerate(wjobs[i]):
            for c0 in range(0, n, 256):
                us.append((pi, k, n, off, c0, min(256, n - c0)))
        return us

    def ensure_cast(i, u):
        if (i, u) in cast_done:
            return cast_done[(i, u)]
        pi, k, n, off, c0, nc_ = job_units(i)[u]
        sl = wsl[i % NSLOT]
        if job_from_scratch(i):
            q = i % 4
            uo = (q % 2) * 4096 + unit_off(i, u)
            dst = V(wslh[q], wslh[q].t[:, :].bitcast(BF16)[:, uo:uo + k * nc_].rearrange("p (k n) -> p k n", k=k))
            cast_done[(i, u)] = dst
            return dst
        src = V(sl, sl.t[:, off:off + k * n].rearrange("p (k n) -> p k n", k=k)[:, :, c0:c0 + nc_])
        hbi[0] = (hbi[0] + 1) % 3
        cei[0] = (cei[0] + 1) % 2
        hb = hbuf[hbi[0]]
        dst = V(hb, hb.t[:, 0:k * nc_].rearrange("p (k n) -> p k n", k=k))
        P.cp(("dve", "act")[cei[0]], dst, src)
        if HBM_BF16 and wmeta[i][0] == 0 and wmeta[i][2] is not None:
            uo = unit_off(i, u)
            P.dma("act", wscr[wmeta[i][1], wmeta[i][2], :, uo:uo + k * nc_], V(hb, hb.t[:, 0:k * nc_]), hst[hbi[0]])
        cast_done[(i, u)] = dst
        return dst

    def wcast_real(wv, nk, c0, c1):
        if not BF16_DENSE:
            return V(wv.tt, wv.ap[:, 0:nk, c0:c1])
        i, pi = wv.job, wv.part
        units = job_units(i)
        u = [j for j, uu in enumerate(units) if uu[0] == pi and uu[4] <= c0 < uu[4] + uu[5]][0]
        d = ensure_cast(i, u)
        if u == len(units) - 1 and i + 1 < len(wjobs) and job_castable(i + 1) and wstate["issued"] > i + 1:
            ensure_cast(i + 1, 0)
        cw = c0 - units[u][4]
        return V(d.tt, d.ap[:, 0:nk, cw:cw + (c1 - c0)])

    wcast_impl[0] = wcast_real

    def wget():
        i = wstate["next"]
        if wstate["issued"] <= i:
            wissue()
        s_ = i % NSLOT
        wstate["next"] = i + 1
        vs = []
        for pi, (ap, k, n, off) in enumerate(wjobs[i]):
            v_ = V(wsl[s_], wsl[s_].t[:, off:off + k * n].rearrange("p (k n) -> p k n", k=k))
            v_.job, v_.part = i, pi
            vs.append(v_)
        if job_castable(i):
            for u in range(len(job_units(i))):
                ensure_cast(i, u)
        while wstate["issued"] < len(wjobs):
            j = wstate["issued"]
            if job_from_scratch(j):
                if i < last0 or j > i + 3:
                    break
            elif j > i + 1:
                break
            wissue()
        return vs if len(vs) > 1 else vs[0]

    def wcol(wv, k, c0, c1):
        return V(wv.tt, wv.ap[:, k, c0:c1])

    def rms_rstd(src, W):
        ps = P.nps()
        for dc in range(8):
            sq = scr()
            P.act(sq[:, 0:W], src[:, dc, 0:W], AF.Square)
            P.mm1(ps[:, 0:W], onesm, sq[:, 0:W], dc == 0, dc == 7)
        sd = scr()
        P.act(sd[:, 0:W], ps[:, 0:W], AF.Sqrt, bias=epsn, scale=1.0 / D)
        rs = scr()
        P.recip(rs[:, 0:W], sd[:, 0:W])
        return rs

    def bc_s(v2d):
        return V(v2d.tt, v2d.ap.unsqueeze(1).broadcast_to([128, 4, NB]))

    def s3(vv):
        return V(vv.tt, vv.ap.rearrange("p (t b) -> p t b", b=NB))

    def modulate(l, src_rs, Amod, shift_k, npc, has_s, W):
        for dc in range(8):
            eng = "dve" if dc % 2 == 0 else "pool"
            t = scr()
            P.tt(eng, t[:, 0:W], xs[:, dc, 0:W], src_rs[:, 0:W], ALU.mult)
            P.ts(eng, hs[:, dc, 0:npc], t[:, 0:npc], Amod[:, dc, 0:1], mod[l][:, shift_k * 8 + dc, 0:1],
                 ALU.mult, ALU.add)
            if has_s:
                tv = s3(t[:, npc:npc + NSC])
                hv = s3(hs[:, dc, npc:npc + NSC])
                P.tt(eng, tv, tv, bc_s(Amod[:, dc, 1:17]), ALU.mult)
                P.tt(eng, hv, tv, bc_s(mod[l][:, shift_k * 8 + dc, 1:17]), ALU.add)

    def residual(l, ps, dc, gate_k, npc, has_s):
        P.stt(xs[:, dc, 0:npc], ps[:, 0:npc], mod[l][:, gate_k * 8 + dc, 0:1], xs[:, dc, 0:npc], ALU.mult, ALU.add)
        if has_s:
            t = scr()
            P.tt("dve", s3(t[:, 0:NSC]), s3(ps[:, npc:npc + NSC]), bc_s(mod[l][:, gate_k * 8 + dc, 1:17]), ALU.mult)
            P.tt("pool", xs[:, dc, npc:npc + NSC], xs[:, dc, npc:npc + NSC], t[:, 0:NSC], ALU.add)

    def evac_past(ps, dst, pp, sp, npc, has_s, eng=None):
        P.cp(eng or P.ev(), dst[:, pp:pp + npc], ps[:, 0:npc])
        if has_s:
            sc0 = pp + npc + NB * sp
            P.cp(eng or P.ev(), dst[:, sc0:sc0 + NSC], ps[:, npc:npc + NSC])

    def parts(pp, sp, npc, has_s):
        r = [(pp, npc, 1, 0)]
        if has_s:
            r.append((pp + npc + NB * sp, NSC, NB, npc))
        return r

    try:
        col0 = 0
        for si, npc in enumerate(segs):
            has_s = si == 0
            last = si == len(segs) - 1
            W = npc + (NSC if has_s else 0)
            nrc = npc // RC
            nwc = npc // WC
            xv = xT.rearrange("(k p) n -> p k n", p=128)
            rv = rot_d.rearrange("c p n -> p c n")
            P.dma("sp", xs[:, :, 0:npc], xv[:, :, col0:col0 + npc], ds_x)
            P.dma("sp", rot[:, :, 0:npc], rv[:, :, col0:col0 + npc], ds_rot)
            if has_s:
                P.dma("sp", xs[:, :, npc:W], xv[:, :, NPT:NPT + NSC], ds_x)
                P.dma("sp", rot[:, :, npc:W], rv[:, :, NPT:NPT + NSC], ds_rot)
            cosT = rot[:, 0, 0:W]
            sinT = rot[:, 1, 0:W]

            for l in range(nlayers):
                if si == 0:
                    for j in range(12):
                        wv = wget()
                        for c4 in range(4):
                            fc = j * 4 + c4
                            ps = P.nps()
                            wc = wcast(wv, 8, c4 * 128, (c4 + 1) * 128)
                            P.mm(ps[:, 0:17], [(V(wc.tt, wc.ap[:, k, :]), csil_b[:, k, :]) for k in range(8)])
                            P.ts("dve", mod[l][:, fc, :], ps[:, 0:17], vec(l, "bada", fc), None, ALU.add)
                    for dc in range(8):
                        for (Ax, sk, gn) in ((A1[l], 1, "n1g"), (A2[l], 4, "n2g")):
                            gb = V(vt_[l], vt_[l].t[:, VOFF[gn] + dc:VOFF[gn] + dc + 1].to_broadcast([128, 17]))
                            P.stt(Ax[:, dc, :], mod[l][:, sk * 8 + dc, :], 1.0, gb, ALU.add, ALU.mult)

                if stop_after == 'mod':
                    raise _Stop()
                rs = rms_rstd(xs, W)
                modulate(l, rs, A1[l], 0, npc, has_s, W)

                def zchunk(wv, c4):
                    ps = P.nps()
                    wc = wcast(wv, 8, c4 * 128, (c4 + 1) * 128)
                    P.mm(ps[:, 0:W], [(V(wc.tt, wc.ap[:, k, :]), hs[:, k, 0:W]) for k in range(8)])
                    return ps

                if stop_after == 'norm1':
                    raise _Stop()
                PP, SPP = 15, 15
                wv = wget()
                for c in range(2):
                    ps = zchunk(wv, c)
                    P.cp("pool", ubuf[c][:, 0:PP], pool_c[l][:, c, :])
                    if has_s:
                        P.dma("sp", ubuf[c][:, PP + npc:PP + npc + 240], st_pool[l, :, c, :], ds_ub[c])
                    evac_past(ps, ubuf[c], PP, SPP, npc, has_s)
                if stop_after == 'z_a':
                    raise _Stop()
                for c in range(2):
                    ps = zchunk(wv, 2 + c)
                    P.cp(P.ev(), zr[c][:, 0:W], ps[:, 0:W])
                if stop_after == 'z_b':
                    raise _Stop()
                wv = wget()
                for c in range(4):
                    ps = zchunk(wv, c)
                    P.cp(P.ev(), zr[2 + c][:, 0:W], ps[:, 0:W])
                if stop_after == 'z_c':
                    raise _Stop()
                wv = wget()
                for c in range(2):
                    ps = zchunk(wv, c)
                    P.act(zr[6 + c][:, 0:W], ps[:, 0:W], AF.Silu)
                if stop_after == 'z_d':
                    raise _Stop()
                wv = wget()
                for c in range(4):
                    ps = zchunk(wv, c)
                    P.cp(P.ev(), zsc[c][:, 0:W], ps[:, 0:W])
                if stop_after == 'z_e':
                    raise _Stop()
                wv = wget()
                if has_s:
                    P.dma("sp", sc_s[:, :, :], st_sc[l], ds_scs)
                for c in range(2):
                    ps = zchunk(wv, c)
                    evac_past(ps, zsc[4 + c], 2, 2, npc, has_s)
                if has_s:
                    P.dma("sp", sh_s[:, :, :], st_sh[l], ds_shs)
                if stop_after == 'z_f':
                    raise _Stop()
                for j in range(2):
                    wv = wget()
                    for c in range(4):
                        ps = zchunk(wv, c)
                        evac_past(ps, zw[4 * j + c], 1, 1, npc, has_s)

                if stop_after == 'zproj':
                    raise _Stop()
                for c in range(2):
                    U = ubuf[c]
                    Aw, Bw = wide
                    for (c0, n, st, f0) in parts(PP, SPP, npc, has_s):
                        r0 = c0 - 15 * st
                        e = c0 + n
                        P.tt("pool", Aw[:, r0 + st:e], U[:, r0 + st:e], U[:, r0:e - st], ALU.add)
                        P.tt("pool", Bw[:, r0 + 3 * st:e], Aw[:, r0 + 3 * st:e], Aw[:, r0 + st:e - 2 * st], ALU.add)
                        if c == 1:
                            P.tt("pool", Aw[:, r0 + 7 * st:e], Bw[:, r0 + 7 * st:e], Bw[:, r0 + 3 * st:e - 4 * st], ALU.add)
                            P.tt("pool", Bw[:, r0 + 15 * st:e], Aw[:, r0 + 15 * st:e], Aw[:, r0 + 7 * st:e - 8 * st], ALU.add)
                    m = scr()
                    for (c0, n, st, f0) in parts(PP, SPP, npc, has_s):
                        for half, src in ((0, Aw), (1, Bw)):
                            p0, p1 = half * 64, half * 64 + 64
                            P.stt(m[p0:p1, f0:f0 + n], src[p0:p1, c0:c0 + n], tab("invw", p0, p1, c, c + 1),
                                  U[p0:p1, c0:c0 + n], ALU.mult, ALU.subtract)
                            if si == 0 and st == 1:
                                t = scr()
                                P.tt("dve", t[p0:p1, 0:16], src[p0:p1, c0:c0 + 16], tab("icnt", p0, p1, c * 16, c * 16 + 16),
                                     ALU.mult)
                                P.tt("dve", m[p0:p1, 0:16], t[p0:p1, 0:16], U[p0:p1, c0:c0 + 16], ALU.subtract)
                    ps = P.nps()
                    P.mm(ps[:, 0:W], [(pbd[l][:, c, :], m[:, 0:W])])
                    P.ts("dve", yb[c][:, 0:W], ps[:, 0:W], vec(l, "pscale", c), None, ALU.mult)
                    P.cp("pool", pool_c[l][:, c, :], U[:, PP + npc - 15:PP + npc])
                    if has_s:
                        s0 = PP + npc
                        P.dma("act", o_pool_s[l, :, c, :], U[:, s0 + 64:s0 + 64 + 240], ds_ub[c])
                if last:
                    P.dma("act", o_pool_p[l], pool_c[l][:, :, :], ds_carry[0])

                if stop_after == 'pool':
                    raise _Stop()
                qr = [rx[0], rx[1]]
                kr = [rx[2], rx[3]]
                qd = [rx[4], rx[5]]
                for hp in range(2):
                    for (src, dst, scl) in ((zr[hp], qr[hp], 1.0), (zr[2 + hp], kr[hp], 0.125)):
                        ps = P.nps()
                        P.mm(ps[:, 0:W], [(perm, src[:, 0:W])])
                        t1 = scr()
                        P.stt(t1[:, 0:W], src[:, 0:W], scl, cosT, ALU.mult, ALU.mult)
                        P.stt(dst[:, 0:W], ps[:, 0:W], scl, sinT, ALU.mult, ALU.mult)
                        P.tt("pool", dst[:, 0:W], dst[:, 0:W], t1[:, 0:W], ALU.add)
                    qdt = V(tabs, tabs.t[:, TOFF["qdec"] + hp * 128:TOFF["qdec"] + (hp + 1) * 128]
                            .unsqueeze(1).broadcast_to([128, nrc, RC]))
                    P.tt("dve", V(qd[hp], qd[hp].t[:, 0:npc].rearrange("p (c i) -> p c i", i=RC)),
                         V(qr[hp], qr[hp].t[:, 0:npc].rearrange("p (c i) -> p c i", i=RC)), qdt, ALU.mult)
                    if has_s:
                        qd4 = V(tabs, tabs.t[:, TOFF["qdec"] + hp * 128:TOFF["qdec"] + hp * 128 + 4]
                                .unsqueeze(2).broadcast_to([128, 4, NB]))
                        P.tt("dve", s3(qd[hp][:, npc:W]), s3(qr[hp][:, npc:W]), qd4, ALU.mult)
                if stop_after == 'r_a':
                    raise _Stop()
                ops_ = [P.psr[6], P.psr[7]]
                for j in range(nrc):
                    cs = slice(j * RC, (j + 1) * RC)
                    pv = P.nps()
                    pk = P.nps()
                    for hp in range(2):
                        P.tr(pv[:, hp * 128:(hp + 1) * 128], zr[4 + hp][:, cs], ident)
                        P.tr(pk[:, hp * 128:(hp + 1) * 128], kr[hp][:, cs], ident)
                    vtk = vtok[j % 2]
                    ktk = ktok[j % 2]
                    P.cp("act", vtk[:, :], pv[:, 0:256])
                    P.tt("dve", ktk[:, :], pk[:, 0:256], tab("kdec", 0, 128, 0, 256), ALU.mult)
                    for h in range(4):
                        hp, h2 = h // 2, h % 2
                        p0, p1 = h2 * 64, h2 * 64 + 64
                        hc = slice(h * 64, h * 64 + 64)
                        pss = P.nps()
                        P.mm(pss[:, 0:RC], [(kr[hp][p0:p1, cs], qr[hp][p0:p1, cs])])
                        st_ = sTs[h % 2]
                        P.tt("dve", st_[:, :], pss[:, 0:RC], tab("dmask", 0, 128, h * 128, h * 128 + 128), ALU.mult)
                        P.mm(ops_[hp][p0:p1, cs], [(vtk[:, hc], st_[:, :]), (retS[l][p0:p1, hp, :], qd[hp][p0:p1, cs])])
                        psS = P.nps()
                        P.mm(psS[p0:p1, 0:64], [(ktk[:, hc], vtk[:, hc])])
                        P.stt(retS[l][p0:p1, hp, :], retS[l][p0:p1, hp, :], tab("cdec", p0, p1, hp, hp + 1),
                              psS[p0:p1, 0:64], ALU.mult, ALU.add)
                if stop_after == 'r_b':
                    raise _Stop()
                if last:
                    P.dma("act", o_ret_p[l], retS[l][:, :, :], ds_carry[1])
                if has_s:
                    P.dma("sp", sstate[:, :, :, :], st_ret[l], ds_st)
                    ck("ret sstate dma")
                    for b in range(NB):
                        cb = slice(npc + b, npc + NSC, NB)
                        ob = slice(npc + 4 * b, npc + 4 * b + 4)
                        pt = P.nps()
                        for hp in range(2):
                            P.trx(pt[0:4, hp * 128:(hp + 1) * 128], zr[4 + hp][:, cb], idv, 128, 4)
                            P.trx(pt[0:4, 256 + hp * 128:256 + (hp + 1) * 128], kr[hp][:, cb], idv, 128, 4)
                        vtk = vtok[b % 2]
                        ktk = ktok[b % 2]
                        sm = smask[b % 2]
                        P.cp("pool", sm[0:64, :, 0, :], sstate[0:64, b, :, :])
                        P.cp("pool", sm[64:128, :, 1, :], sstate[64:128, b, :, :])
                        ck("ret s transposes")
                        P.cp("act", vtk[0:4, :], pt[0:4, 0:256])
                        ck("ret s cp act")
                        P.cp("act", ktk[0:4, :], pt[0:4, 256:512])
                        P.tt("dve", ktk[0:4, :], ktk[0:4, :], tab("kdec4", 0, 4, 0, 256), ALU.mult)
                        ck("ret s tt dve")
                        for h in range(4):
                            hp, h2 = h // 2, h % 2
                            p0, p1 = h2 * 64, h2 * 64 + 64
                            hc = slice(h * 64, h * 64 + 64)
                            pss = P.nps()
                            P.mm(pss[0:4, 0:4], [(kr[hp][p0:p1, cb], qr[hp][p0:p1, cb])])
                            ck("ret s scores")
                            st_ = sTs[h % 2]
                            P.cp("act", st_[0:4, 0:4], pss[0:4, 0:4])
                            P.tt("dve", st_[0:4, 0:4], st_[0:4, 0:4], tab("dmask", 0, 4, h * 128, h * 128 + 4), ALU.mult)
                            ck("ret s mask")
                            P.mm(ops_[hp][p0:p1, ob], [(vtk[0:4, hc], st_[0:4, 0:4]), (sm[:, hp, h2, :], qd[hp][:, cb])])
                            ck("ret s o")
                            psS = P.nps()
                            P.mm(psS[p0:p1, 0:64], [(ktk[0:4, hc], vtk[0:4, hc])])
                            ck("ret s S mm")
                            P.stt(sstate[p0:p1, b, hp, :], sstate[p0:p1, b, hp, :], tab("cdec4", p0, p1, hp, hp + 1),
                                  psS[p0:p1, 0:64], ALU.mult, ALU.add)
                    P.dma("act", o_ret_s[l], sstate[:, :, :, :], ds_st)
                if stop_after == 'r_c':
                    raise _Stop()
                for hp in range(2):
                    o = scr()
                    P.cp("act", o[:, 0:npc], ops_[hp][:, 0:npc])
                    if has_s:
                        P.cp("act", s3(o[:, npc:W]), V(ops_[hp], ops_[hp].t[:, npc:W].rearrange("p (b t) -> p t b", t=4)))
                    sq = scr()
                    P.act(sq[:, 0:W], o[:, 0:W], AF.Square)
                    ps = P.nps()
                    P.mm(ps[:, 0:W], [(bones, sq[:, 0:W])])
                    sd = scr()
                    P.act(sd[:, 0:W], ps[:, 0:W], AF.Sqrt, bias=epsn, scale=1.0 / 64)
                    P.recip(sd[:, 0:W], sd[:, 0:W])
                    P.tt("dve", o[:, 0:W], o[:, 0:W], sd[:, 0:W], ALU.mult)
                    P.tt("pool", yb[2 + hp][:, 0:W], o[:, 0:W], zr[6 + hp][:, 0:W], ALU.mult)

                if stop_after == 'ret':
                    raise _Stop()
                for c in range(2):
                    pb = zsc[4 + c]
                    hc_ = zsc[c]
                    Bc = zsc[2 + c]
                    P.cp("pool", pb[:, 0:2], sc_c[l][:, c, :])
                    if has_s:
                        P.cp("pool", pb[:, 2 + npc:2 + npc + 32], sc_s[:, c, :])
                    cv = scr()
                    for (c0, n, st, f0) in parts(2, 2, npc, has_s):
                        P.tt("pool", pb[:, c0:c0 + n], pb[:, c0:c0 + n], hc_[:, f0:f0 + n], ALU.mult)
                    for (c0, n, st, f0) in parts(2, 2, npc, has_s):
                        P.ts("dve", cv[:, f0:f0 + n], pb[:, c0:c0 + n], vec(l, "scw", 4 + c), None, ALU.mult)
                        P.stt(cv[:, f0:f0 + n], pb[:, c0 - st:c0 - st + n], vec(l, "scw", 2 + c), cv[:, f0:f0 + n],
                              ALU.mult, ALU.add)
                        P.stt(cv[:, f0:f0 + n], pb[:, c0 - 2 * st:c0 - 2 * st + n], vec(l, "scw", c), cv[:, f0:f0 + n],
                              ALU.mult, ALU.add)
                    P.tt("pool", yb[4 + c][:, 0:W], cv[:, 0:W], Bc[:, 0:W], ALU.mult)
                    P.cp("pool", sc_c[l][:, c, :], pb[:, npc:npc + 2])
                    if has_s:
                        s0 = 2 + npc + 32
                        P.cp("pool", sc_s[:, c, :], pb[:, s0 + 32:s0 + 64])
                if has_s:
                    P.dma("act", o_sc_s[l], sc_s[:, :, :], ds_scs)
                if last:
                    P.dma("act", o_sc_p[l], sc_c[l][:, :, :], ds_carry[2])

                if stop_after == 'sc':
                    raise _Stop()
                R = [arena[0], arena[1]]
                K = [arena[2], arena[3]]
                Vv = [arena[4], arena[5]]
                C6, C7 = arena[6], arena[7]
                flat = [R[0], R[1], K[0], K[1], Vv[0], Vv[1], C6, C7]
                for c in range(8):
                    Z = zw[c]
                    P.cp("pool", Z[:, 0:1], sh_c[l][:, c:c + 1])
                    if has_s:
                        P.cp("pool", Z[:, 1 + npc:1 + npc + 16], sh_s[:, c, :])
                    for (c0, n, st, f0) in parts(1, 1, npc, has_s):
                        t = scr()
                        P.tt("pool", t[:, 0:n], Z[:, c0 - st:c0 - st + n], Z[:, c0:c0 + n], ALU.subtract)
                        P.stt(flat[c][:, f0:f0 + n], t[:, 0:n], vec(l, "mu", c), Z[:, c0:c0 + n], ALU.mult, ALU.add)
                    P.cp("pool", sh_c[l][:, c:c + 1], Z[:, npc:npc + 1])
                    if has_s:
                        s0 = 1 + npc + 16
                        P.cp("pool", sh_s[:, c, :], Z[:, s0 + 48:s0 + 64])
                if has_s:
                    P.dma("act", o_sh_s[l], sh_s[:, :, :], ds_shs)
                if last:
                    P.dma("act", o_sh_p[l], sh_c[l][:, :], ds_carry[3])
                ck("w_shift")
                LW = [arena[14], arena[15]]
                AS = [arena[16], arena[17]]
                G = [arena[18], arena[19]]
                KK = [arena[20], arena[21]]
                AT = [arena[8], arena[9]]
                RT = [arena[10], arena[11]]
                BT = [arena[12], arena[13]]
                KT = [rx[0], rx[1]]
                BH = [rx[2], rx[3]]
                KH = [rx[4], rx[5]]
                BON = [rx[6], rx[7]]
                Y = [rx[8], rx[9]]
                tw = scr()
                P.act(tw[0:64, 0:W], C6[0:64, 0:W], AF.Tanh)
                sg = scr()
                P.act(sg[:, 0:W], C7[:, 0:W], AF.Sigmoid)
                for hp in range(2):
                    cc = slice(hp * 128, hp * 128 + 128)
                    ps = P.nps()
                    P.mm(ps[:, 0:W], [(lwa[l][0:64, cc], tw[0:64, 0:W])])
                    t = scr()
                    P.act(t[:, 0:W], ps[:, 0:W], AF.Sigmoid, bias=vec(l, "w0", hp))
                    P.ts("dve", LW[hp][:, 0:W], t[:, 0:W], -math.exp(-0.5), None, ALU.mult)
                    ps = P.nps()
                    P.mm(ps[:, 0:W], [(lwa[l][64:128, cc], C6[64:128, 0:W])])
                    P.act(AS[hp][:, 0:W], ps[:, 0:W], AF.Sigmoid, bias=vec(l, "a0", hp))
                    ps = P.nps()
                    P.mm(ps[:, 0:W], [(lgl[l][:, cc], sg[:, 0:W])])
                    P.cp("act", G[hp][:, 0:W], ps[:, 0:W])
                ck("w_lora")
                for hp in range(2):
                    t = scr()
                    P.ts("dve", t[:, 0:W], K[hp][:, 0:W], vec(l, "kk", hp), None, ALU.mult)
                    sq = scr()
                    P.act(sq[:, 0:W], t[:, 0:W], AF.Square)
                    ps = P.nps()
                    P.mm(ps[:, 0:W], [(bones, sq[:, 0:W])])
                    sd = scr()
                    P.act(sd[:, 0:W], ps[:, 0:W], AF.Sqrt)
                    P.ts("dve", sd[:, 0:W], sd[:, 0:W], 1e-12, None, ALU.max)
                    P.recip(sd[:, 0:W], sd[:, 0:W])
                    P.tt("dve", KK[hp][:, 0:W], t[:, 0:W], sd[:, 0:W], ALU.mult)
                    t2 = scr()
                    P.ts("dve", t2[:, 0:W], AS[hp][:, 0:W], vec(l, "ka", hp), omka[l][:, hp:hp + 1], ALU.mult, ALU.add)
                    P.tt("pool", K[hp][:, 0:W], K[hp][:, 0:W], t2[:, 0:W], ALU.mult)
                    t3 = scr()
                    P.stt(t3[:, 0:W], R[hp][:, 0:W], vec(l, "rk", hp), K[hp][:, 0:W], ALU.mult, ALU.mult)
                    ps = P.nps()
                    P.mm(ps[:, 0:W], [(bones, t3[:, 0:W])])
                    P.tt("dve", BON[hp][:, 0:W], ps[:, 0:W], Vv[hp][:, 0:W], ALU.mult)
                    P.tt("pool", AS[hp][:, 0:W], AS[hp][:, 0:W], KK[hp][:, 0:W], ALU.mult)
                    T1, T2 = scr(), scr()
                    seq = [LW[hp], T1, T2, T1, T2, T1, T2]
                    for i, s_ in enumerate((1, 2, 4, 8, 16, 32)):
                        a_, b_ = seq[i], seq[i + 1]
                        av = V(a_, a_.t[:, 0:npc].rearrange("p (c i) -> p c i", i=WC))
                        bv = V(b_, b_.t[:, 0:npc].rearrange("p (c i) -> p c i", i=WC))
                        P.tt("pool", V(b_, bv.ap[:, :, s_:WC]), V(a_, av.ap[:, :, s_:WC]), V(a_, av.ap[:, :, 0:WC - s_]), ALU.add)
                        P.cp("pool", V(b_, bv.ap[:, :, 0:s_]), V(a_, av.ap[:, :, 0:s_]))
                    CUM = T2
                    if has_s:
                        l3 = s3(LW[hp][:, npc:W])
                        a3 = s3(T1[:, npc:W])
                        c3 = s3(CUM[:, npc:W])
                        P.tt("pool", V(T1, a3.ap[:, 1:4, :]), V(LW[hp], l3.ap[:, 1:4, :]), V(LW[hp], l3.ap[:, 0:3, :]), ALU.add)
                        P.cp("pool", V(T1, a3.ap[:, 0:1, :]), V(LW[hp], l3.ap[:, 0:1, :]))
                        P.tt("pool", V(CUM, c3.ap[:, 2:4, :]), V(T1, a3.ap[:, 2:4, :]), V(T1, a3.ap[:, 0:2, :]), ALU.add)
                        P.cp("pool", V(CUM, c3.ap[:, 0:2, :]), V(T1, a3.ap[:, 0:2, :]))
                    EC = scr()
                    P.act(EC[:, 0:W], CUM[:, 0:W], AF.Exp)
                    P.tt("dve", RT[hp][:, 0:W], R[hp][:, 0:W], EC[:, 0:W], ALU.mult)
                    EN = scr()
                    P.act(EN[:, 0:W], CUM[:, 0:W], AF.Exp, scale=-1.0)
                    P.tt("dve", BT[hp][:, 0:W], AS[hp][:, 0:W], EN[:, 0:W], ALU.mult)
                    P.tt("pool", KT[hp][:, 0:W], K[hp][:, 0:W], EN[:, 0:W], ALU.mult)
                    cex = scr()
                    P.tt("dve", cex[:, 0:W], CUM[:, 0:W], LW[hp][:, 0:W], ALU.subtract)
                    P.act(cex[:, 0:W], cex[:, 0:W], AF.Exp)
                    P.stt(AT[hp][:, 0:W], KK[hp][:, 0:W], -1.0, cex[:, 0:W], ALU.mult, ALU.mult)
                    dl = scr()
                    cv_ = V(CUM, CUM.t[:, 0:npc].rearrange("p (c i) -> p c i", i=WC))
                    P.tt("dve", V(dl, dl.t[:, 0:npc].rearrange("p (c i) -> p c i", i=WC)),
                         V(CUM, cv_.ap[:, :, WC - 1:WC].broadcast_to([128, nwc, WC])), cv_, ALU.subtract)
                    if has_s:
                        c3 = s3(CUM[:, npc:W])
                        P.tt("dve", s3(dl[:, npc:W]), V(CUM, c3.ap[:, 3:4, :].broadcast_to([128, 4, NB])), c3, ALU.subtract)
                    P.act(dl[:, 0:W], dl[:, 0:W], AF.Exp)
                    P.tt("dve", BH[hp][:, 0:W], AS[hp][:, 0:W], dl[:, 0:W], ALU.mult)
                    P.tt("pool", KH[hp][:, 0:W], K[hp][:, 0:W], dl[:, 0:W], ALU.mult)
                    P.cp("pool", LW[hp][:, 0:W], EC[:, 0:W])
                ck("w_prep")
                ECs = LW

                def rwkv_group(chunks):
                    units = []
                    for ci_, ch in enumerate(chunks):
                        C = ch["C"]
                        cs = ch["cols"]
                        pa_ = P.nps()
                        pb_ = P.nps()
                        for hp in range(2):
                            P.trx(pa_[0:C, hp * 128:(hp + 1) * 128], Vv[hp][:, cs], idv, 128, C)
                            P.trx(pa_[0:C, 256 + hp * 128:256 + (hp + 1) * 128], BH[hp][:, cs], idv, 128, C)
                            P.trx(pb_[0:C, hp * 128:(hp + 1) * 128], KH[hp][:, cs], idv, 128, C)
                        tk = tokm[ch["tok"]]
                        P.cp("act", tk[0:C, 0:512], pa_[0:C, 0:512])
                        P.cp("dve" if C >= 128 else "act", tk[0:C, 512:768], pb_[0:C, 0:256])
                        ch["tk"] = tk
                        for h in range(4):
                            units.append((ch, h, ci_ * 4 + h))
                    ck("w_tok")
                    for (ch, h, u) in units:
                        C, cs = ch["C"], ch["cols"]
                        hp, h2 = h // 2, h % 2
                        p0, p1 = h2 * 64, h2 * 64 + 64
                        at_, rt_, bt_, kt_ = AT[hp][p0:p1, cs], RT[hp][p0:p1, cs], BT[hp][p0:p1, cs], KT[hp][p0:p1, cs]
                        pa_ = P.nps()
                        P.mm(pa_[0:C, 0:C], [(bt_, at_)])
                        P.mm(pa_[0:C, C:2 * C], [(bt_, rt_)])
                        P.mm(pa_[0:C, 2 * C:3 * C], [(kt_, at_)])
                        P.mm(pa_[0:C, 3 * C:4 * C], [(kt_, rt_)])
                        P.mm(pa_[0:C, 4 * C:5 * C], [(at_, bt_)])
                        mk = tab("msk64" if C == 64 else "msk4", 0, C, 0, 5 * C)
                        if C >= 128:
                            P.tt("dve", Am[u][0:C, 0:5 * C], pa_[0:C, 0:5 * C], mk, ALU.mult)
                        else:
                            P.cp("act", Am[u][0:C, 0:5 * C], pa_[0:C, 0:5 * C])
                            P.tt("dve", Am[u][0:C, 0:5 * C], Am[u][0:C, 0:5 * C], mk, ALU.mult)
                        P.tt("pool", Tm[u][0:C, 0:C], Am[u][0:C, 0:C], tab("ident", 0, C, 0, C), ALU.add)
                    ck("w_amat")
                    nlev = 5 if chunks[0]["C"] == 64 else 1
                    cur = {u: (Am[u][0:ch["C"], 0:ch["C"]], Am[u][0:ch["C"], 4 * ch["C"]:5 * ch["C"]]) for (ch, h, u) in units}
                    for lev in range(nlev):
                        pend = {}
                        for (ch, h, u) in units:
                            C = ch["C"]
                            cp_, cpt_ = cur[u]
                            pp_ = P.nps()
                            P.mm(pp_[0:C, C:2 * C], [(cp_, cpt_)])
                            if lev < nlev - 1:
                                P.mm(pp_[0:C, 0:C], [(cpt_, cp_)])
                            buf = Pq[u]
                            eve = P.ev() if C >= 128 else "act"
                            if lev < nlev - 1:
                                P.cp(eve, buf[0:C, 0:2 * C], pp_[0:C, 0:2 * C])
                            else:
                                P.cp(eve, buf[0:C, C:2 * C], pp_[0:C, C:2 * C])
                            cur[u] = (buf[0:C, 0:C], buf[0:C, C:2 * C])
                        for (ch, h, u) in units:
                            C = ch["C"]
                            pt_ = P.nps()
                            P.mm(pt_[0:C, 0:C], [(cur[u][1], Tm[u][0:C, 0:C])])
                            if C >= 128:
                                P.tt("dve", Tm[u][0:C, 0:C], Tm[u][0:C, 0:C], pt_[0:C, 0:C], ALU.add)
                            else:
                                P.cp("act", Am[u][0:C, 4 * C:5 * C], pt_[0:C, 0:C])
                                P.tt("dve", Tm[u][0:C, 0:C], Tm[u][0:C, 0:C], Am[u][0:C, 4 * C:5 * C], ALU.add)
                    ck("w_lev")
                    for ci_, ch in enumerate(chunks):
                        C, cs, tk = ch["C"], ch["cols"], ch["tk"]
                        xt, ut, yt = XTs[0], UTs[0], YTs[0]
                        hd = []
                        for h in range(4):
                            hp, h2 = h // 2, h % 2
                            p0, p1 = h2 * 64, h2 * 64 + 64
                            hd.append((h, hp, p0, p1, slice(h * 64, h * 64 + 64), ci_ * 4 + h))
                        px = P.nps()
                        for (h, hp, p0, p1, hc, u) in hd:
                            P.mm(px[0:C, hc], [(AT[hp][:, cs], ch["stm"](h % 2, hp)),
                                               (Am[u][0:C, 2 * C:3 * C], tk[0:C, hc])])
                        P.cp("act", xt[0:C, :], px[0:C, 0:256])
                        ck("w_xt")
                        pu = P.nps()
                        for (h, hp, p0, p1, hc, u) in hd:
                            P.mm(pu[0:C, hc], [(Tm[u][0:C, 0:C], xt[0:C, hc])])
                        P.cp("dve" if C >= 128 else "act", ut[0:C, :], pu[0:C, 0:256])
                        ck("w_ut")
                        py = P.nps()
                        for (h, hp, p0, p1, hc, u) in hd:
                            P.mm(py[0:C, hc], [(RT[hp][:, cs], ch["stm"](h % 2, hp)),
                                               (Am[u][0:C, C:2 * C], ut[0:C, hc]),
                                               (Am[u][0:C, 3 * C:4 * C], tk[0:C, hc])])
                        P.cp("act", yt[0:C, :], py[0:C, 0:256])
                        ck("w_yt")
                        pS = P.nps()
                        for (h, hp, p0, p1, hc, u) in hd:
                            P.mm(pS[p0:p1, hp * 64:(hp + 1) * 64], [(tk[0:C, 256 + h * 64:256 + (h + 1) * 64], ut[0:C, hc]),
                                                                    (tk[0:C, 512 + h * 64:512 + (h + 1) * 64], tk[0:C, hc])])
                        for hp in range(2):
                            stv = ch["st"](0, 128, hp)
                            P.stt(stv, stv, ch["pc"](hp), pS[:, hp * 64:(hp + 1) * 64], ALU.mult, ALU.add)
                        ch["refresh"]()
                        ck("w_st")
                        pyb = P.nps()
                        for hp in range(2):
                            P.trx(pyb[:, hp * C:(hp + 1) * C], yt[0:C, hp * 128:(hp + 1) * 128], idv, C, 128)
                        for hp in range(2):
                            P.cp(P.ev(), ch["ydst"](hp), pyb[:, hp * C:(hp + 1) * C])

                def refresh_p():
                    P.cp("pool", wkvM[l][0:64, :, 0, :], wkvS[l][0:64, :, :])
                    P.cp("pool", wkvM[l][64:128, :, 1, :], wkvS[l][64:128, :, :])

                for g0 in range(0, nwc, 2):
                    chs = []
                    for k_ in range(g0, min(g0 + 2, nwc)):
                        cs = slice(k_ * WC, (k_ + 1) * WC)
                        chs.append(dict(
                            cols=cs, C=WC, tok=(k_ % 2),
                            st=(lambda p0, p1, hp: wkvS[l][p0:p1, hp, :]),
                            stm=(lambda h2, hp: wkvM[l][:, hp, h2, :]),
                            refresh=refresh_p,
                            pc=(lambda hp, k_=k_: ECs[hp][:, (k_ + 1) * WC - 1:(k_ + 1) * WC]),
                            ydst=(lambda hp, cs=cs: Y[hp][:, cs])))
                    rwkv_group(chs)
                ck("w_prompt")
                if last:
                    P.dma("act", o_wkv_p[l], wkvS[l][:, :, :], ds_carry[4])
                if has_s:
                    P.dma("sp", sstate[:, :, :, :], st_wkv[l], ds_st)
                    for g0 in range(0, NB, 2):
                        chs = []
                        for b in range(g0, g0 + 2):
                            cb = slice(npc + b, npc + NSC, NB)
                            P.cp("pool", smask[b % 2][0:64, :, 0, :], sstate[0:64, b, :, :])
                            P.cp("pool", smask[b % 2][64:128, :, 1, :], sstate[64:128, b, :, :])
                            chs.append(dict(
                                cols=cb, C=4, tok=(b % 2),
                                st=(lambda p0, p1, hp, b=b: sstate[p0:p1, b, hp, :]),
                                stm=(lambda h2, hp, b=b: smask[b % 2][:, hp, h2, :]),
                                refresh=(lambda: None),
                                pc=(lambda hp, b=b: ECs[hp][:, npc + 48 + b:npc + 48 + b + 1]),
                                ydst=(lambda hp, cb=cb: Y[hp][:, cb])))
                        rwkv_group(chs)
                    P.dma("act", o_wkv_s[l], sstate[:, :, :, :], ds_st)
                ck("w_sample")
                for hp in range(2):
                    ps = P.nps()
                    P.mm(ps[:, 0:W], [(bones, Y[hp][:, 0:W])])
                    yc = scr()
                    P.stt(yc[:, 0:W], ps[:, 0:W], -1.0 / 64, Y[hp][:, 0:W], ALU.mult, ALU.add)
                    sq = scr()
                    P.act(sq[:, 0:W], yc[:, 0:W], AF.Square)
                    ps = P.nps()
                    P.mm(ps[:, 0:W], [(bones, sq[:, 0:W])])
                    sd = scr()
                    P.act(sd[:, 0:W], ps[:, 0:W], AF.Sqrt, bias=epsln, scale=1.0 / 64)
                    P.recip(sd[:, 0:W], sd[:, 0:W])
                    P.tt("dve", yc[:, 0:W], yc[:, 0:W], sd[:, 0:W], ALU.mult)
                    P.ts("dve", yc[:, 0:W], yc[:, 0:W], vec(l, "lng", hp), vec(l, "lnb", hp), ALU.mult, ALU.add)
                    P.tt("pool", yc[:, 0:W], yc[:, 0:W], BON[hp][:, 0:W], ALU.add)
                    P.tt("pool", yb[6 + hp][:, 0:W], yc[:, 0:W], G[hp][:, 0:W], ALU.mult)

                if stop_after == 'rwkv':
                    raise _Stop()
                merged = arena[0:8]
                if BF16_DENSE:
                    mbf = [V(arena[8 + dc], arena[8 + dc].t[:, :].bitcast(BF16)[:, 0:W]) for dc in range(8)]
                else:
                    mbf = [merged[dc][:, 0:W] for dc in range(8)]
                for n in range(4):
                    for q in range(4):
                        wg, wb = wget()
                        for c4 in range(2):
                            dc = q * 2 + c4
                            ps = zchunk(wg, c4)
                            sgt = scr()
                            P.act(sgt[:, 0:W], ps[:, 0:W], AF.Sigmoid)
                            pp_ = P.nps()
                            wc2 = wcast(wb, 2, c4 * 128, (c4 + 1) * 128)
                            P.mm(pp_[:, 0:W], [(V(wc2.tt, wc2.ap[:, kk_, :]), yb[2 * n + kk_][:, 0:W]) for kk_ in range(2)])
                            if n == 0:
                                P.tt("dve", merged[dc][:, 0:W], sgt[:, 0:W], pp_[:, 0:W], ALU.mult)
                            else:
                                t = scr()
                                P.tt("dve", t[:, 0:W], sgt[:, 0:W], pp_[:, 0:W], ALU.mult)
                                if BF16_DENSE and n == 3:
                                    P.tt("pool", mbf[dc], merged[dc][:, 0:W], t[:, 0:W], ALU.add)
                                else:
                                    P.tt("pool", merged[dc][:, 0:W], merged[dc][:, 0:W], t[:, 0:W], ALU.add)
                for half in range(2):
                    wv = wget()
                    for c4 in range(4):
                        dc = half * 4 + c4
                        ps = P.nps()
                        wc = wcast(wv, 8, c4 * 128, (c4 + 1) * 128)
                        P.mm(ps[:, 0:W], [(V(wc.tt, wc.ap[:, k, :]), mbf[k]) for k in range(8)])
                        residual(l, ps, dc, 2, npc, has_s)

                if stop_after == 'mix':
                    continue
                rs = rms_rstd(xs, W)
                modulate(l, rs, A2[l], 3, npc, has_s, W)
                if has_s:
                    P.dma("sp", ffn_s[:, :, :], st_ffn[l], ds_ffs)
                actt = arena
                if BF16_DENSE:
                    abf = []
                    for i_ in range(22):
                        if i_ < 20:
                            tl = rx[i_ // 2]
                            abf.append(V(tl, tl.t[:, :].bitcast(BF16)[:, (i_ % 2) * W0:(i_ % 2) * W0 + W]))
                        else:
                            abf.append(yb[i_ - 20][:, 0:W])
                else:
                    abf = [actt[i_][:, 0:W] for i_ in range(22)]
                for j in range(11):
                    wv = wget()
                    for c4 in range(4):
                        ci = 4 * j + c4
                        ps = zchunk(wv, c4)
                        fb = fbuf[ci % 4]
                        P.cp("pool", fb[:, 0:2], ffn_c[l][:, ci, :])
                        if has_s:
                            P.cp("pool", fb[:, 2 + npc:2 + npc + 32], ffn_s[:, ci, :])
                        evac_past(ps, fb, 2, 2, npc, has_s, eng="dve")
                        cv = scr()
                        for (c0, n, st, f0) in parts(2, 2, npc, has_s):
                            P.ts("dve", cv[:, f0:f0 + n], fb[:, c0:c0 + n], vec(l, "ffw", 2 * NFC + ci), None, ALU.mult)
                            P.stt(cv[:, f0:f0 + n], fb[:, c0 - st:c0 - st + n], vec(l, "ffw", NFC + ci), cv[:, f0:f0 + n],
                                  ALU.mult, ALU.add)
                            P.stt(cv[:, f0:f0 + n], fb[:, c0 - 2 * st:c0 - 2 * st + n], vec(l, "ffw", ci), cv[:, f0:f0 + n],
                                  ALU.mult, ALU.add)
                        if ci < 22:
                            P.act(actt[ci][:, 0:W], cv[:, 0:W], AF.Silu)
                        else:
                            P.tt("pool", abf[ci - 22], actt[ci - 22][:, 0:W], cv[:, 0:W], ALU.mult)
                        P.cp("pool", ffn_c[l][:, ci, :], fb[:, npc:npc + 2])
                        if has_s:
                            s0 = 2 + npc + 32
                            P.cp("pool", ffn_s[:, ci, :], fb[:, s0 + 32:s0 + 64])
                if has_s:
                    P.dma("act", o_ffn_s[l], ffn_s[:, :, :], ds_ffs)
                if last:
                    P.dma("act", o_ffn_p[l], ffn_c[l][:, :, :], ds_carry[5])
                for j in range(8):
                    wva, wvb = wget()
                    wca = wcast(wva, 11, 0, 128)
                    wcb = wcast(wvb, 11, 0, 128)
                    ps = P.nps()
                    P.mm(ps[:, 0:W], [(V(wca.tt, wca.ap[:, ci, :]), abf[ci]) for ci in range(11)] +
                                     [(V(wcb.tt, wcb.ap[:, ci, :]), abf[11 + ci]) for ci in range(11)])
                    residual(l, ps, j, 5, npc, has_s)

            rs = rms_rstd(xs, W)
            for dc in range(8):
                P.tt("dve", xs[:, dc, 0:W], xs[:, dc, 0:W], rs[:, 0:W], ALU.mult)
                P.ts("pool", xs[:, dc, 0:W], xs[:, dc, 0:W], vec(0, "fg", dc), None, ALU.mult)
            yv = yT.rearrange("(k p) n -> p k n", p=128)
            P.dma("act", yv[:, :, col0:col0 + npc], xs[:, :, 0:npc], ds_y)
            if has_s:
                P.dma("act", yv[:, :, NPT:NPT + NSC], xs[:, :, npc:W], ds_y)
            col0 += npc
    except _Stop:
        pass

    P.finish(P.alld)
    P.emit()
    return nc


_CACHE = {}


def _fm(v):
    v = np.asarray(v, np.float32).reshape(-1, 128)
    return np.ascontiguousarray(v.T)


def prepare(**inp):
    f = lambda k: np.asarray(inp[k], np.float32)
    x_prompt, x_sample = f("x_prompt"), f("x_sample")
    c_prompt, c_sample = f("c_prompt"), f("c_sample")
    ncores = 8
    vec_l = []
    for l in range(L):
        cols = [_fm(f("norm1_g")[l]), _fm(f("norm2_g")[l]), _fm(f("pool_scale")[l])]
        cols += [_fm(f("sc_w")[l][k]) for k in range(3)]
        cols += [_fm(f("rw_mu")[l]), _fm(f("rw_w0")[l]), _fm(f("rw_a0")[l]), _fm(f("rw_k_k")[l]), _fm(f("rw_k_a")[l]),
                 _fm(f("rw_r_k")[l].reshape(-1)), _fm(f("rw_ln_g")[l]), _fm(f("rw_ln_b")[l])]
        cols += [_fm(f("ffn_w")[l][k]) for k in range(3)]
        cols += [_fm(f("final_g")), _fm(f("b_ada")[l])]
        vec_l.append(np.concatenate(cols, axis=1))
    vecs = np.ascontiguousarray(np.stack(vec_l))
    assert vecs.shape == (L, 128, NV), vecs.shape
    pool_bd = np.zeros((L, 128, 2, 128), np.float32)
    pw = f("pool_w")
    for l in range(L):
        for g in range(4):
            c, half = g // 2, g % 2
            pool_bd[l, half * 64:(half + 1) * 64, c, half * 64:(half + 1) * 64] = pw[l, g]
    lora_wa = np.ascontiguousarray(np.concatenate([f("rw_w_lora"), f("rw_a_lora")], axis=1))
    shared = {
        "w_ada": f("w_ada"), "w_in": f("w_in"), "w_br": np.ascontiguousarray(f("w_br").reshape(L, D, D)),
        "w_out": f("w_out"), "w_up": f("w_up"), "w_down": f("w_down"), "vecs": vecs, "pool_bd": pool_bd,
        "lora_wa": lora_wa, "lora_g": f("rw_g_lora"), "tabs": make_tables(), "rot": make_rot(),
    }
    in_maps = []
    for i in range(ncores):
        bs = slice(NB * i, NB * (i + 1))
        xs_ = x_sample[bs]
        xT = np.concatenate([x_prompt[i].T, xs_.transpose(2, 1, 0).reshape(D, NSC)], axis=1)
        c17 = np.concatenate([c_prompt[i:i + 1], c_sample[bs]], axis=0)
        cT = c17.T.reshape(8, 128, 17).transpose(1, 0, 2)
        sp = f("state_pool")[:, bs]
        st_pool = sp.transpose(0, 3, 2, 1).reshape(L, 2, 128, 240).transpose(0, 2, 1, 3)
        sr = f("state_ret")[:, bs]
        st_ret = sr.reshape(L, NB, 2, 2, 64, 64).transpose(0, 3, 4, 1, 2, 5).reshape(L, 128, NB, 2, 64)
        ssc = f("state_sconv")[:, bs]
        st_sc = ssc.transpose(0, 3, 2, 1).reshape(L, 2, 128, 32).transpose(0, 2, 1, 3)
        ssh = f("state_shift")[:, bs]
        st_sh = ssh.transpose(0, 2, 1).reshape(L, 8, 128, NB).transpose(0, 2, 1, 3)
        sw = f("state_wkv")[:, bs]
        st_wkv = sw.reshape(L, NB, 2, 2, 64, 64).transpose(0, 3, 5, 1, 2, 4).reshape(L, 128, NB, 2, 64)
        sf = f("state_ffn")[:, bs]
        st_ffn = sf.transpose(0, 3, 2, 1).reshape(L, NFC, 128, 32).transpose(0, 2, 1, 3)
        m = dict(shared)
        m.update({"xT": xT, "cT": cT, "st_pool": st_pool, "st_ret": st_ret, "st_sc": st_sc, "st_sh": st_sh,
                  "st_wkv": st_wkv, "st_ffn": st_ffn})
        in_maps.append({k: np.ascontiguousarray(v, dtype=np.float32) for k, v in m.items()})
    return in_maps


def kernel(**inp):
    in_maps = prepare(**inp)
    if "nc" not in _CACHE:
        _CACHE["nc"] = build_program()
    res = run_bass_kernel_spmd(_CACHE["nc"], in_maps, core_ids=list(range(8)))
    return assemble(res.results)


def assemble(R, cores=None, npt=NPT):
    B = 8
    cores = list(range(B)) if cores is None else cores
    y_p = np.zeros((B, NPT, D), np.float32)
    y_s = np.zeros((B * NB, 4, D), np.float32)
    p_pool = np.zeros((L, B, 15, 256), np.float32)
    p_ret = np.zeros((L, B, 4, 64, 64), np.float32)
    p_sc = np.zeros((L, B, 2, 256), np.float32)
    p_sh = np.zeros((L, B, 1024), np.float32)
    p_wkv = np.zeros((L, B, 4, 64, 64), np.float32)
    p_ffn = np.zeros((L, B, 2, 2 * DFF), np.float32)
    s_pool = np.zeros((L, B * NB, 15, 256), np.float32)
    s_ret = np.zeros((L, B * NB, 4, 64, 64), np.float32)
    s_sc = np.zeros((L, B * NB, 2, 256), np.float32)
    s_sh = np.zeros((L, B * NB, 1024), np.float32)
    s_wkv = np.zeros((L, B * NB, 4, 64, 64), np.float32)
    s_ffn = np.zeros((L, B * NB, 2, 2 * DFF), np.float32)
    for i in cores:
        r = R[i]
        bs = slice(NB * i, NB * (i + 1))
        yT = np.asarray(r["yT"])
        y_p[i, :npt] = yT[:, :npt].T
        y_s[bs] = yT[:, NPT:].reshape(D, 4, NB).transpose(2, 1, 0)
        p_pool[:, i] = np.asarray(r["o_pool_p"]).transpose(0, 3, 2, 1).reshape(L, 15, 256)
        s_pool[:, bs] = np.asarray(r["o_pool_s"]).reshape(L, 128, 2, 15, NB).transpose(0, 4, 3, 2, 1).reshape(L, NB, 15, 256)
        p_ret[:, i] = np.asarray(r["o_ret_p"]).reshape(L, 2, 64, 2, 64).transpose(0, 3, 1, 2, 4).reshape(L, 4, 64, 64)
        s_ret[:, bs] = np.asarray(r["o_ret_s"]).reshape(L, 2, 64, NB, 2, 64).transpose(0, 3, 4, 1, 2, 5).reshape(L, NB, 4, 64, 64)
        p_sc[:, i] = np.asarray(r["o_sc_p"]).transpose(0, 3, 2, 1).reshape(L, 2, 256)
        s_sc[:, bs] = np.asarray(r["o_sc_s"]).reshape(L, 128, 2, 2, NB).transpose(0, 4, 3, 2, 1).reshape(L, NB, 2, 256)
        p_sh[:, i] = np.asarray(r["o_sh_p"]).transpose(0, 2, 1).reshape(L, 1024)
        s_sh[:, bs] = np.asarray(r["o_sh_s"]).transpose(0, 3, 2, 1).reshape(L, NB, 1024)
        p_wkv[:, i] = np.asarray(r["o_wkv_p"]).reshape(L, 2, 64, 2, 64).transpose(0, 3, 1, 4, 2).reshape(L, 4, 64, 64)
        s_wkv[:, bs] = np.asarray(r["o_wkv_s"]).reshape(L, 2, 64, NB, 2, 64).transpose(0, 3, 4, 1, 5, 2).reshape(L, NB, 4, 64, 64)
        p_ffn[:, i] = np.asarray(r["o_ffn_p"]).transpose(0, 3, 2, 1).reshape(L, 2, 2 * DFF)
        s_ffn[:, bs] = np.asarray(r["o_ffn_s"]).reshape(L, 128, NFC, 2, NB).transpose(0, 4, 3, 2, 1).reshape(L, NB, 2, 2 * DFF)
    return (y_p, y_s, p_pool, p_ret, p_sc, p_sh, p_wkv, p_ffn, s_pool, s_ret, s_sc, s_sh, s_wkv, s_ffn)
```
